# Optimizing a Trainium2 kernel written in Bass

```python
import math
import jax, jax.numpy as jnp
from jax import lax
import numpy as np

D_MODEL = 1024
BATCH = 8
SEQ = 4096
DEPTH = 4

N_MEM = 256
DA_HEADS = 4
DA_HEAD_DIM = 64
DA_V_DIM = 2 * DA_HEAD_DIM
DA_WIDTH = DA_HEADS * DA_V_DIM
HG_HEADS = 4
HG_DK = 128
HG_DV = 128
HG_WIDTH = HG_HEADS * HG_DV
HG_CHUNK = 64
HG_MIN_FORGET = 1e-20
MX_HEADS = 4
MX_HEAD_DIM = 128
MX_WIDTH = MX_HEADS * MX_HEAD_DIM
MIX_WIDTH = DA_WIDTH + HG_WIDTH + MX_WIDTH
IN_WIDTHS = (
    DA_HEADS * 2 * DA_HEAD_DIM,
    DA_HEADS * 2 * DA_HEAD_DIM,
    DA_WIDTH,
    HG_HEADS * HG_DK,
    HG_HEADS * HG_DK,
    HG_HEADS * HG_DK,
    HG_WIDTH,
    HG_WIDTH,
    MX_WIDTH,
)
IN_WIDTH = 4608
ROPE_THETA = 500000.0
ROPE_DIM = DA_HEAD_DIM // 4
Q_BLOCK = 128
D_FF = 2816
CONV_W = 3
LN_EPS = 1e-5
RMS_EPS = 1e-6
DEEPNORM_ALPHA = (2 * DEPTH) ** 0.25
DEEPNORM_BETA = (8 * DEPTH) ** -0.25

kernel_name = "hymba_style_diffattn_hgrn2_memxattn_convffn_encoder"


def layer_norm(x, g, b):
    xf = x.astype(jnp.float32)
    mu = jnp.mean(xf, axis=-1, keepdims=True)
    var = jnp.mean(jnp.square(xf - mu), axis=-1, keepdims=True)
    y = (xf - mu) * lax.rsqrt(var + LN_EPS)
    return (y * g.astype(jnp.float32) + b.astype(jnp.float32)).astype(x.dtype)


def rms_norm(x, g):
    xf = x.astype(jnp.float32)
    y = xf * lax.rsqrt(jnp.mean(jnp.square(xf), axis=-1, keepdims=True) + RMS_EPS)
    return (y * g.astype(jnp.float32)).astype(x.dtype)


def rope_partial(t, cos, sin):
    half = ROPE_DIM // 2
    t1 = t[..., :half]
    t2 = t[..., half:ROPE_DIM]
    return jnp.concatenate([t1 * cos - t2 * sin, t2 * cos + t1 * sin, t[..., ROPE_DIM:]], axis=-1)


def diff_attention(q, k, v, lam):
    B, S, H = q.shape[:3]
    nblk = S // Q_BLOCK
    qb = q.reshape(B, nblk, Q_BLOCK, H, 2, DA_HEAD_DIM).transpose(1, 0, 2, 3, 4, 5)

    def one_block(qi):
        s = jnp.einsum('bqhcd,bkhcd->bhcqk', qi, k).astype(jnp.float32)
        p = jax.nn.softmax(s, axis=-1)
        a = p[:, :, 0] - lam * p[:, :, 1]
        return jnp.einsum('bhqk,bkhv->bqhv', a.astype(v.dtype), v)

    o = lax.map(one_block, qb)
    return o.transpose(1, 0, 2, 3, 4).reshape(B, S, H, DA_V_DIM)


def hgrn2_chunk_scan(q, k, v, g):
    B, S, H, dk = q.shape
    dv = v.shape[-1]
    n = S // HG_CHUNK

    def to_chunks(t):
        return t.reshape(B, n, HG_CHUNK, H, t.shape[-1]).transpose(1, 0, 3, 2, 4)

    qc, kc, vc, gc = to_chunks(q), to_chunks(k), to_chunks(v), to_chunks(g)
    bc = jnp.cumsum(gc, axis=3)
    mask = jnp.tril(jnp.ones((HG_CHUNK, HG_CHUNK), dtype=bool))[:, :, None]

    def step(state, inp):
        qt, kt, vt, bt = inp
        o_inter = jnp.einsum('bhtk,bhkv->bhtv', qt * jnp.exp(bt), state)
        diff = bt[:, :, :, None, :] - bt[:, :, None, :, :]
        dec = jnp.where(mask, jnp.exp(jnp.where(mask, diff, 0.0)), 0.0)
        attn = jnp.einsum('bhtk,bhtsk->bhts', qt, dec * kt[:, :, None, :, :])
        o_intra = jnp.einsum('bhts,bhsv->bhtv', attn, vt)
        b_last = bt[:, :, -1:, :]
        state = jnp.exp(b_last[:, :, 0, :])[..., None] * state + jnp.einsum(
            'bhsk,bhsv->bhkv', kt * jnp.exp(b_last - bt), vt)
        return state, o_inter + o_intra

    s0 = jnp.zeros((B, H, dk, dv), jnp.float32)
    _, o = lax.scan(step, s0, (qc, kc, vc, bc))
    return o.transpose(1, 0, 3, 2, 4).reshape(B, S, H, dv)


def hgrn_lower_bound(lb_logits, layer):
    p = jax.nn.softmax(lb_logits.astype(jnp.float32), axis=0)
    return (jnp.cumsum(p, axis=0) - p[0])[layer]


def hgrn_log_forget(z, lb):
    f = lb + (1.0 - lb) * jax.nn.sigmoid(z)
    return jnp.log(jnp.maximum(f, HG_MIN_FORGET))


def memory_attention(q, mk, mv):
    s = jnp.einsum('bqhd,bmhd->bhqm', q * (MX_HEAD_DIM ** -0.5), mk).astype(jnp.float32)
    p = jax.nn.softmax(s, axis=-1)
    return jnp.einsum('bhqm,bmhd->bqhd', p.astype(mv.dtype), mv)


def depthwise_conv_centred(u, w, b):
    y = lax.conv_general_dilated(
        u, w[:, None, :].astype(u.dtype), window_strides=(1,),
        padding=((CONV_W // 2, CONV_W // 2),),
        dimension_numbers=('NWC', 'WIO', 'NWC'),
        feature_group_count=u.shape[-1])
    return y + b.astype(u.dtype)


def hybrid_layer(x, mem, cos, sin, layer, w_in, da_lambda, da_norm_g, hg_lb_fwd, hg_lb_bwd,
                 hg_norm_g, w_mem_kv, w_out, ln1_g, ln1_b, w_up, conv_w, conv_b, w_down,
                 ln2_g, ln2_b):
    B, S, _ = x.shape
    M = mem.shape[1]
    proj = x @ w_in
    split_idx = [int(i) for i in np.cumsum(IN_WIDTHS)[:-1]]
    da_q, da_k, da_v, hg_q, hg_ff, hg_fb, hg_i, hg_g, mx_q = jnp.split(proj, split_idx, axis=-1)

    dq = rope_partial(da_q.reshape(B, S, DA_HEADS, 2, DA_HEAD_DIM), cos, sin) * (DA_HEAD_DIM ** -0.5)
    dk = rope_partial(da_k.reshape(B, S, DA_HEADS, 2, DA_HEAD_DIM), cos, sin)
    dv = da_v.reshape(B, S, DA_HEADS, DA_V_DIM)
    lam_init = 0.8 - 0.6 * math.exp(-0.3 * layer)
    lf = da_lambda.astype(jnp.float32)
    lam = jnp.exp(jnp.sum(lf[0] * lf[1])) - jnp.exp(jnp.sum(lf[2] * lf[3])) + lam_init
    da_o = diff_attention(dq, dk, dv, lam)
    da_o = (rms_norm(da_o, da_norm_g) * (1.0 - lam_init)).reshape(B, S, DA_WIDTH)

    f32 = jnp.float32
    hq = jax.nn.silu(hg_q.reshape(B, S, HG_HEADS, HG_DK).astype(f32))
    hv = hg_i.reshape(B, S, HG_HEADS, HG_DV).astype(f32)
    lb_f = hgrn_lower_bound(hg_lb_fwd, layer).reshape(HG_HEADS, HG_DK)
    lb_b = hgrn_lower_bound(hg_lb_bwd, layer).reshape(HG_HEADS, HG_DK)
    g_f = hgrn_log_forget(hg_ff.reshape(B, S, HG_HEADS, HG_DK).astype(f32), lb_f)
    g_b = hgrn_log_forget(hg_fb.reshape(B, S, HG_HEADS, HG_DK).astype(f32), lb_b)
    o_fwd = hgrn2_chunk_scan(hq, -jnp.expm1(g_f), hv, g_f)
    o_bwd = jnp.flip(hgrn2_chunk_scan(jnp.flip(hq, 1), jnp.flip(-jnp.expm1(g_b), 1),
                                      jnp.flip(hv, 1), jnp.flip(g_b, 1)), 1)
    hg_o = (o_fwd + o_bwd).astype(x.dtype)
    hg_gate = jax.nn.silu(hg_g.reshape(B, S, HG_HEADS, HG_DV))
    hg_o = (rms_norm(hg_o, hg_norm_g) * hg_gate).reshape(B, S, HG_WIDTH)

    mkv = (mem @ w_mem_kv).reshape(B, M, 2, MX_HEADS, MX_HEAD_DIM)
    mx_o = memory_attention(mx_q.reshape(B, S, MX_HEADS, MX_HEAD_DIM), mkv[:, :, 0], mkv[:, :, 1])
    mx_o = mx_o.reshape(B, S, MX_WIDTH)

    mix = jnp.concatenate([da_o, hg_o, mx_o], axis=-1) @ w_out
    x = layer_norm(DEEPNORM_ALPHA * x + mix, ln1_g, ln1_b)

    u = depthwise_conv_centred(x @ w_up, conv_w, conv_b)
    gate, val = jnp.split(u, 2, axis=-1)
    ffn = (jax.nn.silu(gate) * val) @ w_down
    return layer_norm(DEEPNORM_ALPHA * x + ffn, ln2_g, ln2_b)


def setup_inputs(seed: int = 0) -> dict:
    key = jax.random.key(seed)
    ks = jax.random.split(key, 24)
    nrm = jax.random.normal
    D = D_MODEL
    x = nrm(ks[0], (BATCH, SEQ, D), jnp.float32)
    mem = nrm(ks[1], (BATCH, N_MEM, D), jnp.float32)
    offsets = jax.random.randint(ks[2], (BATCH, 1), 0, 1024, dtype=jnp.int32)
    positions = (jnp.arange(SEQ, dtype=jnp.int32)[None, :] + offsets).astype(jnp.int32)
    return {
        "x": x,
        "mem": mem,
        "positions": positions,
        "ln_in_g": 1.0 + 0.02 * nrm(ks[3], (D,), jnp.float32),
        "ln_in_b": 0.02 * nrm(ks[4], (D,), jnp.float32),
        "w_in": nrm(ks[5], (DEPTH, D, IN_WIDTH), jnp.float32) * D ** -0.5,
        "da_lambda": 0.1 * nrm(ks[6], (DEPTH, 4, DA_HEAD_DIM), jnp.float32),
        "da_norm_g": 1.0 + 0.02 * nrm(ks[7], (DEPTH, DA_V_DIM), jnp.float32),
        "hg_lb_fwd": 0.5 * nrm(ks[8], (DEPTH, HG_HEADS * HG_DK), jnp.float32),
        "hg_lb_bwd": 0.5 * nrm(ks[9], (DEPTH, HG_HEADS * HG_DK), jnp.float32),
        "hg_norm_g": 1.0 + 0.02 * nrm(ks[10], (DEPTH, HG_DV), jnp.float32),
        "w_mem_kv": nrm(ks[11], (DEPTH, D, 2 * MX_WIDTH), jnp.float32) * D ** -0.5,
        "w_out": nrm(ks[12], (DEPTH, MIX_WIDTH, D), jnp.float32) * (MIX_WIDTH ** -0.5 * DEEPNORM_BETA),
        "ln1_g": 1.0 + 0.02 * nrm(ks[13], (DEPTH, D), jnp.float32),
        "ln1_b": 0.02 * nrm(ks[14], (DEPTH, D), jnp.float32),
        "w_up": nrm(ks[15], (DEPTH, D, 2 * D_FF), jnp.float32) * D ** -0.5,
        "conv_w": nrm(ks[16], (DEPTH, CONV_W, 2 * D_FF), jnp.float32) * CONV_W ** -0.5,
        "conv_b": 0.02 * nrm(ks[17], (DEPTH, 2 * D_FF), jnp.float32),
        "w_down": nrm(ks[18], (DEPTH, D_FF, D), jnp.float32) * (D_FF ** -0.5 * DEEPNORM_BETA),
        "ln2_g": 1.0 + 0.02 * nrm(ks[19], (DEPTH, D), jnp.float32),
        "ln2_b": 0.02 * nrm(ks[20], (DEPTH, D), jnp.float32),
    }


def reference(x, mem, positions, ln_in_g, ln_in_b, w_in, da_lambda, da_norm_g, hg_lb_fwd,
              hg_lb_bwd, hg_norm_g, w_mem_kv, w_out, ln1_g, ln1_b, w_up, conv_w, conv_b,
              w_down, ln2_g, ln2_b):
    inv_freq = ROPE_THETA ** (-jnp.arange(0, ROPE_DIM, 2, dtype=jnp.float32) / ROPE_DIM)
    ang = positions.astype(jnp.float32)[..., None] * inv_freq
    cos = jnp.cos(ang)[:, :, None, None, :].astype(x.dtype)
    sin = jnp.sin(ang)[:, :, None, None, :].astype(x.dtype)

    h = layer_norm(x, ln_in_g, ln_in_b)
    for l in range(DEPTH):
        h = hybrid_layer(h, mem, cos, sin, l, w_in[l], da_lambda[l], da_norm_g[l],
                         hg_lb_fwd, hg_lb_bwd, hg_norm_g[l], w_mem_kv[l], w_out[l],
                         ln1_g[l], ln1_b[l], w_up[l], conv_w[l], conv_b[l], w_down[l],
                         ln2_g[l], ln2_b[l])
    return h
```

```python
import math
from contextlib import ExitStack
import numpy as np
import concourse.bass as bass
import concourse.mybir as mybir
from concourse.bass_utils import run_bass_kernel_spmd

F32 = mybir.dt.float32
BF16 = mybir.dt.bfloat16
I32 = mybir.dt.int32
AF = mybir.ActivationFunctionType
ALU = mybir.AluOpType

D = 1024
S = 4096
NM = 256
DEPTH = 4
DFF = 2816
ALPHA = (2 * DEPTH) ** 0.25
LN_EPS = 1e-5
RMS_EPS = 1e-6
ROPE_THETA = 500000.0
HC = 64
NCH = S // HC

ENGS = ("pe", "act", "dve", "pool", "sp")
NDMA = 40


class Tok:
    __slots__ = ("sem", "val", "eng", "idx")

    def __init__(self, sem, val, eng, idx):
        self.sem, self.val, self.eng, self.idx = sem, val, eng, idx


class _Rec:
    def __init__(self):
        self.call = None

    def __getattr__(self, name):
        def f(*args, **kwargs):
            self.call = (name, args, kwargs)
            return None
        return f


class KB:
    def __init__(self, nc, es):
        self.nc = nc
        self.q = {e: [] for e in ENGS}
        self.cnt = {e: 0 for e in ENGS}
        self.waited = {e: {} for e in ENGS}
        self.sems = {}
        for e in ("pe", "act", "dve", "pool"):
            self.sems[e] = es.enter_context(nc.semaphore("s_" + e))
        for i in range(NDMA):
            self.sems[("dma", i)] = es.enter_context(nc.semaphore(f"s_dma{i}"))
        self.ndma = 0
        self.dma_toks = [None] * NDMA
        self.out_toks = []
        self.last = {e: None for e in ENGS}

    def _flat(self, deps, out):
        for t in deps:
            if t is None:
                continue
            if isinstance(t, (list, tuple)):
                self._flat(t, out)
            else:
                out.append(t)
        return out

    def _waits(self, eng, deps):
        ws = []
        w = self.waited[eng]
        myidx = len(self.q[eng])
        for t in self._flat(deps, []):
            if t.eng == eng and t.sem == eng and myidx - t.idx > 3:
                continue
            if w.get(t.sem, 0) >= t.val:
                continue
            w[t.sem] = t.val
            ws.append((t.sem, t.val))
        return ws

    def op(self, eng, fn, deps=(), inc=True):
        ws = self._waits(eng, deps)
        idx = len(self.q[eng])
        tok = None
        if inc:
            self.cnt[eng] += 1
            tok = Tok(eng, self.cnt[eng], eng, idx)
            self.last[eng] = tok
        sems = self.sems
        rec = _Rec()
        fn(rec)
        name, args, kwargs = rec.call

        def run(e, ws=ws, name=name, args=args, kwargs=kwargs, inc=inc, eng=eng):
            for (s, v) in ws:
                e.wait_ge(sems[s], v)
            ins = getattr(e, name)(*args, **kwargs)
            if inc:
                ins.then_inc(sems[eng], 1)
        self.q[eng].append(run)
        return tok

    def dma(self, out, in_, deps=(), is_output=False, eng="sp", slow=False):
        i = self.ndma
        self.ndma += 1
        slot = i % NDMA
        key = ("dma", slot)
        val = 16 * (i // NDMA + 1)
        ws = self._waits(eng, list(deps) + [self.dma_toks[slot]])
        tok = Tok(key, val, eng, len(self.q[eng]))
        self.dma_toks[slot] = tok
        sems = self.sems

        def run(e, ws=ws, out=out, in_=in_, key=key, slow=slow):
            for (s, v) in ws:
                e.wait_ge(sems[s], v)
            if slow:
                e.dma_start(out=out, in_=in_, allow_slow_non_contiguous=True).then_inc(sems[key], 16)
            else:
                e.dma_start(out=out, in_=in_).then_inc(sems[key], 16)
        self.q[eng].append(run)
        if is_output:
            self.out_toks.append(tok)
        return tok

    def barrier(self):
        toks = [self.last[e] for e in ("pe", "act", "dve", "pool")] + [t for t in self.dma_toks]
        for eng in ENGS:
            ws = self._waits(eng, toks)
            sems = self.sems

            def run(e, ws=ws):
                for (s, v) in ws:
                    e.wait_ge(sems[s], v)
            self.q[eng].append(run)

    def finish(self, block):
        ws = self._waits("sp", self.out_toks)
        sems = self.sems

        def fin(e, ws=ws):
            for (s, v) in ws:
                e.wait_ge(sems[s], v)
        self.q["sp"].append(fin)
        q = self.q

        @block.sync
        def _(e):
            for f in q["sp"]:
                f(e)

        @block.tensor
        def _(e):
            for f in q["pe"]:
                f(e)

        @block.scalar
        def _(e):
            for f in q["act"]:
                f(e)

        @block.vector
        def _(e):
            for f in q["dve"]:
                f(e)

        @block.gpsimd
        def _(e):
            for f in q["pool"]:
                f(e)


class Buf:
    def __init__(self, t):
        self.t = t
        self.w = []
        self.r = {}

    def wdeps(self):
        return [self.w, list(self.r.values())]

    def wrote(self, tok, fresh=True):
        if fresh:
            self.w = [tok]
            self.r = {}
        else:
            self.w.append(tok)

    def rdeps(self):
        return self.w

    def read(self, tok):
        if tok is not None:
            self.r[tok.eng if not isinstance(tok.sem, tuple) else ("d", len(self.r))] = tok


class Ring:
    def __init__(self, bufs):
        self.bufs = bufs
        self.i = 0

    def next(self):
        b = self.bufs[self.i % len(self.bufs)]
        self.i += 1
        return b


def _col_layout(L):
    off = {}
    c = 0
    off["lbf"] = c; c += 4 * DEPTH
    off["lbb"] = c; c += 4 * DEPTH
    for l in range(L):
        off[("dag", l)] = c; c += 1
        off[("hgg", l)] = c; c += 1
        off[("cw", l)] = c; c += 3 * 44
        off[("cb", l)] = c; c += 44
    off["n"] = c
    return off


def _row_layout(L):
    off = {}
    c = 0
    off["ln_in_g"] = c; c += D
    off["ln_in_b"] = c; c += D
    for l in range(L):
        for nm in ("ln1_g", "ln1_b", "ln2_g", "ln2_b"):
            off[(nm, l)] = c; c += D
        off[("lam", l)] = c; c += 256
    off["n"] = c
    return off


def build(L=DEPTH, dbg=False):
    nc = bass.Bass("TRN2", target_bir_lowering=False)
    CO = _col_layout(L)
    RO = _row_layout(L)
    dk = "ExternalOutput" if dbg else None

    def dram(name, shape, dtype, kind=None):
        if kind is None:
            return nc.dram_tensor(name, shape, dtype).ap()
        return nc.dram_tensor(name, shape, dtype, kind=kind).ap()

    x_d = dram("x", [S, D], F32, "ExternalInput")
    mem_d = dram("mem", [NM, D], F32, "ExternalInput")
    pos_d = dram("pos", [1, S], I32, "ExternalInput")
    colp_d = dram("colp", [128, CO["n"]], F32, "ExternalInput")
    rowp_d = dram("rowp", [1, RO["n"]], F32, "ExternalInput")
    cst_d = dram("cst", [128, 1024], F32, "ExternalInput")
    w_in_d = dram("w_in", [L, D, 4608], F32, "ExternalInput")
    w_rot_d = dram("w_rot", [L, D, 1024], F32, "ExternalInput")
    w_kv_d = dram("w_kv", [L, D, 1024], F32, "ExternalInput")
    w_out_d = dram("w_out", [L, 1536, D], F32, "ExternalInput")
    w_up_d = dram("w_up", [L, D, 2 * DFF], F32, "ExternalInput")
    w_dn_d = dram("w_dn", [L, DFF, D], F32, "ExternalInput")
    out_d = dram("out", [S, D], F32, "ExternalOutput")

    xres_d = dram("xres", [S, D], F32, dk)
    hT_d = dram("hT", [8, 128, S], BF16, dk)
    rope_d = dram("rope", [4, 128, S], F32, dk)
    qT_d = dram("qT", [4, 128, S], BF16, dk)
    kT_d = dram("kT", [4, 128, S], BF16, dk)
    v_d = dram("v", [S, 512], BF16, dk)
    hq_d = dram("hq", [4, 128, S], F32, dk)
    zf_d = dram("zf", [4, 128, S], F32, dk)
    zb_d = dram("zb", [4, 128, S], F32, dk)
    hv_d = dram("hv", [S, 512], BF16, dk)
    hg_d = dram("hg", [4, 128, S], F32, dk)
    mq_d = dram("mq", [4, 128, S], BF16, dk)
    of_d = dram("of", [4, 128, S], F32, dk)
    ob_d = dram("ob", [4, 128, S], F32, dk)
    cat_d = dram("cat", [12, 128, S], BF16, dk)
    if dbg:
        dbg_mk = dram("dbg_mk", [128, 4 * NM], BF16, dk)
        dbg_mv = dram("dbg_mv", [128, 1024], BF16, dk)
        dbg_sm = dram("dbg_sm", [128, 4], F32, dk)
        dbg_memT = dram("dbg_memT", [128, 8 * NM], BF16, dk)
        dbg_p = dram("dbg_p", [128, 1024], BF16, dk)
        dbg_oz = dram("dbg_oz", [128, 3, 512], F32, dk)

    with ExitStack() as es:
        kb = KB(nc, es)
        block = es.enter_context(nc.Block())

        uniq = [0]

        def sb(name, shape, dtype, ctx=es):
            uniq[0] += 1
            return ctx.enter_context(nc.sbuf_tensor(f"{name}_{uniq[0]}", shape, dtype))

        cst = sb("cst_s", [128, 1024], F32)
        colp = sb("colp_s", [128, CO["n"]], F32)
        identb = sb("identb", [128, 128], BF16)
        onesb = sb("onesb", [128, 128], BF16)
        trib = sb("trib", [64, 2, 8, 64], BF16)
        lbt = sb("lbt", [128, 2, L, 4], F32)
        oml = sb("oml", [128, 2, L, 4], F32)
        epsc = sb("epsc", [128, 2], F32)
        memT = sb("memT", [128, 8, NM], BF16)
        PSB = [Buf(es.enter_context(nc.psum_tensor(f"psb{i}", [128, 512], F32))) for i in range(0)]
        PP = [Buf(es.enter_context(nc.psum_tensor(f"pp{i}", [128, 2, 512], F32))) for i in range(4)]

        ident = cst[:, 0:128]
        onesf = cst[:, 128:256]

        t_cst = kb.dma(cst[:], cst_d[:, :])
        t_colp = kb.dma(colp[:], colp_d[:, :])
        t_ib = kb.op("dve", lambda e: e.tensor_copy(out=identb[:], in_=cst[:, 0:128]), deps=[t_cst])
        t_ob = kb.op("dve", lambda e: e.tensor_copy(out=onesb[:], in_=cst[:, 128:256]), deps=[t_cst])
        for dr in range(2):
            for rep in range(8):
                t_tb = kb.op("dve", lambda e, dr=dr, rep=rep: e.tensor_copy(
                    out=trib[:, dr, rep, :], in_=cst[0:64, 256 + 64 * dr:320 + 64 * dr]), deps=[t_cst])
        kb.op("dve", lambda e: e.memset(epsc[:, 0:1], LN_EPS))
        kb.op("dve", lambda e: e.memset(epsc[:, 1:2], RMS_EPS))
        with ExitStack() as ps:
            ex = sb("lb_ex", [128, 2, DEPTH, 4], F32, ps)
            ssum = sb("lb_s", [128, 2, 4], F32, ps)
            t1 = kb.op("act", lambda e: e.activation(
                out=ex[:].rearrange("p a l h -> p (a l h)"), in_=colp[:, CO["lbf"]:CO["lbf"] + 8 * DEPTH], func=AF.Exp),
                deps=[t_colp])
            t2 = kb.op("dve", lambda e: e.tensor_tensor(out=ssum[:], in0=ex[:, :, 0, :], in1=ex[:, :, 1, :], op=ALU.add), deps=[t1])
            t2 = kb.op("dve", lambda e: e.tensor_tensor(out=ssum[:], in0=ssum[:], in1=ex[:, :, 2, :], op=ALU.add), deps=[t2])
            t2 = kb.op("dve", lambda e: e.tensor_tensor(out=ssum[:], in0=ssum[:], in1=ex[:, :, 3, :], op=ALU.add), deps=[t2])
            t2 = kb.op("dve", lambda e: e.reciprocal(out=ssum[:], in_=ssum[:]), deps=[t2])
            t3 = kb.op("dve", lambda e: e.memset(lbt[:, :, 0, :], 0.0))
            for l in range(1, L):
                if l == 1:
                    t3 = kb.op("dve", lambda e: e.tensor_copy(out=lbt[:, :, 1, :], in_=ex[:, :, 1, :]), deps=[t1, t3])
                else:
                    t3 = kb.op("dve", lambda e, l=l: e.tensor_tensor(out=lbt[:, :, l, :], in0=lbt[:, :, l - 1, :],
                                                                   in1=ex[:, :, l, :], op=ALU.add), deps=[t3])
            for l in range(1, L):
                t3 = kb.op("dve", lambda e, l=l: e.tensor_tensor(out=lbt[:, :, l, :], in0=lbt[:, :, l, :], in1=ssum[:],
                                                               op=ALU.mult), deps=[t3, t2])
            t3 = kb.op("dve", lambda e: e.tensor_scalar(out=oml[:].rearrange("p a l h -> p (a l h)"),
                                                       in0=lbt[:].rearrange("p a l h -> p (a l h)"),
                                                       scalar1=-1.0, scalar2=1.0, op0=ALU.mult, op1=ALU.add), deps=[t3])
            kb.barrier()

        cast_rr = [0]

        def cast(out, in_, deps):
            engs = ("dve", "pool", "act")
            eg = engs[cast_rr[0] % 3]
            cast_rr[0] += 1
            if eg == "act":
                return kb.op("act", lambda e: e.activation(out=out, in_=in_, func=AF.Copy), deps=deps)
            return kb.op(eg, lambda e: e.tensor_copy(out=out, in_=in_), deps=deps)

        def load_w(dst, src, nk, ncol, stg):
            toks = []
            step = 2048
            for k in range(nk):
                for c0 in range(0, ncol, step):
                    n = min(step, ncol - c0)
                    b = stg.next()
                    td = kb.dma(b.t[:, 0:n], src[k * 128:(k + 1) * 128, c0:c0 + n], deps=b.wdeps())
                    b.wrote(td)
                    tc = cast(dst[:, k, c0:c0 + n], b.t[:, 0:n], deps=[td])
                    b.read(tc)
                    toks.append(tc)
            return toks

        def ln_tile(r, rdeps, n, t0, G, Bt, gdeps, lnb, last, hTt, hcol):
            st, mv, sd, nmr = lnb["st"], lnb["mv"], lnb["sd"], lnb["nmr"]
            ta = kb.op("dve", lambda e: e.bn_stats(out=st.t[0:n, 0, :], in_=r.t[0:n, 0:512]), deps=[rdeps, st.wdeps()])
            tb = kb.op("dve", lambda e: e.bn_stats(out=st.t[0:n, 1, :], in_=r.t[0:n, 512:1024]), deps=[rdeps])
            st.wrote(tb)
            tc = kb.op("dve", lambda e: e.bn_aggr(out=mv.t[0:n, :], in_=st.t[0:n].rearrange("p a b -> p (a b)")),
                       deps=[ta, tb, mv.wdeps()])
            st.read(tc)
            mv.wrote(tc)
            td = kb.op("act", lambda e: e.activation(out=sd.t[0:n, :], in_=mv.t[0:n, 1:2], func=AF.Sqrt,
                                                    bias=epsc[0:n, 0:1], scale=1.0), deps=[tc, sd.wdeps()])
            sd.wrote(td)
            te = kb.op("dve", lambda e: e.reciprocal(out=sd.t[0:n, :], in_=sd.t[0:n, :]), deps=[td])
            sd.wrote(te)
            tf = kb.op("dve", lambda e: e.scalar_tensor_tensor(out=nmr.t[0:n, :], in0=mv.t[0:n, 0:1], scalar=-1.0,
                                                              in1=sd.t[0:n, :], op0=ALU.mult, op1=ALU.mult),
                       deps=[te, nmr.wdeps()])
            nmr.wrote(tf)
            mv.read(tf)
            tg = kb.op("act", lambda e: e.activation(out=r.t[0:n, :], in_=r.t[0:n, :], func=AF.Identity,
                                                    scale=sd.t[0:n, 0:1], bias=nmr.t[0:n, 0:1]), deps=[tf, te, tb])
            sd.read(tg)
            nmr.read(tg)
            th = kb.op("pool", lambda e: e.tensor_tensor(out=r.t[0:n, :], in0=r.t[0:n, :], in1=G[0:n, :], op=ALU.mult),
                       deps=[tg, gdeps])
            ti = kb.op("pool", lambda e: e.tensor_tensor(out=r.t[0:n, :], in0=r.t[0:n, :], in1=Bt[0:n, :], op=ALU.add),
                       deps=[th])
            r.wrote(ti)
            if last:
                tdm = kb.dma(out_d[t0:t0 + n, :], r.t[0:n, :], deps=[ti], is_output=True)
                r.read(tdm)
                return []
            tdm = kb.dma(xres_d[t0:t0 + n, :], r.t[0:n, :], deps=[ti])
            r.read(tdm)
            toks = []
            for hf in range(2):
                pb = lnb["pt"].next()
                for j in range(4):
                    kc = hf * 4 + j
                    tk = kb.op("pe", lambda e, pb=pb, j=j, kc=kc: e.transpose(
                        out=pb.t[:, 0, j * 128:j * 128 + n], in_=r.t[0:n, kc * 128:(kc + 1) * 128],
                        identity=ident[0:n, 0:n]), deps=[ti, pb.wdeps() if j == 0 else None, t_cst], inc=(j == 3))
                pb.wrote(tk)
                r.read(tk)
                src = pb.t[:, 0, :].rearrange("p (j t) -> p j t", j=4)[:, :, 0:n]
                dst = hTt.t[:, hf * 4:(hf + 1) * 4, hcol:hcol + n]
                if hf == 0:
                    te2 = kb.op("act", lambda e, src=src, dst=dst: e.activation(out=dst, in_=src, func=AF.Copy),
                                deps=[tk, hTt.wdeps()])
                else:
                    te2 = kb.op("dve", lambda e, src=src, dst=dst: e.tensor_copy(out=dst, in_=src),
                                deps=[tk, hTt.wdeps()])
                pb.read(te2)
                toks.append(te2)
            return toks

        def ln_bufs(ctx):
            return {
                "st": Buf(sb("ln_st", [128, 2, 6], F32, ctx)),
                "mv": Buf(sb("ln_mv", [128, 2], F32, ctx)),
                "sd": Buf(sb("ln_sd", [128, 1], F32, ctx)),
                "nmr": Buf(sb("ln_nmr", [128, 1], F32, ctx)),
                "pt": Ring([PP[3]]),
            }

        def load_gb(G, Bt, og, ob_):
            ta = kb.dma(G[:], rowp_d[0:1, og:og + D].to_broadcast([128, D]))
            tb = kb.dma(Bt[:], rowp_d[0:1, ob_:ob_ + D].to_broadcast([128, D]))
            return [ta, tb]

        hT_v = hT_d.rearrange("k p t -> p k t")

        with ExitStack() as ps:
            mt_ = sb("memf", [128, 2, D], F32, ps)
            td = kb.dma(mt_[:], mem_d.rearrange("(a p) d -> p a d", p=128))
            for a in range(2):
                for hf in range(2):
                    pb = PP[hf]
                    for j in range(4):
                        kc = hf * 4 + j
                        tk = kb.op("pe", lambda e, pb=pb, j=j, kc=kc, a=a: e.transpose(
                            out=pb.t[:, 0, j * 128:(j + 1) * 128], in_=mt_[:, a, kc * 128:(kc + 1) * 128],
                            identity=ident), deps=[td, t_cst, pb.wdeps() if j == 0 else None], inc=(j == 3))
                    pb.wrote(tk)
                    te = kb.op("dve", lambda e, pb=pb, hf=hf, a=a: e.tensor_copy(
                        out=memT[:, hf * 4:(hf + 1) * 4, a * 128:(a + 1) * 128],
                        in_=pb.t[:, 0, :].rearrange("p (j t) -> p j t", j=4)), deps=[tk])
                    pb.read(te)
            posi = sb("posi", [128, S], I32, ps)
            ang = sb("ang", [128, S], F32, ps)
            a2 = sb("ang2", [128, S], F32, ps)
            ki = sb("ki", [128, S], I32, ps)
            kf = sb("kf", [128, S], F32, ps)
            tp = kb.dma(posi[:], pos_d[0:1, :].to_broadcast([128, S]))
            t0_ = kb.op("dve", lambda e: e.tensor_copy(out=ang[:], in_=posi[:]), deps=[tp])
            t0_ = kb.op("dve", lambda e: e.tensor_scalar(out=ang[:], in0=ang[:], scalar1=cst[:, 384:385], scalar2=None,
                                                        op0=ALU.mult), deps=[t0_, t_cst])
            TWO_PI = 2.0 * math.pi

            def reduce_sin(shift, out_scale_col, dst_idx_list):
                t = kb.op("dve", lambda e: e.tensor_scalar(out=kf[:], in0=ang[:], scalar1=shift, scalar2=1.0 / TWO_PI,
                                                          op0=ALU.add, op1=ALU.mult), deps=[t0_])
                t = kb.op("dve", lambda e: e.tensor_copy(out=ki[:], in_=kf[:]), deps=[t])
                t = kb.op("dve", lambda e: e.tensor_copy(out=kf[:], in_=ki[:]), deps=[t])
                t = kb.op("dve", lambda e: e.scalar_tensor_tensor(out=a2[:], in0=kf[:], scalar=-TWO_PI, in1=ang[:],
                                                                 op0=ALU.mult, op1=ALU.add), deps=[t])
                if shift != 0.0:
                    t = kb.op("dve", lambda e: e.tensor_scalar(out=a2[:], in0=a2[:], scalar1=shift, scalar2=None,
                                                              op0=ALU.add), deps=[t])
                t = kb.op("dve", lambda e: e.tensor_scalar(out=kf[:], in0=a2[:], scalar1=math.pi, scalar2=-TWO_PI,
                                                          op0=ALU.is_gt, op1=ALU.mult), deps=[t])
                t = kb.op("dve", lambda e: e.tensor_tensor(out=a2[:], in0=a2[:], in1=kf[:], op=ALU.add), deps=[t])
                t = kb.op("dve", lambda e: e.tensor_scalar(out=kf[:], in0=a2[:], scalar1=-math.pi, scalar2=TWO_PI,
                                                          op0=ALU.is_lt, op1=ALU.mult), deps=[t])
                t = kb.op("dve", lambda e: e.tensor_tensor(out=a2[:], in0=a2[:], in1=kf[:], op=ALU.add), deps=[t])
                t = kb.op("dve", lambda e: e.tensor_scalar(out=a2[:], in0=a2[:], scalar1=-3.14159, scalar2=3.14159,
                                                          op0=ALU.max, op1=ALU.min), deps=[t])
                t = kb.op("act", lambda e: e.activation(out=a2[:], in_=a2[:], func=AF.Sin), deps=[t])
                return t

            t = reduce_sin(math.pi / 2.0, None, None)
            t = kb.op("dve", lambda e: e.tensor_scalar(out=kf[:], in0=a2[:], scalar1=cst[:, 385:386],
                                                      scalar2=cst[:, 386:387], op0=ALU.mult, op1=ALU.add), deps=[t])
            tck = kb.dma(rope_d[2], kf[:], deps=[t])
            t = kb.op("pool", lambda e: e.tensor_scalar(out=a2[:], in0=kf[:], scalar1=0.125, scalar2=None, op0=ALU.mult),
                      deps=[t])
            tcq = kb.dma(rope_d[0], a2[:], deps=[t])
            t = reduce_sin(0.0, None, None) if False else None
            kb.barrier()
            t = reduce_sin(0.0, None, None)
            t = kb.op("dve", lambda e: e.tensor_scalar(out=kf[:], in0=a2[:], scalar1=cst[:, 387:388], scalar2=None,
                                                      op0=ALU.mult), deps=[t])
            kb.dma(rope_d[3], kf[:], deps=[t])
            t = kb.op("pool", lambda e: e.tensor_scalar(out=a2[:], in0=kf[:], scalar1=0.125, scalar2=None, op0=ALU.mult),
                      deps=[t])
            kb.dma(rope_d[1], a2[:], deps=[t])
            kb.barrier()

        with ExitStack() as ps:
            G = sb("G0", [128, D], F32, ps)
            Bt = sb("B0", [128, D], F32, ps)
            gd = load_gb(G, Bt, RO["ln_in_g"], RO["ln_in_b"])
            lnb = ln_bufs(ps)
            rr = Ring([Buf(sb(f"r0_{i}", [128, D], F32, ps)) for i in range(3)])
            hr = Ring([Buf(sb(f"hTt0_{i}", [128, 8, 512], BF16, ps)) for i in range(2)])
            for tt in range(S // 512):
                hTt = hr.next()
                toks = []
                for s4 in range(4):
                    t0 = tt * 512 + s4 * 128
                    r = rr.next()
                    td = kb.dma(r.t[:], x_d[t0:t0 + 128, :], deps=r.wdeps())
                    r.wrote(td)
                    toks += ln_tile(r, [td], 128, t0, G, Bt, gd, lnb, False, hTt, s4 * 128)
                hTt.wrote(toks[0])
                for t in toks[1:]:
                    hTt.wrote(t, fresh=False)
                tdm = kb.dma(hT_v[:, :, tt * 512:(tt + 1) * 512], hTt.t[:], deps=toks)
                hTt.read(tdm)
            kb.barrier()

        for l in range(L):
            lam_init = 0.8 - 0.6 * math.exp(-0.3 * l)
            with ExitStack() as ps:
                wi = sb("wi", [128, 8, 4608], BF16, ps)
                wr = sb("wr", [128, 8, 1024], BF16, ps)
                with ExitStack() as ps2:
                    stg = Ring([Buf(sb(f"stg{i}", [128, 2048], F32, ps2)) for i in range(3)])
                    wtoks = load_w(wi, w_in_d[l], 8, 4608, stg)
                    wtoks += load_w(wr, w_rot_d[l], 8, 1024, stg)
                    kb.barrier()
                hin = Ring([Buf(sb(f"hin{i}", [128, 8, 512], BF16, ps)) for i in range(2)])
                rtab = Ring([Buf(sb(f"rtab{i}", [128, 4, 512], F32, ps)) for i in range(2)])
                tmpA = Ring([Buf(sb(f"tmpA{i}", [128, 512], F32, ps)) for i in range(2)])
                tmpB = Ring([Buf(sb(f"tmpB{i}", [128, 512], F32, ps)) for i in range(2)])
                stF = Ring([Buf(sb(f"stF{i}", [128, 512], F32, ps)) for i in range(4)])
                stH = Ring([Buf(sb(f"stH{i}", [128, 512], BF16, ps)) for i in range(4)])
                pr = Ring(PP)

                def mm_group(dst, kc_list, lhs_fn, rhs_fn, deps):
                    tk = None
                    nk = len(kc_list)
                    for i, kc in enumerate(kc_list):
                        tk = kb.op("pe", lambda e, kc=kc, i=i: e.matmul(dst, lhsT=lhs_fn(kc), rhs=rhs_fn(kc),
                                                                      start=(i == 0), stop=(i == nk - 1)),
                                   deps=deps if i == 0 else (), inc=(i == nk - 1))
                    return tk

                for tt in range(8):
                    c0 = tt * 512
                    h = hin.next()
                    td = kb.dma(h.t[:], hT_v[:, :, c0:c0 + 512], deps=h.wdeps())
                    h.wrote(td)
                    rt = rtab.next()
                    td2 = kb.dma(rt.t[:], rope_d[:, :, c0:c0 + 512].rearrange("a p t -> p a t"), deps=rt.wdeps())
                    rt.wrote(td2)
                    for which in range(2):
                        for hh in range(4):
                            col = which * 512 + hh * 128
                            pb = pr.next()
                            tA = mm_group(pb.t[:, 0, :], range(8), lambda kc, col=col: wi[:, kc, col:col + 128],
                                          lambda kc, h=h: h.t[:, kc, :], [td, pb.wdeps()])
                            tB = mm_group(pb.t[:, 1, :], range(8), lambda kc, col=col: wr[:, kc, col:col + 128],
                                          lambda kc, h=h: h.t[:, kc, :], [])
                            pb.wrote(tB)
                            h.read(tB)
                            a = tmpA.next()
                            b = tmpB.next()
                            t1 = kb.op("dve", lambda e, a=a, pb=pb, rt=rt, which=which: e.tensor_tensor(
                                out=a.t[:], in0=pb.t[:, 0, :], in1=rt.t[:, 2 * which, :], op=ALU.mult),
                                deps=[tA, td2, a.wdeps()])
                            t2 = kb.op("dve", lambda e, b=b, pb=pb, rt=rt, which=which: e.tensor_tensor(
                                out=b.t[:], in0=pb.t[:, 1, :], in1=rt.t[:, 2 * which + 1, :], op=ALU.mult),
                                deps=[tB, td2, b.wdeps()])
                            pb.read(t2)
                            rt.read(t2)
                            a.wrote(t1)
                            b.wrote(t2)
                            so = stH.next()
                            t3 = kb.op("pool", lambda e, a=a, b=b, so=so: e.tensor_tensor(
                                out=so.t[:], in0=a.t[:], in1=b.t[:], op=ALU.add), deps=[t1, t2, so.wdeps()])
                            a.read(t3)
                            b.read(t3)
                            so.wrote(t3)
                            dst = (qT_d if which == 0 else kT_d)[hh][:, c0:c0 + 512]
                            so.read(kb.dma(dst, so.t[:], deps=[t3]))
                    specs = []
                    for hh in range(4):
                        specs.append((1536 + hh * 128, hq_d[hh], AF.Silu, 1.0, False))
                        specs.append((2048 + hh * 128, zf_d[hh], AF.Copy, 1.0, False))
                        specs.append((2560 + hh * 128, zb_d[hh], AF.Copy, 1.0, False))
                        specs.append((3584 + hh * 128, hg_d[hh], AF.Silu, 1.0, False))
                        specs.append((4096 + hh * 128, mq_d[hh], AF.Copy, 128.0 ** -0.5, True))
                    for i in range(0, len(specs), 2):
                        pb = pr.next()
                        tks = []
                        for j in range(2):
                            col = specs[i + j][0]
                            tks.append(mm_group(pb.t[:, j, :], range(8), lambda kc, col=col: wi[:, kc, col:col + 128],
                                                lambda kc, h=h: h.t[:, kc, :], [td, pb.wdeps()] if j == 0 else []))
                        pb.wrote(tks[1])
                        h.read(tks[1])
                        for j in range(2):
                            col, dst, fn, sc, isb = specs[i + j]
                            so = (stH if isb else stF).next()
                            if fn == AF.Copy and not isb and (i + j) % 2 == 0:
                                te = kb.op("dve", lambda e, so=so, pb=pb, j=j: e.tensor_copy(out=so.t[:], in_=pb.t[:, j, :]),
                                           deps=[tks[j], so.wdeps()])
                            else:
                                te = kb.op("act", lambda e, so=so, pb=pb, j=j, fn=fn, sc=sc: e.activation(
                                    out=so.t[:], in_=pb.t[:, j, :], func=fn, scale=sc), deps=[tks[j], so.wdeps()])
                            pb.read(te)
                            so.wrote(te)
                            so.read(kb.dma(dst[:, c0:c0 + 512], so.t[:], deps=[te]))
                    for (col, dst) in ((1024, v_d), (3072, hv_d)):
                        for s2 in range(2):
                            pb = pr.next()
                            tks = []
                            for j in range(2):
                                sub = s2 * 2 + j
                                tks.append(mm_group(pb.t[:, j, :], range(8),
                                                    lambda kc, h=h, sub=sub: h.t[:, kc, sub * 128:(sub + 1) * 128],
                                                    lambda kc, col=col: wi[:, kc, col:col + 512],
                                                    [td, pb.wdeps()] if j == 0 else []))
                            pb.wrote(tks[1])
                            h.read(tks[1])
                            for j in range(2):
                                sub = s2 * 2 + j
                                so = stH.next()
                                if j == 0:
                                    te = kb.op("dve", lambda e, so=so, pb=pb, j=j: e.tensor_copy(out=so.t[:], in_=pb.t[:, j, :]),
                                               deps=[tks[j], so.wdeps()])
                                else:
                                    te = kb.op("act", lambda e, so=so, pb=pb, j=j: e.activation(
                                        out=so.t[:], in_=pb.t[:, j, :], func=AF.Copy), deps=[tks[j], so.wdeps()])
                                pb.read(te)
                                so.wrote(te)
                                so.read(kb.dma(dst[c0 + sub * 128:c0 + (sub + 1) * 128, :], so.t[:], deps=[te]))
                kb.barrier()
            if dbg and dbg == "inproj":
                break

            with ExitStack() as ps:
                lamt = sb("lamt", [128, 256], F32, ps)
                lamp = sb("lamp", [128, 2, 64], F32, ps)
                lams = sb("lams", [128, 2], F32, ps)
                nlam = sb("nlam", [128, 1], F32, ps)
                gcol = sb("gcol", [128, 1], F32, ps)
                ro = RO[("lam", l)]
                td = kb.dma(lamt[:], rowp_d[0:1, ro:ro + 256].to_broadcast([128, 256]))
                lv = lamt[:].rearrange("p (a d) -> p a d", a=4)
                t = kb.op("dve", lambda e: e.tensor_tensor(out=lamp[:, 0, :], in0=lv[:, 0, :], in1=lv[:, 1, :], op=ALU.mult), deps=[td])
                t = kb.op("dve", lambda e: e.tensor_tensor(out=lamp[:, 1, :], in0=lv[:, 2, :], in1=lv[:, 3, :], op=ALU.mult), deps=[td, t])
                t = kb.op("dve", lambda e: e.tensor_reduce(out=lams[:], in_=lamp[:], axis=mybir.AxisListType.X, op=ALU.add), deps=[t])
                t = kb.op("act", lambda e: e.activation(out=lams[:], in_=lams[:], func=AF.Exp), deps=[t])
                t = kb.op("dve", lambda e: e.tensor_tensor(out=nlam[:], in0=lams[:, 1:2], in1=lams[:, 0:1], op=ALU.subtract), deps=[t])
                t = kb.op("dve", lambda e: e.tensor_scalar(out=nlam[:], in0=nlam[:], scalar1=-lam_init, scalar2=None, op0=ALU.add), deps=[t])
                t_lam = t
                cdg = CO[("dag", l)]
                t_g = kb.op("dve", lambda e: e.tensor_scalar(out=gcol[:], in0=colp[:, cdg:cdg + 1], scalar1=1.0 - lam_init,
                                                            scalar2=None, op0=ALU.mult), deps=[t_colp])
                mkT = sb("mkT", [128, 4, NM], BF16, ps)
                mv_ = sb("mv", [128, 2, 512], BF16, ps)
                with ExitStack() as ps2:
                    wkv = sb("wkv", [128, 8, 1024], BF16, ps2)
                    stg = Ring([Buf(sb(f"stgm{i}", [128, 2048], F32, ps2)) for i in range(3)])
                    wt = load_w(wkv, w_kv_d[l], 8, 1024, stg)
                    for hp in range(2):
                        pb = PP[hp]
                        for j in range(2):
                            hh = hp * 2 + j
                            for kc in range(8):
                                tk = kb.op("pe", lambda e, pb=pb, j=j, kc=kc, hh=hh: e.matmul(
                                    pb.t[:, j, 0:NM], lhsT=wkv[:, kc, hh * 128:(hh + 1) * 128], rhs=memT[:, kc, :],
                                    start=(kc == 0), stop=(kc == 7)), deps=[wt] if kc == 0 else (), inc=(kc == 7))
                            te = kb.op("act", lambda e, pb=pb, j=j, hh=hh: e.activation(
                                out=mkT[:, hh, :], in_=pb.t[:, j, 0:NM], func=AF.Copy), deps=[tk])
                    pb = PP[2]
                    for mt in range(2):
                        for kc in range(8):
                            tk = kb.op("pe", lambda e, mt=mt, kc=kc: e.matmul(
                                PP[2].t[:, mt, :], lhsT=memT[:, kc, mt * 128:(mt + 1) * 128], rhs=wkv[:, kc, 512:1024],
                                start=(kc == 0), stop=(kc == 7)), deps=[wt] if kc == 0 else (), inc=(kc == 7))
                        te = kb.op("dve", lambda e, mt=mt: e.tensor_copy(out=mv_[:, mt, :], in_=PP[2].t[:, mt, :]), deps=[tk])
                    kb.barrier()
                    if dbg:
                        kb.dma(dbg_mk[:, :], mkT[:].rearrange("p a m -> p (a m)"))
                        kb.dma(dbg_mv[:, :], mv_[:].rearrange("p a m -> p (a m)"))
                        kb.dma(dbg_memT[:, :], memT[:].rearrange("p a m -> p (a m)"))
                        kb.dma(dbg_sm[:, 0:1], nlam[:], slow=True)
                        kb.dma(dbg_sm[:, 1:2], gcol[:], slow=True)
                        kb.dma(dbg_sm[:, 2:4], lams[:], slow=True)
                        kb.barrier()

                kts = Ring([Buf(sb(f"kts{i}", [128, S], BF16, ps)) for i in range(2)])
                vts = Ring([Buf(sb(f"vts{i}", [128, 32, 128], BF16, ps)) for i in range(2)])
                qts = Ring([Buf(sb(f"qts{i}", [128, 512], BF16, ps)) for i in range(2)])
                pts = Ring([Buf(sb(f"pts{i}", [128, 2, 512], BF16, ps)) for i in range(3)])
                rz = Buf(sb("rz", [128, 2, 512], F32, ps))
                fa = Buf(sb("fa", [128, 512], F32, ps))
                fb = Buf(sb("fb", [128, 512], F32, ps))
                fo = Buf(sb("fo", [128, 512], F32, ps))
                fsq = Buf(sb("fsq", [128, 512], F32, ps))
                fsd = Buf(sb("fsd", [128, 512], F32, ps))
                fob = Ring([Buf(sb(f"fob{i}", [128, 512], BF16, ps)) for i in range(2)])
                scr = Ring([PP[0], PP[1]])
                Ob, Zb = PP[2], PP[3]

                def attn_core(qb, nkt, lhs_s, lhs_v, ncomp):
                    tO = tZ = None
                    for kt in range(nkt):
                        sc = scr.next()
                        tks = []
                        if ncomp == 2:
                            for c in range(2):
                                tks.append(kb.op("pe", lambda e, sc=sc, c=c, kt=kt: e.matmul(
                                    sc.t[:, c, :], lhsT=lhs_s(kt, c), rhs=qb.t[c * 64:(c + 1) * 64, :], start=True, stop=True),
                                    deps=[qb.rdeps(), sc.wdeps()] if c == 0 else (), inc=(c == 1)))
                            tS = tks[1]
                            ncol = 2
                        else:
                            tS = None
                        sc.wrote(tS)
                        p = pts.next()
                        te0 = kb.op("act", lambda e, p=p, sc=sc: e.activation(
                            out=p.t[:, 0, :], in_=sc.t[:, 0, :], func=AF.Exp), deps=[tS, p.wdeps()])
                        te = kb.op("act", lambda e, p=p, sc=sc: e.activation(
                            out=p.t[:, 1, :], in_=sc.t[:, 1, :], func=AF.Exp), deps=[tS])
                        sc.read(te)
                        p.wrote(te)
                        for c in range(2):
                            tO = kb.op("pe", lambda e, p=p, c=c, kt=kt: e.matmul(
                                Ob.t[:, c, :], lhsT=lhs_v(kt), rhs=p.t[:, c, :], start=(kt == 0), stop=(kt == nkt - 1)),
                                deps=[te, Ob.wdeps()] if (c == 0 and kt == 0) else ([te] if c == 0 else ()), inc=False)
                        for c in range(2):
                            tZ = kb.op("pe", lambda e, p=p, c=c, kt=kt: e.matmul(
                                Zb.t[:, c, :], lhsT=onesb[:], rhs=p.t[:, c, :], start=(kt == 0), stop=(kt == nkt - 1)),
                                deps=[Zb.wdeps(), t_ob] if (c == 0 and kt == 0) else (), inc=(c == 1))
                        p.read(tZ)
                    Ob.wrote(tZ)
                    Zb.wrote(tZ)
                    return tZ

                def rms_store(o_buf, t_o, scale_col, gate_buf, dst, extra_deps=()):
                    t1 = kb.op("act", lambda e: e.activation(out=fsq.t[:], in_=o_buf.t[:], func=AF.Square),
                               deps=[t_o, fsq.wdeps()])
                    fsq.wrote(t1)
                    o_buf.read(t1)
                    t2 = kb.op("pe", lambda e: e.matmul(Zb.t[:, 0, :], lhsT=onesf, rhs=fsq.t[:], start=True, stop=True),
                               deps=[t1, Zb.wdeps(), t_cst])
                    fsq.read(t2)
                    Zb.wrote(t2)
                    t3 = kb.op("act", lambda e: e.activation(out=fsd.t[:], in_=Zb.t[:, 0, :], func=AF.Sqrt,
                                                            bias=epsc[:, 1:2], scale=1.0 / 128.0), deps=[t2, fsd.wdeps()])
                    Zb.read(t3)
                    t4 = kb.op("dve", lambda e: e.reciprocal(out=fsd.t[:], in_=fsd.t[:]), deps=[t3])
                    fsd.wrote(t4)
                    t5 = kb.op("dve", lambda e: e.tensor_tensor(out=o_buf.t[:], in0=o_buf.t[:], in1=fsd.t[:], op=ALU.mult),
                               deps=[t4, t1])
                    fsd.read(t5)
                    so = fob.next()
                    if gate_buf is None:
                        t6 = kb.op("dve", lambda e, so=so: e.tensor_scalar(out=so.t[:], in0=o_buf.t[:], scalar1=scale_col,
                                                                         scalar2=None, op0=ALU.mult),
                                   deps=[t5, so.wdeps(), extra_deps])
                    else:
                        t6 = kb.op("dve", lambda e, so=so: e.scalar_tensor_tensor(
                            out=so.t[:], in0=o_buf.t[:], scalar=scale_col, in1=gate_buf.t[:], op0=ALU.mult, op1=ALU.mult),
                            deps=[t5, so.wdeps(), gate_buf.rdeps(), extra_deps])
                        gate_buf.read(t6)
                    o_buf.wrote(t6)
                    so.wrote(t6)
                    so.read(kb.dma(dst, so.t[:], deps=[t6]))

                for hh in range(4 if dbg != "mem" else 0):
                    kt_ = kts.next()
                    tdk = kb.dma(kt_.t[:], kT_d[hh], deps=kt_.wdeps())
                    kt_.wrote(tdk)
                    vt_ = vts.next()
                    tdv = kb.dma(vt_.t[:], v_d[:, hh * 128:(hh + 1) * 128].rearrange("(k p) d -> p k d", p=128),
                                 deps=vt_.wdeps())
                    vt_.wrote(tdv)
                    for qt in range(8):
                        qb = qts.next()
                        tdq = kb.dma(qb.t[:], qT_d[hh][:, qt * 512:(qt + 1) * 512], deps=qb.wdeps())
                        qb.wrote(tdq)
                        first = [True]

                        def lhs_s(kt, c, kt_=kt_):
                            return kt_.t[c * 64:(c + 1) * 64, kt * 128:(kt + 1) * 128]

                        def lhs_v(kt, vt_=vt_):
                            return vt_.t[:, kt, :]
                        qb.w.append(tdk)
                        qb.w.append(tdv)
                        tZ = attn_core(qb, 32, lhs_s, lhs_v, 2)
                        qb.read(tZ)
                        kt_.read(tZ)
                        vt_.read(tZ)
                        t1a = kb.op("dve", lambda e: e.reciprocal(out=rz.t[:, 0, :], in_=Zb.t[:, 0, :]), deps=[tZ, rz.wdeps()])
                        t1 = kb.op("dve", lambda e: e.reciprocal(out=rz.t[:, 1, :], in_=Zb.t[:, 1, :]), deps=[tZ])
                        Zb.read(t1)
                        rz.wrote(t1)
                        t2 = kb.op("dve", lambda e: e.tensor_tensor(out=fa.t[:], in0=Ob.t[:, 0, :], in1=rz.t[:, 0, :], op=ALU.mult),
                                   deps=[t1, fa.wdeps()])
                        t3 = kb.op("dve", lambda e: e.tensor_tensor(out=fb.t[:], in0=Ob.t[:, 1, :], in1=rz.t[:, 1, :], op=ALU.mult),
                                   deps=[t1, fb.wdeps()])
                        Ob.read(t3)
                        rz.read(t3)
                        fa.wrote(t2)
                        fb.wrote(t3)
                        t4 = kb.op("dve", lambda e: e.scalar_tensor_tensor(out=fo.t[:], in0=fb.t[:], scalar=nlam[:, 0:1],
                                                                          in1=fa.t[:], op0=ALU.mult, op1=ALU.add),
                                   deps=[t2, t3, t_lam, fo.wdeps()])
                        fa.read(t4)
                        fb.read(t4)
                        fo.wrote(t4)
                        rms_store(fo, t4, gcol[:, 0:1], None, cat_d[hh][:, qt * 512:(qt + 1) * 512], extra_deps=[t_g])
                if dbg == "dattn":
                    kb.barrier()
                    break
                for hh in range(4):
                    for qt in range(8):
                        qb = qts.next()
                        tdq = kb.dma(qb.t[:], mq_d[hh][:, qt * 512:(qt + 1) * 512], deps=qb.wdeps())
                        qb.wrote(tdq)
                        sc = scr.next()
                        for mt in range(2):
                            tS = kb.op("pe", lambda e, sc=sc, mt=mt, hh=hh, qb=qb: e.matmul(
                                sc.t[:, mt, :], lhsT=mkT[:, hh, mt * 128:(mt + 1) * 128], rhs=qb.t[:], start=True, stop=True),
                                deps=[tdq, sc.wdeps()] if mt == 0 else (), inc=(mt == 1))
                        sc.wrote(tS)
                        qb.read(tS)
                        p = pts.next()
                        te0 = kb.op("act", lambda e, p=p, sc=sc: e.activation(
                            out=p.t[:, 0, :], in_=sc.t[:, 0, :], func=AF.Exp), deps=[tS, p.wdeps()])
                        te = kb.op("act", lambda e, p=p, sc=sc: e.activation(
                            out=p.t[:, 1, :], in_=sc.t[:, 1, :], func=AF.Exp), deps=[tS])
                        sc.read(te)
                        p.wrote(te)
                        for mt in range(2):
                            tO = kb.op("pe", lambda e, p=p, mt=mt, hh=hh: e.matmul(
                                Ob.t[:, 0, :], lhsT=mv_[:, mt, hh * 128:(hh + 1) * 128], rhs=p.t[:, mt, :],
                                start=(mt == 0), stop=(mt == 1)), deps=[te, Ob.wdeps()] if mt == 0 else (), inc=False)
                        for mt in range(2):
                            tZ = kb.op("pe", lambda e, p=p, mt=mt: e.matmul(
                                Zb.t[:, 0, :], lhsT=onesb[:], rhs=p.t[:, mt, :], start=(mt == 0), stop=(mt == 1)),
                                deps=[Zb.wdeps()] if mt == 0 else (), inc=(mt == 1))
                        p.read(tZ)
                        Ob.wrote(tZ)
                        Zb.wrote(tZ)
                        if dbg and hh == 0 and qt == 0:
                            kb.dma(dbg_p[:, :], p.t[:].rearrange("p a t -> p (a t)"), deps=[te])
                            tq1 = kb.op("dve", lambda e: e.tensor_copy(out=fa.t[:], in_=Ob.t[:, 0, :]), deps=[tZ])
                            tq2 = kb.op("dve", lambda e: e.tensor_copy(out=fb.t[:], in_=Zb.t[:, 0, :]), deps=[tZ])
                            kb.dma(dbg_oz[:, 0, :], fa.t[:], deps=[tq1])
                            kb.dma(dbg_oz[:, 1, :], fb.t[:], deps=[tq2])
                            kb.barrier()
                        t1 = kb.op("dve", lambda e: e.reciprocal(out=rz.t[:, 0, :], in_=Zb.t[:, 0, :]), deps=[tZ, rz.wdeps()])
                        Zb.read(t1)
                        rz.wrote(t1)
                        if dbg and hh == 0 and qt == 0:
                            kb.dma(dbg_oz[:, 2, :], rz.t[:, 0, :], deps=[t1])
                        so = fob.next()
                        t2 = kb.op("dve", lambda e, so=so: e.tensor_tensor(out=so.t[:], in0=Ob.t[:, 0, :], in1=rz.t[:, 0, :],
                                                                          op=ALU.mult), deps=[t1, so.wdeps()])
                        Ob.read(t2)
                        rz.read(t2)
                        so.wrote(t2)
                        so.read(kb.dma(cat_d[8 + hh][:, qt * 512:(qt + 1) * 512], so.t[:], deps=[t2]))
                kb.barrier()
            if dbg in ("mem", "attn"):
                break

            with ExitStack() as ps:
                hqs = sb("hqs", [128, S], F32, ps)
                hvs = sb("hvs", [64, NCH, 128], BF16, ps)
                Zt = [sb(f"Zt{d}", [128, S], F32, ps) for d in range(2)]
                KK = [sb(f"KK{d}", [128, S], F32, ps) for d in range(2)]
                Et = [sb(f"Et{d}", [128, S + 1], F32, ps) for d in range(2)]
                Qt = [sb(f"Qt{d}", [128, S], BF16, ps) for d in range(2)]
                Kt = [sb(f"Kt{d}", [128, S], BF16, ps) for d in range(2)]
                csc = [sb(f"csc{d}", [128, 4, NCH], F32, ps) for d in range(2)]
                Sst = [sb(f"Sst{d}", [128, 128], F32, ps) for d in range(2)]
                Stm = [sb(f"Stm{d}", [128, 128], F32, ps) for d in range(2)]
                Sbf = [[Buf(sb(f"Sbf{d}_{i}", [128, 128], BF16, ps)) for i in range(2)] for d in range(2)]
                KTs = [Ring([Buf(sb(f"KTs{d}_{i}", [64, 128], BF16, ps)) for i in range(2)]) for d in range(2)]
                ATs = [Ring([Buf(sb(f"ATs{d}_{i}", [64, 64], BF16, ps)) for i in range(2)]) for d in range(2)]
                ost = [Ring([Buf(sb(f"ost{d}_{i}", [128, 512], F32, ps)) for i in range(2)]) for d in range(2)]
                PT = [PP[2 + d].t[:].rearrange("p a t -> p (a t)").bitcast(BF16) for d in range(2)]
                for hh in range(4):
                    kb.barrier()
                    t_hq = kb.dma(hqs[:], hq_d[hh])
                    t_hv = kb.dma(hvs[:], hv_d[:, hh * 128:(hh + 1) * 128].rearrange("(c p) d -> p c d", p=HC))
                    gate_done = []
                    for d in range(2):
                        zsrc = zf_d if d == 0 else zb_d
                        tz = kb.dma(Zt[d][:], zsrc[hh])
                        lbc = lbt[:, d, l, hh:hh + 1]
                        omc = oml[:, d, l, hh:hh + 1]
                        t = kb.op("act", lambda e, d=d: e.activation(out=Zt[d][:], in_=Zt[d][:], func=AF.Sigmoid), deps=[tz])
                        t = kb.op("dve", lambda e, d=d, lbc=lbc, omc=omc: e.tensor_scalar(
                            out=Zt[d][:], in0=Zt[d][:], scalar1=omc, scalar2=lbc, op0=ALU.mult, op1=ALU.add), deps=[t])
                        tkk = kb.op("pool", lambda e, d=d: e.tensor_scalar(
                            out=KK[d][:], in0=Zt[d][:], scalar1=-1.0, scalar2=1.0, op0=ALU.mult, op1=ALU.add), deps=[t])
                        t = kb.op("dve", lambda e, d=d: e.tensor_scalar(
                            out=Zt[d][:], in0=Zt[d][:], scalar1=1e-20, scalar2=None, op0=ALU.max), deps=[t, tkk])
                        t = kb.op("act", lambda e, d=d: e.activation(out=Zt[d][:], in_=Zt[d][:], func=AF.Ln), deps=[t])
                        t0m = kb.op("dve", lambda e, d=d: e.memset(Et[d][:, 0:1], 0.0))
                        opx = ALU.add if d == 0 else ALU.subtract
                        t = kb.op("dve", lambda e, d=d, opx=opx: e.tensor_tensor_scan(
                            out=Et[d][:, 1:S + 1], data0=cst[:, 128:129].to_broadcast([128, S]), data1=Zt[d][:],
                            initial=0.0, op0=ALU.mult, op1=opx), deps=[t, t0m, t_cst])
                        eo = 1 if d == 0 else 0
                        Ev = Et[d][:, eo:eo + S].rearrange("p (c t) -> p c t", t=HC)
                        rho = Ev[:, :, HC // 2:HC // 2 + 1]
                        Dv = Zt[d][:].rearrange("p (c t) -> p c t", t=HC)
                        tD = kb.op("dve", lambda e, Dv=Dv, Ev=Ev, rho=rho: e.tensor_tensor(
                            out=Dv, in0=Ev, in1=rho.to_broadcast([128, NCH, HC]), op=ALU.subtract), deps=[t])
                        Ec = Et[d][:, 0:S].rearrange("p (c t) -> p c t", t=HC)[:, :, 0]
                        En = Et[d][:, HC:S + HC] if False else None
                        Enx = Et[d][:, 1:S + 1].rearrange("p (c t) -> p c t", t=HC)[:, :, HC - 1]
                        rh2 = Ev[:, :, HC // 2]
                        if d == 0:
                            specs3 = ((Enx, Ec), (Enx, rh2), (rh2, Ec))
                        else:
                            specs3 = ((Ec, Enx), (Ec, rh2), (rh2, Enx))
                        ts3 = []
                        for i3, (aa, bb) in enumerate(specs3):
                            ts3.append(kb.op("dve", lambda e, d=d, i3=i3, aa=aa, bb=bb: e.tensor_tensor(
                                out=csc[d][:, i3, :], in0=aa, in1=bb, op=ALU.subtract), deps=[t]))
                        tcs = kb.op("act", lambda e, d=d: e.activation(
                            out=csc[d][:, 0:3, :].rearrange("p a c -> p (a c)"),
                            in_=csc[d][:, 0:3, :].rearrange("p a c -> p (a c)"), func=AF.Exp), deps=[ts3])
                        tx = kb.op("act", lambda e, d=d: e.activation(out=Et[d][:, 0:S], in_=Zt[d][:], func=AF.Exp),
                                   deps=[tD, ts3])
                        tq = kb.op("dve", lambda e, d=d: e.tensor_tensor(out=Qt[d][:], in0=hqs[:], in1=Et[d][:, 0:S], op=ALU.mult),
                                   deps=[tx, t_hq])
                        tx2 = kb.op("act", lambda e, d=d: e.activation(out=Et[d][:, 0:S], in_=Zt[d][:], func=AF.Exp, scale=-1.0),
                                    deps=[tq])
                        tk_ = kb.op("pool", lambda e, d=d: e.tensor_tensor(out=Kt[d][:], in0=KK[d][:], in1=Et[d][:, 0:S], op=ALU.mult),
                                    deps=[tx2, tkk])
                        tS0 = kb.op("dve", lambda e, d=d: e.memset(Sst[d][:], 0.0))
                        tS1 = kb.op("pool", lambda e, d=d: e.memset(Sbf[d][0].t[:], 0.0))
                        Sbf[d][0].wrote(tS1)
                        gate_done.append([tq, tk_, tcs, tS0, tS1, t_hv])
                    pend = [None, None]
                    st_tok = [gate_done[0][3], gate_done[1][3]]
                    rdT = [[None, None], [None, None]]
                    rdA = [[None, None], [None, None]]
                    rdM = [[None, None], [None, None]]
                    for i in range(NCH + 1):
                        for d in range(2):
                            if i < NCH:
                                c = i if d == 0 else NCH - 1 - i
                                par = i % 2
                                cs = slice(c * HC, (c + 1) * HC)
                                bank = PP[d]
                                Mv = bank.t[:, 0, par * 256:par * 256 + 128]
                                Av = bank.t[0:64, 0, par * 256 + 128:par * 256 + 192]
                                Tv = PT[d][0:64, par * 128:(par + 1) * 128]
                                gd_ = gate_done[d] if i == 0 else ()
                                kts_ = KTs[d].next()
                                ats_ = ATs[d].next()
                                tT = kb.op("pe", lambda e, d=d, cs=cs, Tv=Tv: e.transpose(
                                    out=Tv, in_=Kt[d][:, cs], identity=identb[:]), deps=[gd_, t_ib, rdT[d][par]])
                                tA = kb.op("pe", lambda e, d=d, cs=cs, Av=Av: e.matmul(
                                    Av, lhsT=Kt[d][:, cs], rhs=Qt[d][:, cs], start=True, stop=True), deps=[rdA[d][par]])
                                teT = kb.op("act", lambda e, kts_=kts_, Tv=Tv: e.activation(out=kts_.t[:], in_=Tv, func=AF.Copy),
                                            deps=[tT, kts_.wdeps()])
                                kts_.wrote(teT)
                                rdT[d][par] = teT
                                teA = kb.op("dve", lambda e, ats_=ats_, Av=Av, d=d: e.tensor_tensor(
                                    out=ats_.t[:], in0=Av, in1=trib[:, d, 0, :], op=ALU.mult), deps=[tA, ats_.wdeps(), t_tb])
                                ats_.wrote(teA)
                                rdA[d][par] = teA
                                tM = kb.op("pe", lambda e, kts_=kts_, c=c, Mv=Mv: e.matmul(
                                    Mv, lhsT=kts_.t[:], rhs=hvs[:, c, :], start=True, stop=True), deps=[teT, rdM[d][par]])
                                kts_.read(tM)
                            if pend[d] is not None:
                                (pc, pi, sbuf_prev, ats_prev) = pend[d]
                                pcs = slice(pc * HC, (pc + 1) * HC)
                                slot = pi % 8
                                Ov = PP[d].t[:, 1, slot * HC:(slot + 1) * HC]
                                odeps = [sbuf_prev.rdeps(), ats_prev.rdeps()]
                                if slot == 0:
                                    odeps.append(PP[d].r.get("oev"))
                                kb.op("pe", lambda e, d=d, pcs=pcs, Ov=Ov, sbuf_prev=sbuf_prev: e.matmul(
                                    Ov, lhsT=sbuf_prev.t[:], rhs=Qt[d][:, pcs], start=True, stop=False), deps=odeps, inc=False)
                                tO = kb.op("pe", lambda e, d=d, pc=pc, Ov=Ov, ats_prev=ats_prev: e.matmul(
                                    Ov, lhsT=hvs[:, pc, :], rhs=ats_prev.t[:], start=False, stop=True))
                                sbuf_prev.read(tO)
                                ats_prev.read(tO)
                                if slot == 7:
                                    ob_ = ost[d].next()
                                    tev = kb.op("act", lambda e, d=d, ob_=ob_: e.activation(
                                        out=ob_.t[:], in_=PP[d].t[:, 1, :], func=AF.Copy), deps=[tO, ob_.wdeps()])
                                    PP[d].r["oev"] = tev
                                    ob_.wrote(tev)
                                    g8 = pi // 8
                                    if d == 0:
                                        tok0 = g8 * 512
                                        dstv = of_d[hh][:, tok0:tok0 + 512]
                                        srcv = ob_.t[:]
                                    else:
                                        tok0 = (NCH - 8 * (g8 + 1)) * HC
                                        dstv = ob_d[hh][:, tok0:tok0 + 512].rearrange("p (j t) -> p j t", t=HC)
                                        srcv = ob_.t[:].rearrange("p (j t) -> p j t", t=HC)
                                    if d == 0:
                                        ob_.read(kb.dma(dstv, srcv, deps=[tev]))
                                    else:
                                        for j in range(8):
                                            ob_.read(kb.dma(dstv[:, 7 - j, :], srcv[:, j, :], deps=[tev]))
                                pend[d] = None
                            if i < NCH:
                                cur_sb = Sbf[d][i % 2]
                                nxt_sb = Sbf[d][(i + 1) % 2]
                                pend[d] = (c, i, cur_sb, ats_)
                                tu1 = kb.op("dve", lambda e, d=d, c=c: e.tensor_scalar(
                                    out=Stm[d][:], in0=Sst[d][:], scalar1=csc[d][:, 0, c:c + 1], scalar2=None, op0=ALU.mult),
                                    deps=[st_tok[d], gd_])
                                tu2 = kb.op("dve", lambda e, d=d, c=c, Mv=Mv: e.scalar_tensor_tensor(
                                    out=Sst[d][:], in0=Mv, scalar=csc[d][:, 1, c:c + 1], in1=Stm[d][:], op0=ALU.mult, op1=ALU.add),
                                    deps=[tM, tu1])
                                st_tok[d] = tu2
                                rdM[d][par] = tu2
                                if i + 1 < NCH:
                                    cn = c + 1 if d == 0 else c - 1
                                    tsb = kb.op("act", lambda e, d=d, cn=cn, nxt_sb=nxt_sb: e.activation(
                                        out=nxt_sb.t[:], in_=Sst[d][:], func=AF.Identity, scale=csc[d][:, 2, cn:cn + 1]),
                                        deps=[tu2, nxt_sb.wdeps()])
                                    nxt_sb.wrote(tsb)
                                    st_tok[d] = [tu2, tsb]
                kb.barrier()
                if dbg == "hgrn_raw":
                    break
            with ExitStack() as ps:
                fsq = Buf(sb("fsq2", [128, 512], F32, ps))
                fsd = Buf(sb("fsd2", [128, 512], F32, ps))
                fob = Ring([Buf(sb(f"fob2{i}", [128, 512], BF16, ps)) for i in range(2)])
                lf = Ring([Buf(sb(f"lf{i}", [128, 512], F32, ps)) for i in range(2)])
                lb_ = Ring([Buf(sb(f"lb{i}", [128, 512], F32, ps)) for i in range(2)])
                lg = Ring([Buf(sb(f"lg{i}", [128, 512], F32, ps)) for i in range(2)])
                Zb = PP[3]
                chg = CO[("hgg", l)]
                for hh in range(4):
                    for qt in range(8):
                        sl = slice(qt * 512, (qt + 1) * 512)
                        a = lf.next(); b = lb_.next(); g = lg.next()
                        ta = kb.dma(a.t[:], of_d[hh][:, sl], deps=a.wdeps()); a.wrote(ta)
                        tb = kb.dma(b.t[:], ob_d[hh][:, sl], deps=b.wdeps()); b.wrote(tb)
                        tg = kb.dma(g.t[:], hg_d[hh][:, sl], deps=g.wdeps()); g.wrote(tg)
                        t1 = kb.op("pool", lambda e, a=a, b=b: e.tensor_tensor(out=a.t[:], in0=a.t[:], in1=b.t[:], op=ALU.add),
                                   deps=[ta, tb])
                        b.read(t1)
                        a.wrote(t1)
                        t2 = kb.op("act", lambda e, a=a: e.activation(out=fsq.t[:], in_=a.t[:], func=AF.Square), deps=[t1, fsq.wdeps()])
                        fsq.wrote(t2)
                        t3 = kb.op("pe", lambda e: e.matmul(Zb.t[:, 0, :], lhsT=onesf, rhs=fsq.t[:], start=True, stop=True),
                                   deps=[t2, Zb.wdeps(), t_cst])
                        fsq.read(t3)
                        Zb.wrote(t3)
                        t4 = kb.op("act", lambda e: e.activation(out=fsd.t[:], in_=Zb.t[:, 0, :], func=AF.Sqrt,
                                                                bias=epsc[:, 1:2], scale=1.0 / 128.0), deps=[t3, fsd.wdeps()])
                        Zb.read(t4)
                        t5 = kb.op("dve", lambda e: e.reciprocal(out=fsd.t[:], in_=fsd.t[:]), deps=[t4])
                        fsd.wrote(t5)
                        t6 = kb.op("dve", lambda e, a=a: e.tensor_tensor(out=a.t[:], in0=a.t[:], in1=fsd.t[:], op=ALU.mult), deps=[t5, t2])
                        fsd.read(t6)
                        so = fob.next()
                        t7 = kb.op("dve", lambda e, a=a, g=g, so=so: e.scalar_tensor_tensor(
                            out=so.t[:], in0=a.t[:], scalar=colp[:, chg:chg + 1], in1=g.t[:], op0=ALU.mult, op1=ALU.mult),
                            deps=[t6, tg, so.wdeps(), t_colp])
                        a.read(t7); g.read(t7)
                        so.wrote(t7)
                        so.read(kb.dma(cat_d[4 + hh][:, sl], so.t[:], deps=[t7]))
                kb.barrier()
            if dbg == "mix":
                break

            with ExitStack() as ps:
                wo = sb("wo", [128, 12, D], BF16, ps)
                with ExitStack() as ps2:
                    stg = Ring([Buf(sb(f"stgo{i}", [128, 2048], F32, ps2)) for i in range(3)])
                    wt = load_w(wo, w_out_d[l], 12, D, stg)
                    kb.barrier()
                G = sb("G1", [128, D], F32, ps)
                Bt = sb("B1", [128, D], F32, ps)
                gd = load_gb(G, Bt, RO[("ln1_g", l)], RO[("ln1_b", l)])
                lnb = ln_bufs(ps)
                rr = Ring([Buf(sb(f"r1_{i}", [128, D], F32, ps)) for i in range(3)])
                hr = Ring([Buf(sb(f"hTt1_{i}", [128, 8, 512], BF16, ps)) for i in range(2)])
                cin = Ring([Buf(sb(f"cin{i}", [128, 12, 512], BF16, ps)) for i in range(2)])
                accr = Ring([PP[0], PP[1], PP[2]])
                cat_v = cat_d.rearrange("c p t -> p c t")
                for tt in range(8):
                    ci = cin.next()
                    tdc = kb.dma(ci.t[:], cat_v[:, :, tt * 512:(tt + 1) * 512], deps=ci.wdeps())
                    ci.wrote(tdc)
                    hTt = hr.next()
                    toks = []
                    for s4 in range(4):
                        t0 = tt * 512 + s4 * 128
                        acc = accr.next()
                        for nn in range(2):
                            for kc in range(12):
                                tk = kb.op("pe", lambda e, acc=acc, nn=nn, kc=kc, ci=ci, s4=s4: e.matmul(
                                    acc.t[:, nn, :], lhsT=ci.t[:, kc, s4 * 128:(s4 + 1) * 128], rhs=wo[:, kc, nn * 512:(nn + 1) * 512],
                                    start=(kc == 0), stop=(kc == 11)),
                                    deps=[tdc, acc.wdeps()] if (kc == 0 and nn == 0) else (), inc=(kc == 11 and nn == 1))
                        acc.wrote(tk)
                        ci.read(tk)
                        r = rr.next()
                        td = kb.dma(r.t[:], xres_d[t0:t0 + 128, :], deps=r.wdeps())
                        r.wrote(td)
                        tr0 = kb.op("dve", lambda e, r=r, acc=acc: e.scalar_tensor_tensor(
                            out=r.t[:, 0:512], in0=r.t[:, 0:512], scalar=ALPHA, in1=acc.t[:, 0, :],
                            op0=ALU.mult, op1=ALU.add), deps=[td, tk])
                        tr = kb.op("dve", lambda e, r=r, acc=acc: e.scalar_tensor_tensor(
                            out=r.t[:, 512:1024], in0=r.t[:, 512:1024], scalar=ALPHA, in1=acc.t[:, 1, :],
                            op0=ALU.mult, op1=ALU.add), deps=[td, tk])
                        acc.read(tr)
                        toks += ln_tile(r, [tr], 128, t0, G, Bt, gd, lnb, False, hTt, s4 * 128)
                    hTt.wrote(toks[0])
                    for t in toks[1:]:
                        hTt.wrote(t, fresh=False)
                    hTt.read(kb.dma(hT_v[:, :, tt * 512:(tt + 1) * 512], hTt.t[:], deps=toks))
                kb.barrier()
            if dbg == "ln1":
                break

            with ExitStack() as ps:
                wu = sb("wu", [128, 8, 2 * DFF], BF16, ps)
                wd = sb("wd", [128, 22, D], BF16, ps)
                with ExitStack() as ps2:
                    stg = Ring([Buf(sb(f"stgf{i}", [128, 2048], F32, ps2)) for i in range(3)])
                    wt = load_w(wu, w_up_d[l], 8, 2 * DFF, stg)
                    wt += load_w(wd, w_dn_d[l], 22, D, stg)
                    kb.barrier()
                G = sb("G2", [128, D], F32, ps)
                Bt = sb("B2", [128, D], F32, ps)
                gd = load_gb(G, Bt, RO[("ln2_g", l)], RO[("ln2_b", l)])
                lnb = ln_bufs(ps)
                rr = Ring([Buf(sb(f"r2_{i}", [128, D], F32, ps)) for i in range(2)])
                hr = Ring([Buf(sb(f"hTt2_{i}", [128, 8, 256], BF16, ps)) for i in range(2)])
                hin = Ring([Buf(sb(f"hin2_{i}", [128, 8, 256], BF16, ps)) for i in range(2)])
                gT = Ring([Buf(sb(f"gT{i}", [128, 22, 256], BF16, ps)) for i in range(2)])
                ca = Ring([Buf(sb(f"ca{i}", [128, 256], F32, ps)) for i in range(2)])
                cb_ = Ring([Buf(sb(f"cb{i}", [128, 256], F32, ps)) for i in range(2)])
                cs_ = Ring([Buf(sb(f"cs{i}", [128, 256], F32, ps)) for i in range(2)])
                ur = Ring([PP[0], PP[1]])
                acc = PP[2]
                last = (l == L - 1)
                WT = 254
                ntile = (S + WT - 1) // WT
                cw0 = CO[("cw", l)]
                cb0 = CO[("cb", l)]
                for ti in range(ntile):
                    T0 = ti * WT
                    W = min(WT, S - T0)
                    lo = T0 - 1
                    h = hin.next()
                    src_lo = max(lo, 0)
                    src_hi = min(lo + W + 2, S)
                    dlo = src_lo - lo
                    hdeps = h.wdeps()
                    tl = []
                    if dlo > 0:
                        tl.append(kb.op("pool", lambda e, h=h: e.memset(h.t[:, :, 0:1], 0.0), deps=hdeps))
                    if src_hi < lo + W + 2:
                        tl.append(kb.op("pool", lambda e, h=h, W=W: e.memset(h.t[:, :, W + 1:W + 2], 0.0), deps=hdeps))
                    tl.append(kb.dma(h.t[:, :, dlo:dlo + (src_hi - src_lo)], hT_v[:, :, src_lo:src_hi], deps=hdeps))
                    h.wrote(tl[0])
                    for t in tl[1:]:
                        h.wrote(t, fresh=False)
                    g = gT.next()
                    NW = W + 2
                    gw = []
                    for fc in range(22):
                        ub = ur.next()
                        for j, col in enumerate((fc * 128, DFF + fc * 128)):
                            for kc in range(8):
                                tk = kb.op("pe", lambda e, ub=ub, j=j, kc=kc, col=col, h=h, NW=NW: e.matmul(
                                    ub.t[:, j, 0:NW], lhsT=wu[:, kc, col:col + 128], rhs=h.t[:, kc, 0:NW],
                                    start=(kc == 0), stop=(kc == 7)),
                                    deps=[h.rdeps(), ub.wdeps()] if (kc == 0 and j == 0) else (), inc=(kc == 7 and j == 1))
                        ub.wrote(tk)
                        h.read(tk)
                        a = ca.next(); b = cb_.next(); sg = cs_.next()
                        res = []
                        for j, buf in ((0, a), (1, b)):
                            ch = j * 22 + fc
                            w0 = colp[:, cw0 + 0 * 44 + ch:cw0 + 0 * 44 + ch + 1]
                            w1 = colp[:, cw0 + 1 * 44 + ch:cw0 + 1 * 44 + ch + 1]
                            w2 = colp[:, cw0 + 2 * 44 + ch:cw0 + 2 * 44 + ch + 1]
                            t1 = kb.op("act", lambda e, buf=buf, ub=ub, j=j, w1=w1, W=W: e.activation(
                                out=buf.t[:, 0:W], in_=ub.t[:, j, 1:W + 1], func=AF.Identity, scale=w1), deps=[tk, buf.wdeps(), t_colp])
                            t2 = kb.op("dve", lambda e, buf=buf, ub=ub, j=j, w0=w0, W=W: e.scalar_tensor_tensor(
                                out=buf.t[:, 0:W], in0=ub.t[:, j, 0:W], scalar=w0, in1=buf.t[:, 0:W], op0=ALU.mult, op1=ALU.add),
                                deps=[t1])
                            t3 = kb.op("dve", lambda e, buf=buf, ub=ub, j=j, w2=w2, W=W: e.scalar_tensor_tensor(
                                out=buf.t[:, 0:W], in0=ub.t[:, j, 2:W + 2], scalar=w2, in1=buf.t[:, 0:W], op0=ALU.mult, op1=ALU.add),
                                deps=[t2])
                            buf.wrote(t3)
                            res.append(t3)
                        ub.read(res[1])
                        bg = colp[:, cb0 + fc:cb0 + fc + 1]
                        bv = colp[:, cb0 + 22 + fc:cb0 + 22 + fc + 1]
                        t4 = kb.op("act", lambda e, a=a, sg=sg, bg=bg, W=W: e.activation(
                            out=sg.t[:, 0:W], in_=a.t[:, 0:W], func=AF.Silu, bias=bg, scale=1.0), deps=[res[0], sg.wdeps()])
                        a.read(t4)
                        sg.wrote(t4)
                        t5 = kb.op("dve", lambda e, b=b, sg=sg, g=g, fc=fc, bv=bv, W=W: e.scalar_tensor_tensor(
                            out=g.t[:, fc, 0:W], in0=b.t[:, 0:W], scalar=bv, in1=sg.t[:, 0:W], op0=ALU.add, op1=ALU.mult),
                            deps=[res[1], t4, g.wdeps() if fc == 0 else None])
                        b.read(t5); sg.read(t5)
                        gw.append(t5)
                    g.wrote(gw[0])
                    for t in gw[1:]:
                        g.wrote(t, fresh=False)
                    hTt = hr.next()
                    toks = []
                    s0 = 0
                    while s0 < W:
                        n = min(128, W - s0)
                        t0 = T0 + s0
                        for nn in range(2):
                            for fc in range(22):
                                tk = kb.op("pe", lambda e, nn=nn, fc=fc, g=g, s0=s0, n=n: e.matmul(
                                    acc.t[0:n, nn, :], lhsT=g.t[:, fc, s0:s0 + n], rhs=wd[:, fc, nn * 512:(nn + 1) * 512],
                                    start=(fc == 0), stop=(fc == 21)),
                                    deps=[g.rdeps(), acc.wdeps()] if (fc == 0 and nn == 0) else (), inc=(fc == 21 and nn == 1))
                        acc.wrote(tk)
                        g.read(tk)
                        r = rr.next()
                        td = kb.dma(r.t[0:n, :], xres_d[t0:t0 + n, :], deps=r.wdeps())
                        r.wrote(td)
                        tr0 = kb.op("dve", lambda e, r=r, n=n: e.scalar_tensor_tensor(
                            out=r.t[0:n, 0:512], in0=r.t[0:n, 0:512], scalar=ALPHA, in1=acc.t[0:n, 0, :],
                            op0=ALU.mult, op1=ALU.add), deps=[td, tk])
                        tr = kb.op("dve", lambda e, r=r, n=n: e.scalar_tensor_tensor(
                            out=r.t[0:n, 512:1024], in0=r.t[0:n, 512:1024], scalar=ALPHA, in1=acc.t[0:n, 1, :],
                            op0=ALU.mult, op1=ALU.add), deps=[td, tk])
                        acc.read(tr)
                        toks += ln_tile(r, [tr], n, t0, G, Bt, gd, lnb, last, hTt, s0)
                        s0 += n
                    if not last:
                        hTt.wrote(toks[0])
                        for t in toks[1:]:
                            hTt.wrote(t, fresh=False)
                        hTt.read(kb.dma(hT_v[:, :, T0:T0 + W], hTt.t[:, :, 0:W], deps=toks))
                kb.barrier()
        kb.finish(block)
    return nc


def _consts():
    c = np.zeros((128, 1024), np.float32)
    c[:, 0:128] = np.eye(128, dtype=np.float32)
    c[:, 128:256] = 1.0
    s = np.arange(64)[:, None]
    t = np.arange(64)[None, :]
    c[0:64, 256:320] = (s <= t)
    c[0:64, 320:384] = (s >= t)
    p = np.arange(128)
    dd = p % 64
    inv = (ROPE_THETA ** (-(np.arange(0, 16, 2, dtype=np.float32)) / 16.0)).astype(np.float32)
    c[:, 384] = inv[dd % 8]
    m = (dd < 16).astype(np.float32)
    c[:, 385] = m
    c[:, 386] = 1.0 - m
    c[:, 387] = np.where(dd < 8, -1.0, np.where(dd < 16, 1.0, 0.0))
    return c


def _rot_perm():
    idx = np.arange(512)
    dd = idx % 64
    base = idx - dd
    pd = np.where(dd < 8, dd + 8, np.where(dd < 16, dd - 8, dd))
    return base + pd


def _prep(inputs, L):
    CO = _col_layout(L)
    RO = _row_layout(L)
    f = lambda a: np.ascontiguousarray(np.asarray(a), dtype=np.float32)
    colp = np.zeros((128, CO["n"]), np.float32)
    lbf = f(inputs["hg_lb_fwd"])
    lbb = f(inputs["hg_lb_bwd"])
    colp[:, CO["lbf"]:CO["lbf"] + 16] = lbf.reshape(DEPTH, 4, 128).transpose(2, 0, 1).reshape(128, 16)
    colp[:, CO["lbb"]:CO["lbb"] + 16] = lbb.reshape(DEPTH, 4, 128).transpose(2, 0, 1).reshape(128, 16)
    for l in range(L):
        colp[:, CO[("dag", l)]] = f(inputs["da_norm_g"])[l]
        colp[:, CO[("hgg", l)]] = f(inputs["hg_norm_g"])[l]
        cw = f(inputs["conv_w"])[l]
        colp[:, CO[("cw", l)]:CO[("cw", l)] + 132] = cw.reshape(3, 44, 128).transpose(2, 0, 1).reshape(128, 132)
        colp[:, CO[("cb", l)]:CO[("cb", l)] + 44] = f(inputs["conv_b"])[l].reshape(44, 128).T
    rowp = np.zeros((1, RO["n"]), np.float32)
    rowp[0, RO["ln_in_g"]:RO["ln_in_g"] + D] = f(inputs["ln_in_g"])
    rowp[0, RO["ln_in_b"]:RO["ln_in_b"] + D] = f(inputs["ln_in_b"])
    for l in range(L):
        for nm in ("ln1_g", "ln1_b", "ln2_g", "ln2_b"):
            rowp[0, RO[(nm, l)]:RO[(nm, l)] + D] = f(inputs[nm])[l]
        rowp[0, RO[("lam", l)]:RO[("lam", l)] + 256] = f(inputs["da_lambda"])[l].reshape(256)
    w_in = f(inputs["w_in"])[:L]
    perm = _rot_perm()
    w_rot = np.ascontiguousarray(np.concatenate([w_in[:, :, 0:512][:, :, perm], w_in[:, :, 512:1024][:, :, perm]], axis=2))
    shared = {
        "colp": colp, "rowp": rowp, "cst": _consts(),
        "w_in": w_in, "w_rot": w_rot,
        "w_kv": f(inputs["w_mem_kv"])[:L], "w_out": f(inputs["w_out"])[:L],
        "w_up": f(inputs["w_up"])[:L], "w_dn": f(inputs["w_down"])[:L],
    }
    return shared


_NC_CACHE = {}


def kernel(**inputs):
    L = DEPTH
    x = np.asarray(inputs["x"], dtype=np.float32)
    mem = np.asarray(inputs["mem"], dtype=np.float32)
    pos = np.asarray(inputs["positions"]).astype(np.int32)
    B = x.shape[0]
    shared = _prep(inputs, L)
    if "nc" not in _NC_CACHE:
        _NC_CACHE["nc"] = build(L)
    nc = _NC_CACHE["nc"]
    in_maps = []
    for b in range(B):
        m = dict(shared)
        m["x"] = np.ascontiguousarray(x[b])
        m["mem"] = np.ascontiguousarray(mem[b])
        m["pos"] = np.ascontiguousarray(pos[b].reshape(1, S))
        in_maps.append(m)
    res = run_bass_kernel_spmd(nc, in_maps, core_ids=list(range(B)))
    return np.stack([np.asarray(r["out"], dtype=np.float32) for r in res.results], axis=0)
```

```python
import math
from contextlib import ExitStack
import numpy as np
import concourse.bass as bass
import concourse.mybir as mybir
from concourse.bass_utils import run_bass_kernel_spmd

F32 = mybir.dt.float32
BF16 = mybir.dt.bfloat16
I32 = mybir.dt.int32
AF = mybir.ActivationFunctionType
ALU = mybir.AluOpType

D = 1024
S = 4096
NM = 256
DEPTH = 4
DFF = 2816
ALPHA = (2 * DEPTH) ** 0.25
LN_EPS = 1e-5
RMS_EPS = 1e-6
ROPE_THETA = 500000.0
HC = 64
NCH = S // HC

ENGS = ("pe", "act", "dve", "pool", "sp")
NDMA = 40


class Tok:
    __slots__ = ("sem", "val", "eng", "idx")

    def __init__(self, sem, val, eng, idx):
        self.sem, self.val, self.eng, self.idx = sem, val, eng, idx


class _Rec:
    def __init__(self):
        self.call = None

    def __getattr__(self, name):
        def f(*args, **kwargs):
            self.call = (name, args, kwargs)
            return None
        return f


class KB:
    def __init__(self, nc, es):
        self.nc = nc
        self.q = {e: [] for e in ENGS}
        self.cnt = {e: 0 for e in ENGS}
        self.waited = {e: {} for e in ENGS}
        self.sems = {}
        for e in ("pe", "act", "dve", "pool"):
            self.sems[e] = es.enter_context(nc.semaphore("s_" + e))
        for i in range(NDMA):
            self.sems[("dma", i)] = es.enter_context(nc.semaphore(f"s_dma{i}"))
        self.ndma = 0
        self.dma_toks = [None] * NDMA
        self.out_toks = []
        self.last = {e: None for e in ENGS}

    def _flat(self, deps, out):
        for t in deps:
            if t is None:
                continue
            if isinstance(t, (list, tuple)):
                self._flat(t, out)
            else:
                out.append(t)
        return out

    def _waits(self, eng, deps):
        ws = []
        w = self.waited[eng]
        myidx = len(self.q[eng])
        for t in self._flat(deps, []):
            if t.eng == eng and t.sem == eng and myidx - t.idx > 3:
                continue
            if w.get(t.sem, 0) >= t.val:
                continue
            w[t.sem] = t.val
            ws.append((t.sem, t.val))
        return ws

    def op(self, eng, fn, deps=(), inc=True):
        ws = self._waits(eng, deps)
        idx = len(self.q[eng])
        tok = None
        if inc:
            self.cnt[eng] += 1
            tok = Tok(eng, self.cnt[eng], eng, idx)
            self.last[eng] = tok
        sems = self.sems
        rec = _Rec()
        fn(rec)
        name, args, kwargs = rec.call

        def run(e, ws=ws, name=name, args=args, kwargs=kwargs, inc=inc, eng=eng):
            for (s, v) in ws:
                e.wait_ge(sems[s], v)
            ins = getattr(e, name)(*args, **kwargs)
            if inc:
                ins.then_inc(sems[eng], 1)
        self.q[eng].append(run)
        return tok

    def dma(self, out, in_, deps=(), is_output=False, eng="sp", slow=False):
        i = self.ndma
        self.ndma += 1
        slot = i % NDMA
        key = ("dma", slot)
        val = 16 * (i // NDMA + 1)
        ws = self._waits(eng, list(deps) + [self.dma_toks[slot]])
        tok = Tok(key, val, eng, len(self.q[eng]))
        self.dma_toks[slot] = tok
        sems = self.sems

        def run(e, ws=ws, out=out, in_=in_, key=key, slow=slow):
            for (s, v) in ws:
                e.wait_ge(sems[s], v)
            if slow:
                e.dma_start(out=out, in_=in_, allow_slow_non_contiguous=True).then_inc(sems[key], 16)
            else:
                e.dma_start(out=out, in_=in_).then_inc(sems[key], 16)
        self.q[eng].append(run)
        if is_output:
            self.out_toks.append(tok)
        return tok

    def barrier(self):
        toks = [self.last[e] for e in ("pe", "act", "dve", "pool")] + [t for t in self.dma_toks]
        for eng in ENGS:
            ws = self._waits(eng, toks)
            sems = self.sems

            def run(e, ws=ws):
                for (s, v) in ws:
                    e.wait_ge(sems[s], v)
            self.q[eng].append(run)

    def finish(self, block):
        ws = self._waits("sp", self.out_toks)
        sems = self.sems

        def fin(e, ws=ws):
            for (s, v) in ws:
                e.wait_ge(sems[s], v)
        self.q["sp"].append(fin)
        q = self.q

        @block.sync
        def _(e):
            for f in q["sp"]:
                f(e)

        @block.tensor
        def _(e):
            for f in q["pe"]:
                f(e)

        @block.scalar
        def _(e):
            for f in q["act"]:
                f(e)

        @block.vector
        def _(e):
            for f in q["dve"]:
                f(e)

        @block.gpsimd
        def _(e):
            for f in q["pool"]:
                f(e)


class Buf:
    def __init__(self, t):
        self.t = t
        self.w = []
        self.r = {}

    def wdeps(self):
        return [self.w, list(self.r.values())]

    def wrote(self, tok, fresh=True):
        if fresh:
            self.w = [tok]
            self.r = {}
        else:
            self.w.append(tok)

    def rdeps(self):
        return self.w

    def read(self, tok):
        if tok is not None:
            self.r[tok.eng if not isinstance(tok.sem, tuple) else ("d", len(self.r))] = tok


class Ring:
    def __init__(self, bufs):
        self.bufs = bufs
        self.i = 0

    def next(self):
        b = self.bufs[self.i % len(self.bufs)]
        self.i += 1
        return b


def _col_layout(L):
    off = {}
    c = 0
    off["lbf"] = c; c += 4 * DEPTH
    off["lbb"] = c; c += 4 * DEPTH
    for l in range(L):
        off[("dag", l)] = c; c += 1
        off[("hgg", l)] = c; c += 1
        off[("cw", l)] = c; c += 3 * 44
        off[("cb", l)] = c; c += 44
    off["n"] = c
    return off


def _row_layout(L):
    off = {}
    c = 0
    off["ln_in_g"] = c; c += D
    off["ln_in_b"] = c; c += D
    for l in range(L):
        for nm in ("ln1_g", "ln1_b", "ln2_g", "ln2_b"):
            off[(nm, l)] = c; c += D
        off[("lam", l)] = c; c += 256
    off["n"] = c
    return off


def build(L=DEPTH, dbg=False):
    nc = bass.Bass("TRN2", target_bir_lowering=False)
    CO = _col_layout(L)
    RO = _row_layout(L)
    dk = "ExternalOutput" if dbg else None

    def dram(name, shape, dtype, kind=None):
        if kind is None:
            return nc.dram_tensor(name, shape, dtype).ap()
        return nc.dram_tensor(name, shape, dtype, kind=kind).ap()

    x_d = dram("x", [S, D], F32, "ExternalInput")
    mem_d = dram("mem", [NM, D], F32, "ExternalInput")
    pos_d = dram("pos", [1, S], I32, "ExternalInput")
    colp_d = dram("colp", [128, CO["n"]], F32, "ExternalInput")
    rowp_d = dram("rowp", [1, RO["n"]], F32, "ExternalInput")
    cst_d = dram("cst", [128, 1024], F32, "ExternalInput")
    w_in_d = dram("w_in", [L, D, 4608], F32, "ExternalInput")
    w_rot_d = dram("w_rot", [L, D, 1024], F32, "ExternalInput")
    w_kv_d = dram("w_kv", [L, D, 1024], F32, "ExternalInput")
    w_out_d = dram("w_out", [L, 1536, D], F32, "ExternalInput")
    w_up_d = dram("w_up", [L, D, 2 * DFF], F32, "ExternalInput")
    w_dn_d = dram("w_dn", [L, DFF, D], F32, "ExternalInput")
    out_d = dram("out", [S, D], F32, "ExternalOutput")

    xres_d = dram("xres", [S, D], F32, dk)
    hT_d = dram("hT", [8, 128, S], BF16, dk)
    rope_d = dram("rope", [4, 128, S], F32, dk)
    qT_d = dram("qT", [4, 128, S], BF16, dk)
    kT_d = dram("kT", [4, 128, S], BF16, dk)
    v_d = dram("v", [S, 512], BF16, dk)
    hq_d = dram("hq", [4, 128, S], F32, dk)
    zf_d = dram("zf", [4, 128, S], F32, dk)
    zb_d = dram("zb", [4, 128, S], F32, dk)
    hv_d = dram("hv", [S, 512], BF16, dk)
    hg_d = dram("hg", [4, 128, S], F32, dk)
    mq_d = dram("mq", [4, 128, S], BF16, dk)
    of_d = dram("of", [4, 128, S], F32, dk)
    ob_d = dram("ob", [4, 128, S], F32, dk)
    cat_d = dram("cat", [12, 128, S], BF16, dk)
    if dbg:
        dbg_mk = dram("dbg_mk", [128, 4 * NM], BF16, dk)
        dbg_mv = dram("dbg_mv", [128, 1024], BF16, dk)
        dbg_sm = dram("dbg_sm", [128, 4], F32, dk)
        dbg_memT = dram("dbg_memT", [128, 8 * NM], BF16, dk)
        dbg_p = dram("dbg_p", [128, 1024], BF16, dk)
        dbg_oz = dram("dbg_oz", [128, 3, 512], F32, dk)

    with ExitStack() as es:
        kb = KB(nc, es)
        block = es.enter_context(nc.Block())

        uniq = [0]

        def sb(name, shape, dtype, ctx=es):
            uniq[0] += 1
            return ctx.enter_context(nc.sbuf_tensor(f"{name}_{uniq[0]}", shape, dtype))

        cst = sb("cst_s", [128, 1024], F32)
        colp = sb("colp_s", [128, CO["n"]], F32)
        identb = sb("identb", [128, 128], BF16)
        onesb = sb("onesb", [128, 128], BF16)
        trib = sb("trib", [64, 2, 8, 64], BF16)
        lbt = sb("lbt", [128, 2, L, 4], F32)
        oml = sb("oml", [128, 2, L, 4], F32)
        epsc = sb("epsc", [128, 2], F32)
        memT = sb("memT", [128, 8, NM], BF16)
        PSB = [Buf(es.enter_context(nc.psum_tensor(f"psb{i}", [128, 512], F32))) for i in range(0)]
        PP = [Buf(es.enter_context(nc.psum_tensor(f"pp{i}", [128, 2, 512], F32))) for i in range(4)]

        ident = cst[:, 0:128]
        onesf = cst[:, 128:256]

        t_cst = kb.dma(cst[:], cst_d[:, :])
        t_colp = kb.dma(colp[:], colp_d[:, :])
        t_ib = kb.op("dve", lambda e: e.tensor_copy(out=identb[:], in_=cst[:, 0:128]), deps=[t_cst])
        t_ob = kb.op("dve", lambda e: e.tensor_copy(out=onesb[:], in_=cst[:, 128:256]), deps=[t_cst])
        for dr in range(2):
            for rep in range(8):
                t_tb = kb.op("dve", lambda e, dr=dr, rep=rep: e.tensor_copy(
                    out=trib[:, dr, rep, :], in_=cst[0:64, 256 + 64 * dr:320 + 64 * dr]), deps=[t_cst])
        kb.op("dve", lambda e: e.memset(epsc[:, 0:1], LN_EPS))
        kb.op("dve", lambda e: e.memset(epsc[:, 1:2], RMS_EPS))
        with ExitStack() as ps:
            ex = sb("lb_ex", [128, 2, DEPTH, 4], F32, ps)
            ssum = sb("lb_s", [128, 2, 4], F32, ps)
            t1 = kb.op("act", lambda e: e.activation(
                out=ex[:].rearrange("p a l h -> p (a l h)"), in_=colp[:, CO["lbf"]:CO["lbf"] + 8 * DEPTH], func=AF.Exp),
                deps=[t_colp])
            t2 = kb.op("dve", lambda e: e.tensor_tensor(out=ssum[:], in0=ex[:, :, 0, :], in1=ex[:, :, 1, :], op=ALU.add), deps=[t1])
            t2 = kb.op("dve", lambda e: e.tensor_tensor(out=ssum[:], in0=ssum[:], in1=ex[:, :, 2, :], op=ALU.add), deps=[t2])
            t2 = kb.op("dve", lambda e: e.tensor_tensor(out=ssum[:], in0=ssum[:], in1=ex[:, :, 3, :], op=ALU.add), deps=[t2])
            t2 = kb.op("dve", lambda e: e.reciprocal(out=ssum[:], in_=ssum[:]), deps=[t2])
            t3 = kb.op("dve", lambda e: e.memset(lbt[:, :, 0, :], 0.0))
            for l in range(1, L):
                if l == 1:
                    t3 = kb.op("dve", lambda e: e.tensor_copy(out=lbt[:, :, 1, :], in_=ex[:, :, 1, :]), deps=[t1, t3])
                else:
                    t3 = kb.op("dve", lambda e, l=l: e.tensor_tensor(out=lbt[:, :, l, :], in0=lbt[:, :, l - 1, :],
                                                                   in1=ex[:, :, l, :], op=ALU.add), deps=[t3])
            for l in range(1, L):
                t3 = kb.op("dve", lambda e, l=l: e.tensor_tensor(out=lbt[:, :, l, :], in0=lbt[:, :, l, :], in1=ssum[:],
                                                               op=ALU.mult), deps=[t3, t2])
            t3 = kb.op("dve", lambda e: e.tensor_scalar(out=oml[:].rearrange("p a l h -> p (a l h)"),
                                                       in0=lbt[:].rearrange("p a l h -> p (a l h)"),
                                                       scalar1=-1.0, scalar2=1.0, op0=ALU.mult, op1=ALU.add), deps=[t3])
            kb.barrier()

        cast_rr = [0]

        def cast(out, in_, deps):
            engs = ("dve", "pool", "act")
            eg = engs[cast_rr[0] % 3]
            cast_rr[0] += 1
            if eg == "act":
                return kb.op("act", lambda e: e.activation(out=out, in_=in_, func=AF.Copy), deps=deps)
            return kb.op(eg, lambda e: e.tensor_copy(out=out, in_=in_), deps=deps)

        def load_w(dst, src, nk, ncol, stg):
            toks = []
            step = 2048
            for k in range(nk):
                for c0 in range(0, ncol, step):
                    n = min(step, ncol - c0)
                    b = stg.next()
                    td = kb.dma(b.t[:, 0:n], src[k * 128:(k + 1) * 128, c0:c0 + n], deps=b.wdeps())
                    b.wrote(td)
                    tc = cast(dst[:, k, c0:c0 + n], b.t[:, 0:n], deps=[td])
                    b.read(tc)
                    toks.append(tc)
            return toks

        def ln_tile(r, rdeps, n, t0, G, Bt, gdeps, lnb, last, hTt, hcol):
            st, mv, sd, nmr = lnb["st"], lnb["mv"], lnb["sd"], lnb["nmr"]
            ta = kb.op("dve", lambda e: e.bn_stats(out=st.t[0:n, 0, :], in_=r.t[0:n, 0:512]), deps=[rdeps, st.wdeps()])
            tb = kb.op("dve", lambda e: e.bn_stats(out=st.t[0:n, 1, :], in_=r.t[0:n, 512:1024]), deps=[rdeps])
            st.wrote(tb)
            tc = kb.op("dve", lambda e: e.bn_aggr(out=mv.t[0:n, :], in_=st.t[0:n].rearrange("p a b -> p (a b)")),
                       deps=[ta, tb, mv.wdeps()])
            st.read(tc)
            mv.wrote(tc)
            td = kb.op("act", lambda e: e.activation(out=sd.t[0:n, :], in_=mv.t[0:n, 1:2], func=AF.Sqrt,
                                                    bias=epsc[0:n, 0:1], scale=1.0), deps=[tc, sd.wdeps()])
            sd.wrote(td)
            te = kb.op("dve", lambda e: e.reciprocal(out=sd.t[0:n, :], in_=sd.t[0:n, :]), deps=[td])
            sd.wrote(te)
            tf = kb.op("dve", lambda e: e.scalar_tensor_tensor(out=nmr.t[0:n, :], in0=mv.t[0:n, 0:1], scalar=-1.0,
                                                              in1=sd.t[0:n, :], op0=ALU.mult, op1=ALU.mult),
                       deps=[te, nmr.wdeps()])
            nmr.wrote(tf)
            mv.read(tf)
            tg = kb.op("act", lambda e: e.activation(out=r.t[0:n, :], in_=r.t[0:n, :], func=AF.Identity,
                                                    scale=sd.t[0:n, 0:1], bias=nmr.t[0:n, 0:1]), deps=[tf, te, tb])
            sd.read(tg)
            nmr.read(tg)
            th = kb.op("pool", lambda e: e.tensor_tensor(out=r.t[0:n, :], in0=r.t[0:n, :], in1=G[0:n, :], op=ALU.mult),
                       deps=[tg, gdeps])
            ti = kb.op("pool", lambda e: e.tensor_tensor(out=r.t[0:n, :], in0=r.t[0:n, :], in1=Bt[0:n, :], op=ALU.add),
                       deps=[th])
            r.wrote(ti)
            if last:
                tdm = kb.dma(out_d[t0:t0 + n, :], r.t[0:n, :], deps=[ti], is_output=True)
                r.read(tdm)
                return []
            tdm = kb.dma(xres_d[t0:t0 + n, :], r.t[0:n, :], deps=[ti])
            r.read(tdm)
            toks = []
            for hf in range(2):
                pb = lnb["pt"].next()
                for j in range(4):
                    kc = hf * 4 + j
                    tk = kb.op("pe", lambda e, pb=pb, j=j, kc=kc: e.transpose(
                        out=pb.t[:, 0, j * 128:j * 128 + n], in_=r.t[0:n, kc * 128:(kc + 1) * 128],
                        identity=ident[0:n, 0:n]), deps=[ti, pb.wdeps() if j == 0 else None, t_cst], inc=(j == 3))
                pb.wrote(tk)
                r.read(tk)
                src = pb.t[:, 0, :].rearrange("p (j t) -> p j t", j=4)[:, :, 0:n]
                dst = hTt.t[:, hf * 4:(hf + 1) * 4, hcol:hcol + n]
                if hf == 0:
                    te2 = kb.op("act", lambda e, src=src, dst=dst: e.activation(out=dst, in_=src, func=AF.Copy),
                                deps=[tk, hTt.wdeps()])
                else:
                    te2 = kb.op("dve", lambda e, src=src, dst=dst: e.tensor_copy(out=dst, in_=src),
                                deps=[tk, hTt.wdeps()])
                pb.read(te2)
                toks.append(te2)
            return toks

        def ln_bufs(ctx):
            return {
                "st": Buf(sb("ln_st", [128, 2, 6], F32, ctx)),
                "mv": Buf(sb("ln_mv", [128, 2], F32, ctx)),
                "sd": Buf(sb("ln_sd", [128, 1], F32, ctx)),
                "nmr": Buf(sb("ln_nmr", [128, 1], F32, ctx)),
                "pt": Ring([PP[3]]),
            }

        def load_gb(G, Bt, og, ob_):
            ta = kb.dma(G[:], rowp_d[0:1, og:og + D].to_broadcast([128, D]))
            tb = kb.dma(Bt[:], rowp_d[0:1, ob_:ob_ + D].to_broadcast([128, D]))
            return [ta, tb]

        hT_v = hT_d.rearrange("k p t -> p k t")

        with ExitStack() as ps:
            mt_ = sb("memf", [128, 2, D], F32, ps)
            td = kb.dma(mt_[:], mem_d.rearrange("(a p) d -> p a d", p=128))
            for a in range(2):
                for hf in range(2):
                    pb = PP[hf]
                    for j in range(4):
                        kc = hf * 4 + j
                        tk = kb.op("pe", lambda e, pb=pb, j=j, kc=kc, a=a: e.transpose(
                            out=pb.t[:, 0, j * 128:(j + 1) * 128], in_=mt_[:, a, kc * 128:(kc + 1) * 128],
                            identity=ident), deps=[td, t_cst, pb.wdeps() if j == 0 else None], inc=(j == 3))
                    pb.wrote(tk)
                    te = kb.op("dve", lambda e, pb=pb, hf=hf, a=a: e.tensor_copy(
                        out=memT[:, hf * 4:(hf + 1) * 4, a * 128:(a + 1) * 128],
                        in_=pb.t[:, 0, :].rearrange("p (j t) -> p j t", j=4)), deps=[tk])
                    pb.read(te)
            posi = sb("posi", [128, S], I32, ps)
            ang = sb("ang", [128, S], F32, ps)
            a2 = sb("ang2", [128, S], F32, ps)
            ki = sb("ki", [128, S], I32, ps)
            kf = sb("kf", [128, S], F32, ps)
            tp = kb.dma(posi[:], pos_d[0:1, :].to_broadcast([128, S]))
            t0_ = kb.op("dve", lambda e: e.tensor_copy(out=ang[:], in_=posi[:]), deps=[tp])
            t0_ = kb.op("dve", lambda e: e.tensor_scalar(out=ang[:], in0=ang[:], scalar1=cst[:, 384:385], scalar2=None,
                                                        op0=ALU.mult), deps=[t0_, t_cst])
            TWO_PI = 2.0 * math.pi

            def reduce_sin(shift, out_scale_col, dst_idx_list):
                t = kb.op("dve", lambda e: e.tensor_scalar(out=kf[:], in0=ang[:], scalar1=shift, scalar2=1.0 / TWO_PI,
                                                          op0=ALU.add, op1=ALU.mult), deps=[t0_])
                t = kb.op("dve", lambda e: e.tensor_copy(out=ki[:], in_=kf[:]), deps=[t])
                t = kb.op("dve", lambda e: e.tensor_copy(out=kf[:], in_=ki[:]), deps=[t])
                t = kb.op("dve", lambda e: e.scalar_tensor_tensor(out=a2[:], in0=kf[:], scalar=-TWO_PI, in1=ang[:],
                                                                 op0=ALU.mult, op1=ALU.add), deps=[t])
                if shift != 0.0:
                    t = kb.op("dve", lambda e: e.tensor_scalar(out=a2[:], in0=a2[:], scalar1=shift, scalar2=None,
                                                              op0=ALU.add), deps=[t])
                t = kb.op("dve", lambda e: e.tensor_scalar(out=kf[:], in0=a2[:], scalar1=math.pi, scalar2=-TWO_PI,
                                                          op0=ALU.is_gt, op1=ALU.mult), deps=[t])
                t = kb.op("dve", lambda e: e.tensor_tensor(out=a2[:], in0=a2[:], in1=kf[:], op=ALU.add), deps=[t])
                t = kb.op("dve", lambda e: e.tensor_scalar(out=kf[:], in0=a2[:], scalar1=-math.pi, scalar2=TWO_PI,
                                                          op0=ALU.is_lt, op1=ALU.mult), deps=[t])
                t = kb.op("dve", lambda e: e.tensor_tensor(out=a2[:], in0=a2[:], in1=kf[:], op=ALU.add), deps=[t])
                t = kb.op("dve", lambda e: e.tensor_scalar(out=a2[:], in0=a2[:], scalar1=-3.14159, scalar2=3.14159,
                                                          op0=ALU.max, op1=ALU.min), deps=[t])
                t = kb.op("act", lambda e: e.activation(out=a2[:], in_=a2[:], func=AF.Sin), deps=[t])
                return t

            t = reduce_sin(math.pi / 2.0, None, None)
            t = kb.op("dve", lambda e: e.tensor_scalar(out=kf[:], in0=a2[:], scalar1=cst[:, 385:386],
                                                      scalar2=cst[:, 386:387], op0=ALU.mult, op1=ALU.add), deps=[t])
            tck = kb.dma(rope_d[2], kf[:], deps=[t])
            t = kb.op("pool", lambda e: e.tensor_scalar(out=a2[:], in0=kf[:], scalar1=0.125, scalar2=None, op0=ALU.mult),
                      deps=[t])
            tcq = kb.dma(rope_d[0], a2[:], deps=[t])
            t = reduce_sin(0.0, None, None) if False else None
            kb.barrier()
            t = reduce_sin(0.0, None, None)
            t = kb.op("dve", lambda e: e.tensor_scalar(out=kf[:], in0=a2[:], scalar1=cst[:, 387:388], scalar2=None,
                                                      op0=ALU.mult), deps=[t])
            kb.dma(rope_d[3], kf[:], deps=[t])
            t = kb.op("pool", lambda e: e.tensor_scalar(out=a2[:], in0=kf[:], scalar1=0.125, scalar2=None, op0=ALU.mult),
                      deps=[t])
            kb.dma(rope_d[1], a2[:], deps=[t])
            kb.barrier()

        with ExitStack() as ps:
            G = sb("G0", [128, D], F32, ps)
            Bt = sb("B0", [128, D], F32, ps)
            gd = load_gb(G, Bt, RO["ln_in_g"], RO["ln_in_b"])
            lnb = ln_bufs(ps)
            rr = Ring([Buf(sb(f"r0_{i}", [128, D], F32, ps)) for i in range(3)])
            hr = Ring([Buf(sb(f"hTt0_{i}", [128, 8, 512], BF16, ps)) for i in range(2)])
            for tt in range(S // 512):
                hTt = hr.next()
                toks = []
                for s4 in range(4):
                    t0 = tt * 512 + s4 * 128
                    r = rr.next()
                    td = kb.dma(r.t[:], x_d[t0:t0 + 128, :], deps=r.wdeps())
                    r.wrote(td)
                    toks += ln_tile(r, [td], 128, t0, G, Bt, gd, lnb, False, hTt, s4 * 128)
                hTt.wrote(toks[0])
                for t in toks[1:]:
                    hTt.wrote(t, fresh=False)
                tdm = kb.dma(hT_v[:, :, tt * 512:(tt + 1) * 512], hTt.t[:], deps=toks)
                hTt.read(tdm)
            kb.barrier()

        for l in range(L):
            lam_init = 0.8 - 0.6 * math.exp(-0.3 * l)
            with ExitStack() as ps:
                wi = sb("wi", [128, 8, 4608], BF16, ps)
                wr = sb("wr", [128, 8, 1024], BF16, ps)
                with ExitStack() as ps2:
                    stg = Ring([Buf(sb(f"stg{i}", [128, 2048], F32, ps2)) for i in range(3)])
                    wtoks = load_w(wi, w_in_d[l], 8, 4608, stg)
                    wtoks += load_w(wr, w_rot_d[l], 8, 1024, stg)
                    kb.barrier()
                hin = Ring([Buf(sb(f"hin{i}", [128, 8, 512], BF16, ps)) for i in range(2)])
                rtab = Ring([Buf(sb(f"rtab{i}", [128, 4, 512], F32, ps)) for i in range(2)])
                tmpA = Ring([Buf(sb(f"tmpA{i}", [128, 512], F32, ps)) for i in range(2)])
                tmpB = Ring([Buf(sb(f"tmpB{i}", [128, 512], F32, ps)) for i in range(2)])
                stF = Ring([Buf(sb(f"stF{i}", [128, 512], F32, ps)) for i in range(4)])
                stH = Ring([Buf(sb(f"stH{i}", [128, 512], BF16, ps)) for i in range(4)])
                pr = Ring(PP)

                def mm_group(dst, kc_list, lhs_fn, rhs_fn, deps):
                    tk = None
                    nk = len(kc_list)
                    for i, kc in enumerate(kc_list):
                        tk = kb.op("pe", lambda e, kc=kc, i=i: e.matmul(dst, lhsT=lhs_fn(kc), rhs=rhs_fn(kc),
                                                                      start=(i == 0), stop=(i == nk - 1)),
                                   deps=deps if i == 0 else (), inc=(i == nk - 1))
                    return tk

                for tt in range(8):
                    c0 = tt * 512
                    h = hin.next()
                    td = kb.dma(h.t[:], hT_v[:, :, c0:c0 + 512], deps=h.wdeps())
                    h.wrote(td)
                    rt = rtab.next()
                    td2 = kb.dma(rt.t[:], rope_d[:, :, c0:c0 + 512].rearrange("a p t -> p a t"), deps=rt.wdeps())
                    rt.wrote(td2)
                    for which in range(2):
                        for hh in range(4):
                            col = which * 512 + hh * 128
                            pb = pr.next()
                            tA = mm_group(pb.t[:, 0, :], range(8), lambda kc, col=col: wi[:, kc, col:col + 128],
                                          lambda kc, h=h: h.t[:, kc, :], [td, pb.wdeps()])
                            tB = mm_group(pb.t[:, 1, :], range(8), lambda kc, col=col: wr[:, kc, col:col + 128],
                                          lambda kc, h=h: h.t[:, kc, :], [])
                            pb.wrote(tB)
                            h.read(tB)
                            a = tmpA.next()
                            b = tmpB.next()
                            t1 = kb.op("dve", lambda e, a=a, pb=pb, rt=rt, which=which: e.tensor_tensor(
                                out=a.t[:], in0=pb.t[:, 0, :], in1=rt.t[:, 2 * which, :], op=ALU.mult),
                                deps=[tA, td2, a.wdeps()])
                            t2 = kb.op("dve", lambda e, b=b, pb=pb, rt=rt, which=which: e.tensor_tensor(
                                out=b.t[:], in0=pb.t[:, 1, :], in1=rt.t[:, 2 * which + 1, :], op=ALU.mult),
                                deps=[tB, td2, b.wdeps()])
                            pb.read(t2)
                            rt.read(t2)
                            a.wrote(t1)
                            b.wrote(t2)
                            so = stH.next()
                            t3 = kb.op("pool", lambda e, a=a, b=b, so=so: e.tensor_tensor(
                                out=so.t[:], in0=a.t[:], in1=b.t[:], op=ALU.add), deps=[t1, t2, so.wdeps()])
                            a.read(t3)
                            b.read(t3)
                            so.wrote(t3)
                            dst = (qT_d if which == 0 else kT_d)[hh][:, c0:c0 + 512]
                            so.read(kb.dma(dst, so.t[:], deps=[t3]))
                    specs = []
                    for hh in range(4):
                        specs.append((1536 + hh * 128, hq_d[hh], AF.Silu, 1.0, False))
                        specs.append((2048 + hh * 128, zf_d[hh], AF.Copy, 1.0, False))
                        specs.append((2560 + hh * 128, zb_d[hh], AF.Copy, 1.0, False))
                        specs.append((3584 + hh * 128, hg_d[hh], AF.Silu, 1.0, False))
                        specs.append((4096 + hh * 128, mq_d[hh], AF.Copy, 128.0 ** -0.5, True))
                    for i in range(0, len(specs), 2):
                        pb = pr.next()
                        tks = []
                        for j in range(2):
                            col = specs[i + j][0]
                            tks.append(mm_group(pb.t[:, j, :], range(8), lambda kc, col=col: wi[:, kc, col:col + 128],
                                                lambda kc, h=h: h.t[:, kc, :], [td, pb.wdeps()] if j == 0 else []))
                        pb.wrote(tks[1])
                        h.read(tks[1])
                        for j in range(2):
                            col, dst, fn, sc, isb = specs[i + j]
                            so = (stH if isb else stF).next()
                            if fn == AF.Copy and not isb and (i + j) % 2 == 0:
                                te = kb.op("dve", lambda e, so=so, pb=pb, j=j: e.tensor_copy(out=so.t[:], in_=pb.t[:, j, :]),
                                           deps=[tks[j], so.wdeps()])
                            else:
                                te = kb.op("act", lambda e, so=so, pb=pb, j=j, fn=fn, sc=sc: e.activation(
                                    out=so.t[:], in_=pb.t[:, j, :], func=fn, scale=sc), deps=[tks[j], so.wdeps()])
                            pb.read(te)
                            so.wrote(te)
                            so.read(kb.dma(dst[:, c0:c0 + 512], so.t[:], deps=[te]))
                    for (col, dst) in ((1024, v_d), (3072, hv_d)):
                        for s2 in range(2):
                            pb = pr.next()
                            tks = []
                            for j in range(2):
                                sub = s2 * 2 + j
                                tks.append(mm_group(pb.t[:, j, :], range(8),
                                                    lambda kc, h=h, sub=sub: h.t[:, kc, sub * 128:(sub + 1) * 128],
                                                    lambda kc, col=col: wi[:, kc, col:col + 512],
                                                    [td, pb.wdeps()] if j == 0 else []))
                            pb.wrote(tks[1])
                            h.read(tks[1])
                            for j in range(2):
                                sub = s2 * 2 + j
                                so = stH.next()
                                if j == 0:
                                    te = kb.op("dve", lambda e, so=so, pb=pb, j=j: e.tensor_copy(out=so.t[:], in_=pb.t[:, j, :]),
                                               deps=[tks[j], so.wdeps()])
                                else:
                                    te = kb.op("act", lambda e, so=so, pb=pb, j=j: e.activation(
                                        out=so.t[:], in_=pb.t[:, j, :], func=AF.Copy), deps=[tks[j], so.wdeps()])
                                pb.read(te)
                                so.wrote(te)
                                so.read(kb.dma(dst[c0 + sub * 128:c0 + (sub + 1) * 128, :], so.t[:], deps=[te]))
                kb.barrier()
            if dbg and dbg == "inproj":
                break

            with ExitStack() as ps:
                lamt = sb("lamt", [128, 256], F32, ps)
                lamp = sb("lamp", [128, 2, 64], F32, ps)
                lams = sb("lams", [128, 2], F32, ps)
                nlam = sb("nlam", [128, 1], F32, ps)
                gcol = sb("gcol", [128, 1], F32, ps)
                ro = RO[("lam", l)]
                td = kb.dma(lamt[:], rowp_d[0:1, ro:ro + 256].to_broadcast([128, 256]))
                lv = lamt[:].rearrange("p (a d) -> p a d", a=4)
                t = kb.op("dve", lambda e: e.tensor_tensor(out=lamp[:, 0, :], in0=lv[:, 0, :], in1=lv[:, 1, :], op=ALU.mult), deps=[td])
                t = kb.op("dve", lambda e: e.tensor_tensor(out=lamp[:, 1, :], in0=lv[:, 2, :], in1=lv[:, 3, :], op=ALU.mult), deps=[td, t])
                t = kb.op("dve", lambda e: e.tensor_reduce(out=lams[:], in_=lamp[:], axis=mybir.AxisListType.X, op=ALU.add), deps=[t])
                t = kb.op("act", lambda e: e.activation(out=lams[:], in_=lams[:], func=AF.Exp), deps=[t])
                t = kb.op("dve", lambda e: e.tensor_tensor(out=nlam[:], in0=lams[:, 1:2], in1=lams[:, 0:1], op=ALU.subtract), deps=[t])
                t = kb.op("dve", lambda e: e.tensor_scalar(out=nlam[:], in0=nlam[:], scalar1=-lam_init, scalar2=None, op0=ALU.add), deps=[t])
                t_lam = t
                cdg = CO[("dag", l)]
                t_g = kb.op("dve", lambda e: e.tensor_scalar(out=gcol[:], in0=colp[:, cdg:cdg + 1], scalar1=1.0 - lam_init,
                                                            scalar2=None, op0=ALU.mult), deps=[t_colp])
                mkT = sb("mkT", [128, 4, NM], BF16, ps)
                mv_ = sb("mv", [128, 2, 512], BF16, ps)
                with ExitStack() as ps2:
                    wkv = sb("wkv", [128, 8, 1024], BF16, ps2)
                    stg = Ring([Buf(sb(f"stgm{i}", [128, 2048], F32, ps2)) for i in range(3)])
                    wt = load_w(wkv, w_kv_d[l], 8, 1024, stg)
                    for hp in range(2):
                        pb = PP[hp]
                        for j in range(2):
                            hh = hp * 2 + j
                            for kc in range(8):
                                tk = kb.op("pe", lambda e, pb=pb, j=j, kc=kc, hh=hh: e.matmul(
                                    pb.t[:, j, 0:NM], lhsT=wkv[:, kc, hh * 128:(hh + 1) * 128], rhs=memT[:, kc, :],
                                    start=(kc == 0), stop=(kc == 7)), deps=[wt] if kc == 0 else (), inc=(kc == 7))
                            te = kb.op("act", lambda e, pb=pb, j=j, hh=hh: e.activation(
                                out=mkT[:, hh, :], in_=pb.t[:, j, 0:NM], func=AF.Copy), deps=[tk])
                    pb = PP[2]
                    for mt in range(2):
                        for kc in range(8):
                            tk = kb.op("pe", lambda e, mt=mt, kc=kc: e.matmul(
                                PP[2].t[:, mt, :], lhsT=memT[:, kc, mt * 128:(mt + 1) * 128], rhs=wkv[:, kc, 512:1024],
                                start=(kc == 0), stop=(kc == 7)), deps=[wt] if kc == 0 else (), inc=(kc == 7))
                        te = kb.op("dve", lambda e, mt=mt: e.tensor_copy(out=mv_[:, mt, :], in_=PP[2].t[:, mt, :]), deps=[tk])
                    kb.barrier()
                    if dbg:
                        kb.dma(dbg_mk[:, :], mkT[:].rearrange("p a m -> p (a m)"))
                        kb.dma(dbg_mv[:, :], mv_[:].rearrange("p a m -> p (a m)"))
                        kb.dma(dbg_memT[:, :], memT[:].rearrange("p a m -> p (a m)"))
                        kb.dma(dbg_sm[:, 0:1], nlam[:], slow=True)
                        kb.dma(dbg_sm[:, 1:2], gcol[:], slow=True)
                        kb.dma(dbg_sm[:, 2:4], lams[:], slow=True)
                        kb.barrier()

                kts = Ring([Buf(sb(f"kts{i}", [128, S], BF16, ps)) for i in range(2)])
                vts = Ring([Buf(sb(f"vts{i}", [128, 32, 128], BF16, ps)) for i in range(2)])
                qts = Ring([Buf(sb(f"qts{i}", [128, 512], BF16, ps)) for i in range(2)])
                pts = Ring([Buf(sb(f"pts{i}", [128, 2, 512], BF16, ps)) for i in range(3)])
                rz = Buf(sb("rz", [128, 2, 512], F32, ps))
                fa = Buf(sb("fa", [128, 512], F32, ps))
                fb = Buf(sb("fb", [128, 512], F32, ps))
                fo = Buf(sb("fo", [128, 512], F32, ps))
                fsq = Buf(sb("fsq", [128, 512], F32, ps))
                fsd = Buf(sb("fsd", [128, 512], F32, ps))
                fob = Ring([Buf(sb(f"fob{i}", [128, 512], BF16, ps)) for i in range(2)])
                scr = Ring([PP[0], PP[1]])
                Ob, Zb = PP[2], PP[3]

                def attn_core(qb, nkt, lhs_s, lhs_v, ncomp):
                    def issue_S(kt):
                        sc = scr.next()
                        tS = None
                        for c in range(2):
                            tS = kb.op("pe", lambda e, sc=sc, c=c, kt=kt: e.matmul(
                                sc.t[:, c, :], lhsT=lhs_s(kt, c), rhs=qb.t[c * 64:(c + 1) * 64, :], start=True, stop=True),
                                deps=[qb.rdeps(), sc.wdeps()] if c == 0 else (), inc=(c == 1))
                        sc.wrote(tS)
                        return sc, tS
                    tZ = None
                    pend = issue_S(0)
                    for kt in range(nkt):
                        sc, tS = pend
                        pend = issue_S(kt + 1) if kt + 1 < nkt else None
                        p = pts.next()
                        te0 = kb.op("act", lambda e, p=p, sc=sc: e.activation(
                            out=p.t[:, 0, :], in_=sc.t[:, 0, :], func=AF.Exp), deps=[tS, p.wdeps()])
                        te = kb.op("act", lambda e, p=p, sc=sc: e.activation(
                            out=p.t[:, 1, :], in_=sc.t[:, 1, :], func=AF.Exp), deps=[tS])
                        sc.read(te)
                        p.wrote(te)
                        for c in range(2):
                            kb.op("pe", lambda e, p=p, c=c, kt=kt: e.matmul(
                                Ob.t[:, c, :], lhsT=lhs_v(kt), rhs=p.t[:, c, :], start=(kt == 0), stop=(kt == nkt - 1)),
                                deps=[te0 if c == 0 else te, Ob.wdeps() if (c == 0 and kt == 0) else None], inc=False)
                        for c in range(2):
                            tZ = kb.op("pe", lambda e, p=p, c=c, kt=kt: e.matmul(
                                Zb.t[:, c, :], lhsT=onesb[:], rhs=p.t[:, c, :], start=(kt == 0), stop=(kt == nkt - 1)),
                                deps=[Zb.wdeps(), t_ob] if (c == 0 and kt == 0) else (), inc=(c == 1))
                        p.read(tZ)
                    Ob.wrote(tZ)
                    Zb.wrote(tZ)
                    return tZ

                def rms_store(o_buf, t_o, scale_col, gate_buf, dst, extra_deps=()):
                    t1 = kb.op("act", lambda e: e.activation(out=fsq.t[:], in_=o_buf.t[:], func=AF.Square),
                               deps=[t_o, fsq.wdeps()])
                    fsq.wrote(t1)
                    o_buf.read(t1)
                    t2 = kb.op("pe", lambda e: e.matmul(Zb.t[:, 0, :], lhsT=onesf, rhs=fsq.t[:], start=True, stop=True),
                               deps=[t1, Zb.wdeps(), t_cst])
                    fsq.read(t2)
                    Zb.wrote(t2)
                    t3 = kb.op("act", lambda e: e.activation(out=fsd.t[:], in_=Zb.t[:, 0, :], func=AF.Sqrt,
                                                            bias=epsc[:, 1:2], scale=1.0 / 128.0), deps=[t2, fsd.wdeps()])
                    Zb.read(t3)
                    t4 = kb.op("dve", lambda e: e.reciprocal(out=fsd.t[:], in_=fsd.t[:]), deps=[t3])
                    fsd.wrote(t4)
                    t5 = kb.op("dve", lambda e: e.tensor_tensor(out=o_buf.t[:], in0=o_buf.t[:], in1=fsd.t[:], op=ALU.mult),
                               deps=[t4, t1])
                    fsd.read(t5)
                    so = fob.next()
                    if gate_buf is None:
                        t6 = kb.op("dve", lambda e, so=so: e.tensor_scalar(out=so.t[:], in0=o_buf.t[:], scalar1=scale_col,
                                                                         scalar2=None, op0=ALU.mult),
                                   deps=[t5, so.wdeps(), extra_deps])
                    else:
                        t6 = kb.op("dve", lambda e, so=so: e.scalar_tensor_tensor(
                            out=so.t[:], in0=o_buf.t[:], scalar=scale_col, in1=gate_buf.t[:], op0=ALU.mult, op1=ALU.mult),
                            deps=[t5, so.wdeps(), gate_buf.rdeps(), extra_deps])
                        gate_buf.read(t6)
                    o_buf.wrote(t6)
                    so.wrote(t6)
                    so.read(kb.dma(dst, so.t[:], deps=[t6]))

                for hh in range(4 if dbg != "mem" else 0):
                    kt_ = kts.next()
                    tdk = kb.dma(kt_.t[:], kT_d[hh], deps=kt_.wdeps())
                    kt_.wrote(tdk)
                    vt_ = vts.next()
                    tdv = kb.dma(vt_.t[:], v_d[:, hh * 128:(hh + 1) * 128].rearrange("(k p) d -> p k d", p=128),
                                 deps=vt_.wdeps())
                    vt_.wrote(tdv)
                    for qt in range(8):
                        qb = qts.next()
                        tdq = kb.dma(qb.t[:], qT_d[hh][:, qt * 512:(qt + 1) * 512], deps=qb.wdeps())
                        qb.wrote(tdq)
                        first = [True]

                        def lhs_s(kt, c, kt_=kt_):
                            return kt_.t[c * 64:(c + 1) * 64, kt * 128:(kt + 1) * 128]

                        def lhs_v(kt, vt_=vt_):
                            return vt_.t[:, kt, :]
                        qb.w.append(tdk)
                        qb.w.append(tdv)
                        tZ = attn_core(qb, 32, lhs_s, lhs_v, 2)
                        qb.read(tZ)
                        kt_.read(tZ)
                        vt_.read(tZ)
                        t1a = kb.op("dve", lambda e: e.reciprocal(out=rz.t[:, 0, :], in_=Zb.t[:, 0, :]), deps=[tZ, rz.wdeps()])
                        t1 = kb.op("dve", lambda e: e.reciprocal(out=rz.t[:, 1, :], in_=Zb.t[:, 1, :]), deps=[tZ])
                        Zb.read(t1)
                        rz.wrote(t1)
                        t2 = kb.op("dve", lambda e: e.tensor_tensor(out=fa.t[:], in0=Ob.t[:, 0, :], in1=rz.t[:, 0, :], op=ALU.mult),
                                   deps=[t1, fa.wdeps()])
                        t3 = kb.op("dve", lambda e: e.tensor_tensor(out=fb.t[:], in0=Ob.t[:, 1, :], in1=rz.t[:, 1, :], op=ALU.mult),
                                   deps=[t1, fb.wdeps()])
                        Ob.read(t3)
                        rz.read(t3)
                        fa.wrote(t2)
                        fb.wrote(t3)
                        t4 = kb.op("dve", lambda e: e.scalar_tensor_tensor(out=fo.t[:], in0=fb.t[:], scalar=nlam[:, 0:1],
                                                                          in1=fa.t[:], op0=ALU.mult, op1=ALU.add),
                                   deps=[t2, t3, t_lam, fo.wdeps()])
                        fa.read(t4)
                        fb.read(t4)
                        fo.wrote(t4)
                        rms_store(fo, t4, gcol[:, 0:1], None, cat_d[hh][:, qt * 512:(qt + 1) * 512], extra_deps=[t_g])
                if dbg == "dattn":
                    kb.barrier()
                    break
                for hh in range(4):
                    for qt in range(8):
                        qb = qts.next()
                        tdq = kb.dma(qb.t[:], mq_d[hh][:, qt * 512:(qt + 1) * 512], deps=qb.wdeps())
                        qb.wrote(tdq)
                        sc = scr.next()
                        for mt in range(2):
                            tS = kb.op("pe", lambda e, sc=sc, mt=mt, hh=hh, qb=qb: e.matmul(
                                sc.t[:, mt, :], lhsT=mkT[:, hh, mt * 128:(mt + 1) * 128], rhs=qb.t[:], start=True, stop=True),
                                deps=[tdq, sc.wdeps()] if mt == 0 else (), inc=(mt == 1))
                        sc.wrote(tS)
                        qb.read(tS)
                        p = pts.next()
                        te0 = kb.op("act", lambda e, p=p, sc=sc: e.activation(
                            out=p.t[:, 0, :], in_=sc.t[:, 0, :], func=AF.Exp), deps=[tS, p.wdeps()])
                        te = kb.op("act", lambda e, p=p, sc=sc: e.activation(
                            out=p.t[:, 1, :], in_=sc.t[:, 1, :], func=AF.Exp), deps=[tS])
                        sc.read(te)
                        p.wrote(te)
                        for mt in range(2):
                            tO = kb.op("pe", lambda e, p=p, mt=mt, hh=hh: e.matmul(
                                Ob.t[:, 0, :], lhsT=mv_[:, mt, hh * 128:(hh + 1) * 128], rhs=p.t[:, mt, :],
                                start=(mt == 0), stop=(mt == 1)), deps=[te, Ob.wdeps()] if mt == 0 else (), inc=False)
                        for mt in range(2):
                            tZ = kb.op("pe", lambda e, p=p, mt=mt: e.matmul(
                                Zb.t[:, 0, :], lhsT=onesb[:], rhs=p.t[:, mt, :], start=(mt == 0), stop=(mt == 1)),
                                deps=[Zb.wdeps()] if mt == 0 else (), inc=(mt == 1))
                        p.read(tZ)
                        Ob.wrote(tZ)
                        Zb.wrote(tZ)
                        if dbg and hh == 0 and qt == 0:
                            kb.dma(dbg_p[:, :], p.t[:].rearrange("p a t -> p (a t)"), deps=[te])
                            tq1 = kb.op("dve", lambda e: e.tensor_copy(out=fa.t[:], in_=Ob.t[:, 0, :]), deps=[tZ])
                            tq2 = kb.op("dve", lambda e: e.tensor_copy(out=fb.t[:], in_=Zb.t[:, 0, :]), deps=[tZ])
                            kb.dma(dbg_oz[:, 0, :], fa.t[:], deps=[tq1])
                            kb.dma(dbg_oz[:, 1, :], fb.t[:], deps=[tq2])
                            kb.barrier()
                        t1 = kb.op("dve", lambda e: e.reciprocal(out=rz.t[:, 0, :], in_=Zb.t[:, 0, :]), deps=[tZ, rz.wdeps()])
                        Zb.read(t1)
                        rz.wrote(t1)
                        if dbg and hh == 0 and qt == 0:
                            kb.dma(dbg_oz[:, 2, :], rz.t[:, 0, :], deps=[t1])
                        so = fob.next()
                        t2 = kb.op("dve", lambda e, so=so: e.tensor_tensor(out=so.t[:], in0=Ob.t[:, 0, :], in1=rz.t[:, 0, :],
                                                                          op=ALU.mult), deps=[t1, so.wdeps()])
                        Ob.read(t2)
                        rz.read(t2)
                        so.wrote(t2)
                        so.read(kb.dma(cat_d[8 + hh][:, qt * 512:(qt + 1) * 512], so.t[:], deps=[t2]))
                kb.barrier()
            if dbg in ("mem", "attn"):
                break

            with ExitStack() as ps:
                hqs = sb("hqs", [128, S], F32, ps)
                hvs = sb("hvs", [64, NCH, 128], BF16, ps)
                Zt = [sb(f"Zt{d}", [128, S], F32, ps) for d in range(2)]
                KK = [sb(f"KK{d}", [128, S], F32, ps) for d in range(2)]
                Et = [sb(f"Et{d}", [128, S + 1], F32, ps) for d in range(2)]
                Qt = [sb(f"Qt{d}", [128, S], BF16, ps) for d in range(2)]
                Kt = [sb(f"Kt{d}", [128, S], BF16, ps) for d in range(2)]
                csc = [sb(f"csc{d}", [128, 4, NCH], F32, ps) for d in range(2)]
                Sst = [sb(f"Sst{d}", [128, 128], F32, ps) for d in range(2)]
                Stm = [sb(f"Stm{d}", [128, 128], F32, ps) for d in range(2)]
                Sbf = [[Buf(sb(f"Sbf{d}_{i}", [128, 128], BF16, ps)) for i in range(2)] for d in range(2)]
                KTs = [Ring([Buf(sb(f"KTs{d}_{i}", [64, 128], BF16, ps)) for i in range(2)]) for d in range(2)]
                ATs = [Ring([Buf(sb(f"ATs{d}_{i}", [64, 64], BF16, ps)) for i in range(2)]) for d in range(2)]
                ost = [Ring([Buf(sb(f"ost{d}_{i}", [128, 512], F32, ps)) for i in range(2)]) for d in range(2)]
                PT = [PP[2 + d].t[:].rearrange("p a t -> p (a t)").bitcast(BF16) for d in range(2)]
                for hh in range(4):
                    kb.barrier()
                    t_hq = kb.dma(hqs[:], hq_d[hh])
                    t_hv = kb.dma(hvs[:], hv_d[:, hh * 128:(hh + 1) * 128].rearrange("(c p) d -> p c d", p=HC))
                    gate_done = []
                    for d in range(2):
                        zsrc = zf_d if d == 0 else zb_d
                        tz = kb.dma(Zt[d][:], zsrc[hh])
                        lbc = lbt[:, d, l, hh:hh + 1]
                        omc = oml[:, d, l, hh:hh + 1]
                        t = kb.op("act", lambda e, d=d: e.activation(out=Zt[d][:], in_=Zt[d][:], func=AF.Sigmoid), deps=[tz])
                        t = kb.op("dve", lambda e, d=d, lbc=lbc, omc=omc: e.tensor_scalar(
                            out=Zt[d][:], in0=Zt[d][:], scalar1=omc, scalar2=lbc, op0=ALU.mult, op1=ALU.add), deps=[t])
                        tkk = kb.op("pool", lambda e, d=d: e.tensor_scalar(
                            out=KK[d][:], in0=Zt[d][:], scalar1=-1.0, scalar2=1.0, op0=ALU.mult, op1=ALU.add), deps=[t])
                        t = kb.op("dve", lambda e, d=d: e.tensor_scalar(
                            out=Zt[d][:], in0=Zt[d][:], scalar1=1e-20, scalar2=None, op0=ALU.max), deps=[t, tkk])
                        t = kb.op("act", lambda e, d=d: e.activation(out=Zt[d][:], in_=Zt[d][:], func=AF.Ln), deps=[t])
                        t0m = kb.op("dve", lambda e, d=d: e.memset(Et[d][:, 0:1], 0.0))
                        opx = ALU.add if d == 0 else ALU.subtract
                        t = kb.op("dve", lambda e, d=d, opx=opx: e.tensor_tensor_scan(
                            out=Et[d][:, 1:S + 1], data0=cst[:, 128:129].to_broadcast([128, S]), data1=Zt[d][:],
                            initial=0.0, op0=ALU.mult, op1=opx), deps=[t, t0m, t_cst])
                        eo = 1 if d == 0 else 0
                        Ev = Et[d][:, eo:eo + S].rearrange("p (c t) -> p c t", t=HC)
                        rho = Ev[:, :, HC // 2:HC // 2 + 1]
                        Dv = Zt[d][:].rearrange("p (c t) -> p c t", t=HC)
                        tD = kb.op("dve", lambda e, Dv=Dv, Ev=Ev, rho=rho: e.tensor_tensor(
                            out=Dv, in0=Ev, in1=rho.to_broadcast([128, NCH, HC]), op=ALU.subtract), deps=[t])
                        Ec = Et[d][:, 0:S].rearrange("p (c t) -> p c t", t=HC)[:, :, 0]
                        En = Et[d][:, HC:S + HC] if False else None
                        Enx = Et[d][:, 1:S + 1].rearrange("p (c t) -> p c t", t=HC)[:, :, HC - 1]
                        rh2 = Ev[:, :, HC // 2]
                        if d == 0:
                            specs3 = ((Enx, Ec), (Enx, rh2), (rh2, Ec))
                        else:
                            specs3 = ((Ec, Enx), (Ec, rh2), (rh2, Enx))
                        ts3 = []
                        for i3, (aa, bb) in enumerate(specs3):
                            ts3.append(kb.op("dve", lambda e, d=d, i3=i3, aa=aa, bb=bb: e.tensor_tensor(
                                out=csc[d][:, i3, :], in0=aa, in1=bb, op=ALU.subtract), deps=[t]))
                        tcs = kb.op("act", lambda e, d=d: e.activation(
                            out=csc[d][:, 0:3, :].rearrange("p a c -> p (a c)"),
                            in_=csc[d][:, 0:3, :].rearrange("p a c -> p (a c)"), func=AF.Exp), deps=[ts3])
                        tx = kb.op("act", lambda e, d=d: e.activation(out=Et[d][:, 0:S], in_=Zt[d][:], func=AF.Exp),
                                   deps=[tD, ts3])
                        tq = kb.op("dve", lambda e, d=d: e.tensor_tensor(out=Qt[d][:], in0=hqs[:], in1=Et[d][:, 0:S], op=ALU.mult),
                                   deps=[tx, t_hq])
                        tx2 = kb.op("act", lambda e, d=d: e.activation(out=Et[d][:, 0:S], in_=Zt[d][:], func=AF.Exp, scale=-1.0),
                                    deps=[tq])
                        tk_ = kb.op("pool", lambda e, d=d: e.tensor_tensor(out=Kt[d][:], in0=KK[d][:], in1=Et[d][:, 0:S], op=ALU.mult),
                                    deps=[tx2, tkk])
                        tS0 = kb.op("dve", lambda e, d=d: e.memset(Sst[d][:], 0.0))
                        tS1 = kb.op("pool", lambda e, d=d: e.memset(Sbf[d][0].t[:], 0.0))
                        Sbf[d][0].wrote(tS1)
                        gate_done.append([tq, tk_, tcs, tS0, tS1, t_hv])
                    pend = [None, None]
                    st_tok = [gate_done[0][3], gate_done[1][3]]
                    rdT = [[None, None], [None, None]]
                    rdA = [[None, None], [None, None]]
                    rdM = [[None, None], [None, None]]
                    for i in range(NCH + 1):
                        for d in range(2):
                            if i < NCH:
                                c = i if d == 0 else NCH - 1 - i
                                par = i % 2
                                cs = slice(c * HC, (c + 1) * HC)
                                bank = PP[d]
                                Mv = bank.t[:, 0, par * 256:par * 256 + 128]
                                Av = bank.t[0:64, 0, par * 256 + 128:par * 256 + 192]
                                Tv = PT[d][0:64, par * 128:(par + 1) * 128]
                                gd_ = gate_done[d] if i == 0 else ()
                                kts_ = KTs[d].next()
                                ats_ = ATs[d].next()
                                tT = kb.op("pe", lambda e, d=d, cs=cs, Tv=Tv: e.transpose(
                                    out=Tv, in_=Kt[d][:, cs], identity=identb[:]), deps=[gd_, t_ib, rdT[d][par]])
                                tA = kb.op("pe", lambda e, d=d, cs=cs, Av=Av: e.matmul(
                                    Av, lhsT=Kt[d][:, cs], rhs=Qt[d][:, cs], start=True, stop=True), deps=[rdA[d][par]])
                                teT = kb.op("act", lambda e, kts_=kts_, Tv=Tv: e.activation(out=kts_.t[:], in_=Tv, func=AF.Copy),
                                            deps=[tT, kts_.wdeps()])
                                kts_.wrote(teT)
                                rdT[d][par] = teT
                                teA = kb.op("dve", lambda e, ats_=ats_, Av=Av, d=d: e.tensor_tensor(
                                    out=ats_.t[:], in0=Av, in1=trib[:, d, 0, :], op=ALU.mult), deps=[tA, ats_.wdeps(), t_tb])
                                ats_.wrote(teA)
                                rdA[d][par] = teA
                                tM = kb.op("pe", lambda e, kts_=kts_, c=c, Mv=Mv: e.matmul(
                                    Mv, lhsT=kts_.t[:], rhs=hvs[:, c, :], start=True, stop=True), deps=[teT, rdM[d][par]])
                                kts_.read(tM)
                            if pend[d] is not None:
                                (pc, pi, sbuf_prev, ats_prev) = pend[d]
                                pcs = slice(pc * HC, (pc + 1) * HC)
                                slot = pi % 8
                                Ov = PP[d].t[:, 1, slot * HC:(slot + 1) * HC]
                                odeps = [sbuf_prev.rdeps(), ats_prev.rdeps()]
                                if slot == 0:
                                    odeps.append(PP[d].r.get("oev"))
                                kb.op("pe", lambda e, d=d, pcs=pcs, Ov=Ov, sbuf_prev=sbuf_prev: e.matmul(
                                    Ov, lhsT=sbuf_prev.t[:], rhs=Qt[d][:, pcs], start=True, stop=False), deps=odeps, inc=False)
                                tO = kb.op("pe", lambda e, d=d, pc=pc, Ov=Ov, ats_prev=ats_prev: e.matmul(
                                    Ov, lhsT=hvs[:, pc, :], rhs=ats_prev.t[:], start=False, stop=True))
                                sbuf_prev.read(tO)
                                ats_prev.read(tO)
                                if slot == 7:
                                    ob_ = ost[d].next()
                                    tev = kb.op("act", lambda e, d=d, ob_=ob_: e.activation(
                                        out=ob_.t[:], in_=PP[d].t[:, 1, :], func=AF.Copy), deps=[tO, ob_.wdeps()])
                                    PP[d].r["oev"] = tev
                                    ob_.wrote(tev)
                                    g8 = pi // 8
                                    if d == 0:
                                        tok0 = g8 * 512
                                        dstv = of_d[hh][:, tok0:tok0 + 512]
                                        srcv = ob_.t[:]
                                    else:
                                        tok0 = (NCH - 8 * (g8 + 1)) * HC
                                        dstv = ob_d[hh][:, tok0:tok0 + 512].rearrange("p (j t) -> p j t", t=HC)
                                        srcv = ob_.t[:].rearrange("p (j t) -> p j t", t=HC)
                                    if d == 0:
                                        ob_.read(kb.dma(dstv, srcv, deps=[tev]))
                                    else:
                                        for j in range(8):
                                            ob_.read(kb.dma(dstv[:, 7 - j, :], srcv[:, j, :], deps=[tev]))
                                pend[d] = None
                            if i < NCH:
                                cur_sb = Sbf[d][i % 2]
                                nxt_sb = Sbf[d][(i + 1) % 2]
                                pend[d] = (c, i, cur_sb, ats_)
                                tu1 = kb.op("dve", lambda e, d=d, c=c: e.tensor_scalar(
                                    out=Stm[d][:], in0=Sst[d][:], scalar1=csc[d][:, 0, c:c + 1], scalar2=None, op0=ALU.mult),
                                    deps=[st_tok[d], gd_])
                                tu2 = kb.op("dve", lambda e, d=d, c=c, Mv=Mv: e.scalar_tensor_tensor(
                                    out=Sst[d][:], in0=Mv, scalar=csc[d][:, 1, c:c + 1], in1=Stm[d][:], op0=ALU.mult, op1=ALU.add),
                                    deps=[tM, tu1])
                                st_tok[d] = tu2
                                rdM[d][par] = tu2
                                if i + 1 < NCH:
                                    cn = c + 1 if d == 0 else c - 1
                                    tsb = kb.op("act", lambda e, d=d, cn=cn, nxt_sb=nxt_sb: e.activation(
                                        out=nxt_sb.t[:], in_=Sst[d][:], func=AF.Identity, scale=csc[d][:, 2, cn:cn + 1]),
                                        deps=[tu2, nxt_sb.wdeps()])
                                    nxt_sb.wrote(tsb)
                                    st_tok[d] = [tu2, tsb]
                kb.barrier()
                if dbg == "hgrn_raw":
                    break
            with ExitStack() as ps:
                fsq = Buf(sb("fsq2", [128, 512], F32, ps))
                fsd = Buf(sb("fsd2", [128, 512], F32, ps))
                fob = Ring([Buf(sb(f"fob2{i}", [128, 512], BF16, ps)) for i in range(2)])
                lf = Ring([Buf(sb(f"lf{i}", [128, 512], F32, ps)) for i in range(2)])
                lb_ = Ring([Buf(sb(f"lb{i}", [128, 512], F32, ps)) for i in range(2)])
                lg = Ring([Buf(sb(f"lg{i}", [128, 512], F32, ps)) for i in range(2)])
                Zb = PP[3]
                chg = CO[("hgg", l)]
                for hh in range(4):
                    for qt in range(8):
                        sl = slice(qt * 512, (qt + 1) * 512)
                        a = lf.next(); b = lb_.next(); g = lg.next()
                        ta = kb.dma(a.t[:], of_d[hh][:, sl], deps=a.wdeps()); a.wrote(ta)
                        tb = kb.dma(b.t[:], ob_d[hh][:, sl], deps=b.wdeps()); b.wrote(tb)
                        tg = kb.dma(g.t[:], hg_d[hh][:, sl], deps=g.wdeps()); g.wrote(tg)
                        t1 = kb.op("pool", lambda e, a=a, b=b: e.tensor_tensor(out=a.t[:], in0=a.t[:], in1=b.t[:], op=ALU.add),
                                   deps=[ta, tb])
                        b.read(t1)
                        a.wrote(t1)
                        t2 = kb.op("act", lambda e, a=a: e.activation(out=fsq.t[:], in_=a.t[:], func=AF.Square), deps=[t1, fsq.wdeps()])
                        fsq.wrote(t2)
                        t3 = kb.op("pe", lambda e: e.matmul(Zb.t[:, 0, :], lhsT=onesf, rhs=fsq.t[:], start=True, stop=True),
                                   deps=[t2, Zb.wdeps(), t_cst])
                        fsq.read(t3)
                        Zb.wrote(t3)
                        t4 = kb.op("act", lambda e: e.activation(out=fsd.t[:], in_=Zb.t[:, 0, :], func=AF.Sqrt,
                                                                bias=epsc[:, 1:2], scale=1.0 / 128.0), deps=[t3, fsd.wdeps()])
                        Zb.read(t4)
                        t5 = kb.op("dve", lambda e: e.reciprocal(out=fsd.t[:], in_=fsd.t[:]), deps=[t4])
                        fsd.wrote(t5)
                        t6 = kb.op("dve", lambda e, a=a: e.tensor_tensor(out=a.t[:], in0=a.t[:], in1=fsd.t[:], op=ALU.mult), deps=[t5, t2])
                        fsd.read(t6)
                        so = fob.next()
                        t7 = kb.op("dve", lambda e, a=a, g=g, so=so: e.scalar_tensor_tensor(
                            out=so.t[:], in0=a.t[:], scalar=colp[:, chg:chg + 1], in1=g.t[:], op0=ALU.mult, op1=ALU.mult),
                            deps=[t6, tg, so.wdeps(), t_colp])
                        a.read(t7); g.read(t7)
                        so.wrote(t7)
                        so.read(kb.dma(cat_d[4 + hh][:, sl], so.t[:], deps=[t7]))
                kb.barrier()
            if dbg == "mix":
                break

            with ExitStack() as ps:
                wo = sb("wo", [128, 12, D], BF16, ps)
                with ExitStack() as ps2:
                    stg = Ring([Buf(sb(f"stgo{i}", [128, 2048], F32, ps2)) for i in range(3)])
                    wt = load_w(wo, w_out_d[l], 12, D, stg)
                    kb.barrier()
                G = sb("G1", [128, D], F32, ps)
                Bt = sb("B1", [128, D], F32, ps)
                gd = load_gb(G, Bt, RO[("ln1_g", l)], RO[("ln1_b", l)])
                lnb = ln_bufs(ps)
                rr = Ring([Buf(sb(f"r1_{i}", [128, D], F32, ps)) for i in range(3)])
                hr = Ring([Buf(sb(f"hTt1_{i}", [128, 8, 512], BF16, ps)) for i in range(2)])
                cin = Ring([Buf(sb(f"cin{i}", [128, 12, 512], BF16, ps)) for i in range(2)])
                accr = Ring([PP[0], PP[1], PP[2]])
                cat_v = cat_d.rearrange("c p t -> p c t")
                for tt in range(8):
                    ci = cin.next()
                    tdc = kb.dma(ci.t[:], cat_v[:, :, tt * 512:(tt + 1) * 512], deps=ci.wdeps())
                    ci.wrote(tdc)
                    hTt = hr.next()
                    toks = []
                    for s4 in range(4):
                        t0 = tt * 512 + s4 * 128
                        acc = accr.next()
                        for nn in range(2):
                            for kc in range(12):
                                tk = kb.op("pe", lambda e, acc=acc, nn=nn, kc=kc, ci=ci, s4=s4: e.matmul(
                                    acc.t[:, nn, :], lhsT=ci.t[:, kc, s4 * 128:(s4 + 1) * 128], rhs=wo[:, kc, nn * 512:(nn + 1) * 512],
                                    start=(kc == 0), stop=(kc == 11)),
                                    deps=[tdc, acc.wdeps()] if (kc == 0 and nn == 0) else (), inc=(kc == 11 and nn == 1))
                        acc.wrote(tk)
                        ci.read(tk)
                        r = rr.next()
                        td = kb.dma(r.t[:], xres_d[t0:t0 + 128, :], deps=r.wdeps())
                        r.wrote(td)
                        tr0 = kb.op("dve", lambda e, r=r, acc=acc: e.scalar_tensor_tensor(
                            out=r.t[:, 0:512], in0=r.t[:, 0:512], scalar=ALPHA, in1=acc.t[:, 0, :],
                            op0=ALU.mult, op1=ALU.add), deps=[td, tk])
                        tr = kb.op("dve", lambda e, r=r, acc=acc: e.scalar_tensor_tensor(
                            out=r.t[:, 512:1024], in0=r.t[:, 512:1024], scalar=ALPHA, in1=acc.t[:, 1, :],
                            op0=ALU.mult, op1=ALU.add), deps=[td, tk])
                        acc.read(tr)
                        toks += ln_tile(r, [tr], 128, t0, G, Bt, gd, lnb, False, hTt, s4 * 128)
                    hTt.wrote(toks[0])
                    for t in toks[1:]:
                        hTt.wrote(t, fresh=False)
                    hTt.read(kb.dma(hT_v[:, :, tt * 512:(tt + 1) * 512], hTt.t[:], deps=toks))
                kb.barrier()
            if dbg == "ln1":
                break

            with ExitStack() as ps:
                wu = sb("wu", [128, 8, 2 * DFF], BF16, ps)
                wd = sb("wd", [128, 22, D], BF16, ps)
                with ExitStack() as ps2:
                    stg = Ring([Buf(sb(f"stgf{i}", [128, 2048], F32, ps2)) for i in range(3)])
                    wt = load_w(wu, w_up_d[l], 8, 2 * DFF, stg)
                    wt += load_w(wd, w_dn_d[l], 22, D, stg)
                    kb.barrier()
                G = sb("G2", [128, D], F32, ps)
                Bt = sb("B2", [128, D], F32, ps)
                gd = load_gb(G, Bt, RO[("ln2_g", l)], RO[("ln2_b", l)])
                lnb = ln_bufs(ps)
                rr = Ring([Buf(sb(f"r2_{i}", [128, D], F32, ps)) for i in range(2)])
                hr = Ring([Buf(sb(f"hTt2_{i}", [128, 8, 256], BF16, ps)) for i in range(2)])
                hin = Ring([Buf(sb(f"hin2_{i}", [128, 8, 256], BF16, ps)) for i in range(2)])
                gT = Ring([Buf(sb(f"gT{i}", [128, 22, 256], BF16, ps)) for i in range(2)])
                ca = Ring([Buf(sb(f"ca{i}", [128, 256], F32, ps)) for i in range(2)])
                cb_ = Ring([Buf(sb(f"cb{i}", [128, 256], F32, ps)) for i in range(2)])
                cs_ = Ring([Buf(sb(f"cs{i}", [128, 256], F32, ps)) for i in range(2)])
                ur = Ring([PP[0], PP[1]])
                acc = PP[2]
                last = (l == L - 1)
                WT = 254
                ntile = (S + WT - 1) // WT
                cw0 = CO[("cw", l)]
                cb0 = CO[("cb", l)]
                for ti in range(ntile):
                    T0 = ti * WT
                    W = min(WT, S - T0)
                    lo = T0 - 1
                    h = hin.next()
                    src_lo = max(lo, 0)
                    src_hi = min(lo + W + 2, S)
                    dlo = src_lo - lo
                    hdeps = h.wdeps()
                    tl = []
                    if dlo > 0:
                        tl.append(kb.op("pool", lambda e, h=h: e.memset(h.t[:, :, 0:1], 0.0), deps=hdeps))
                    if src_hi < lo + W + 2:
                        tl.append(kb.op("pool", lambda e, h=h, W=W: e.memset(h.t[:, :, W + 1:W + 2], 0.0), deps=hdeps))
                    tl.append(kb.dma(h.t[:, :, dlo:dlo + (src_hi - src_lo)], hT_v[:, :, src_lo:src_hi], deps=hdeps))
                    h.wrote(tl[0])
                    for t in tl[1:]:
                        h.wrote(t, fresh=False)
                    g = gT.next()
                    NW = W + 2
                    gw = []
                    for fc in range(22):
                        ub = ur.next()
                        for j, col in enumerate((fc * 128, DFF + fc * 128)):
                            for kc in range(8):
                                tk = kb.op("pe", lambda e, ub=ub, j=j, kc=kc, col=col, h=h, NW=NW: e.matmul(
                                    ub.t[:, j, 0:NW], lhsT=wu[:, kc, col:col + 128], rhs=h.t[:, kc, 0:NW],
                                    start=(kc == 0), stop=(kc == 7)),
                                    deps=[h.rdeps(), ub.wdeps()] if (kc == 0 and j == 0) else (), inc=(kc == 7 and j == 1))
                        ub.wrote(tk)
                        h.read(tk)
                        a = ca.next(); b = cb_.next(); sg = cs_.next()
                        res = []
                        for j, buf in ((0, a), (1, b)):
                            ch = j * 22 + fc
                            w0 = colp[:, cw0 + 0 * 44 + ch:cw0 + 0 * 44 + ch + 1]
                            w1 = colp[:, cw0 + 1 * 44 + ch:cw0 + 1 * 44 + ch + 1]
                            w2 = colp[:, cw0 + 2 * 44 + ch:cw0 + 2 * 44 + ch + 1]
                            t1 = kb.op("act", lambda e, buf=buf, ub=ub, j=j, w1=w1, W=W: e.activation(
                                out=buf.t[:, 0:W], in_=ub.t[:, j, 1:W + 1], func=AF.Identity, scale=w1), deps=[tk, buf.wdeps(), t_colp])
                            t2 = kb.op("dve", lambda e, buf=buf, ub=ub, j=j, w0=w0, W=W: e.scalar_tensor_tensor(
                                out=buf.t[:, 0:W], in0=ub.t[:, j, 0:W], scalar=w0, in1=buf.t[:, 0:W], op0=ALU.mult, op1=ALU.add),
                                deps=[t1])
                            t3 = kb.op("dve", lambda e, buf=buf, ub=ub, j=j, w2=w2, W=W: e.scalar_tensor_tensor(
                                out=buf.t[:, 0:W], in0=ub.t[:, j, 2:W + 2], scalar=w2, in1=buf.t[:, 0:W], op0=ALU.mult, op1=ALU.add),
                                deps=[t2])
                            buf.wrote(t3)
                            res.append(t3)
                        ub.read(res[1])
                        bg = colp[:, cb0 + fc:cb0 + fc + 1]
                        bv = colp[:, cb0 + 22 + fc:cb0 + 22 + fc + 1]
                        t4 = kb.op("act", lambda e, a=a, sg=sg, bg=bg, W=W: e.activation(
                            out=sg.t[:, 0:W], in_=a.t[:, 0:W], func=AF.Silu, bias=bg, scale=1.0), deps=[res[0], sg.wdeps()])
                        a.read(t4)
                        sg.wrote(t4)
                        t5 = kb.op("dve", lambda e, b=b, sg=sg, g=g, fc=fc, bv=bv, W=W: e.scalar_tensor_tensor(
                            out=g.t[:, fc, 0:W], in0=b.t[:, 0:W], scalar=bv, in1=sg.t[:, 0:W], op0=ALU.add, op1=ALU.mult),
                            deps=[res[1], t4, g.wdeps() if fc == 0 else None])
                        b.read(t5); sg.read(t5)
                        gw.append(t5)
                    g.wrote(gw[0])
                    for t in gw[1:]:
                        g.wrote(t, fresh=False)
                    hTt = hr.next()
                    toks = []
                    s0 = 0
                    while s0 < W:
                        n = min(128, W - s0)
                        t0 = T0 + s0
                        for nn in range(2):
                            for fc in range(22):
                                tk = kb.op("pe", lambda e, nn=nn, fc=fc, g=g, s0=s0, n=n: e.matmul(
                                    acc.t[0:n, nn, :], lhsT=g.t[:, fc, s0:s0 + n], rhs=wd[:, fc, nn * 512:(nn + 1) * 512],
                                    start=(fc == 0), stop=(fc == 21)),
                                    deps=[g.rdeps(), acc.wdeps()] if (fc == 0 and nn == 0) else (), inc=(fc == 21 and nn == 1))
                        acc.wrote(tk)
                        g.read(tk)
                        r = rr.next()
                        td = kb.dma(r.t[0:n, :], xres_d[t0:t0 + n, :], deps=r.wdeps())
                        r.wrote(td)
                        tr0 = kb.op("dve", lambda e, r=r, n=n: e.scalar_tensor_tensor(
                            out=r.t[0:n, 0:512], in0=r.t[0:n, 0:512], scalar=ALPHA, in1=acc.t[0:n, 0, :],
                            op0=ALU.mult, op1=ALU.add), deps=[td, tk])
                        tr = kb.op("dve", lambda e, r=r, n=n: e.scalar_tensor_tensor(
                            out=r.t[0:n, 512:1024], in0=r.t[0:n, 512:1024], scalar=ALPHA, in1=acc.t[0:n, 1, :],
                            op0=ALU.mult, op1=ALU.add), deps=[td, tk])
                        acc.read(tr)
                        toks += ln_tile(r, [tr], n, t0, G, Bt, gd, lnb, last, hTt, s0)
                        s0 += n
                    if not last:
                        hTt.wrote(toks[0])
                        for t in toks[1:]:
                            hTt.wrote(t, fresh=False)
                        hTt.read(kb.dma(hT_v[:, :, T0:T0 + W], hTt.t[:, :, 0:W], deps=toks))
                kb.barrier()
        kb.finish(block)
    return nc


def _consts():
    c = np.zeros((128, 1024), np.float32)
    c[:, 0:128] = np.eye(128, dtype=np.float32)
    c[:, 128:256] = 1.0
    s = np.arange(64)[:, None]
    t = np.arange(64)[None, :]
    c[0:64, 256:320] = (s <= t)
    c[0:64, 320:384] = (s >= t)
    p = np.arange(128)
    dd = p % 64
    inv = (ROPE_THETA ** (-(np.arange(0, 16, 2, dtype=np.float32)) / 16.0)).astype(np.float32)
    c[:, 384] = inv[dd % 8]
    m = (dd < 16).astype(np.float32)
    c[:, 385] = m
    c[:, 386] = 1.0 - m
    c[:, 387] = np.where(dd < 8, -1.0, np.where(dd < 16, 1.0, 0.0))
    return c


def _rot_perm():
    idx = np.arange(512)
    dd = idx % 64
    base = idx - dd
    pd = np.where(dd < 8, dd + 8, np.where(dd < 16, dd - 8, dd))
    return base + pd


def _prep(inputs, L):
    CO = _col_layout(L)
    RO = _row_layout(L)
    f = lambda a: np.ascontiguousarray(np.asarray(a), dtype=np.float32)
    colp = np.zeros((128, CO["n"]), np.float32)
    lbf = f(inputs["hg_lb_fwd"])
    lbb = f(inputs["hg_lb_bwd"])
    colp[:, CO["lbf"]:CO["lbf"] + 16] = lbf.reshape(DEPTH, 4, 128).transpose(2, 0, 1).reshape(128, 16)
    colp[:, CO["lbb"]:CO["lbb"] + 16] = lbb.reshape(DEPTH, 4, 128).transpose(2, 0, 1).reshape(128, 16)
    for l in range(L):
        colp[:, CO[("dag", l)]] = f(inputs["da_norm_g"])[l]
        colp[:, CO[("hgg", l)]] = f(inputs["hg_norm_g"])[l]
        cw = f(inputs["conv_w"])[l]
        colp[:, CO[("cw", l)]:CO[("cw", l)] + 132] = cw.reshape(3, 44, 128).transpose(2, 0, 1).reshape(128, 132)
        colp[:, CO[("cb", l)]:CO[("cb", l)] + 44] = f(inputs["conv_b"])[l].reshape(44, 128).T
    rowp = np.zeros((1, RO["n"]), np.float32)
    rowp[0, RO["ln_in_g"]:RO["ln_in_g"] + D] = f(inputs["ln_in_g"])
    rowp[0, RO["ln_in_b"]:RO["ln_in_b"] + D] = f(inputs["ln_in_b"])
    for l in range(L):
        for nm in ("ln1_g", "ln1_b", "ln2_g", "ln2_b"):
            rowp[0, RO[(nm, l)]:RO[(nm, l)] + D] = f(inputs[nm])[l]
        rowp[0, RO[("lam", l)]:RO[("lam", l)] + 256] = f(inputs["da_lambda"])[l].reshape(256)
    w_in = f(inputs["w_in"])[:L]
    perm = _rot_perm()
    w_rot = np.ascontiguousarray(np.concatenate([w_in[:, :, 0:512][:, :, perm], w_in[:, :, 512:1024][:, :, perm]], axis=2))
    shared = {
        "colp": colp, "rowp": rowp, "cst": _consts(),
        "w_in": w_in, "w_rot": w_rot,
        "w_kv": f(inputs["w_mem_kv"])[:L], "w_out": f(inputs["w_out"])[:L],
        "w_up": f(inputs["w_up"])[:L], "w_dn": f(inputs["w_down"])[:L],
    }
    return shared


_NC_CACHE = {}


def kernel(**inputs):
    L = DEPTH
    x = np.asarray(inputs["x"], dtype=np.float32)
    mem = np.asarray(inputs["mem"], dtype=np.float32)
    pos = np.asarray(inputs["positions"]).astype(np.int32)
    B = x.shape[0]
    shared = _prep(inputs, L)
    if "nc" not in _NC_CACHE:
        _NC_CACHE["nc"] = build(L)
    nc = _NC_CACHE["nc"]
    in_maps = []
    for b in range(B):
        m = dict(shared)
        m["x"] = np.ascontiguousarray(x[b])
        m["mem"] = np.ascontiguousarray(mem[b])
        m["pos"] = np.ascontiguousarray(pos[b].reshape(1, S))
        in_maps.append(m)
    res = run_bass_kernel_spmd(nc, in_maps, core_ids=list(range(B)))
    return np.stack([np.asarray(r["out"], dtype=np.float32) for r in res.results], axis=0)
```

```python
import math
from contextlib import ExitStack
import numpy as np
import concourse.bass as bass
import concourse.mybir as mybir
from concourse.bass_utils import run_bass_kernel_spmd

F32 = mybir.dt.float32
BF16 = mybir.dt.bfloat16
I32 = mybir.dt.int32
AF = mybir.ActivationFunctionType
ALU = mybir.AluOpType

D = 1024
S = 4096
NM = 256
DEPTH = 4
DFF = 2816
ALPHA = (2 * DEPTH) ** 0.25
LN_EPS = 1e-5
RMS_EPS = 1e-6
ROPE_THETA = 500000.0
HC = 64
NCH = S // HC

ENGS = ("pe", "act", "dve", "pool", "sp")
NDMA = 40


class Tok:
    __slots__ = ("sem", "val", "eng", "idx")

    def __init__(self, sem, val, eng, idx):
        self.sem, self.val, self.eng, self.idx = sem, val, eng, idx


class _Rec:
    def __init__(self):
        self.call = None

    def __getattr__(self, name):
        def f(*args, **kwargs):
            self.call = (name, args, kwargs)
            return None
        return f


class KB:
    def __init__(self, nc, es):
        self.nc = nc
        self.q = {e: [] for e in ENGS}
        self.cnt = {e: 0 for e in ENGS}
        self.waited = {e: {} for e in ENGS}
        self.sems = {}
        for e in ("pe", "act", "dve", "pool"):
            self.sems[e] = es.enter_context(nc.semaphore("s_" + e))
        for i in range(NDMA):
            self.sems[("dma", i)] = es.enter_context(nc.semaphore(f"s_dma{i}"))
        self.ndma = 0
        self.dma_toks = [None] * NDMA
        self.out_toks = []
        self.last = {e: None for e in ENGS}

    def _flat(self, deps, out):
        for t in deps:
            if t is None:
                continue
            if isinstance(t, (list, tuple)):
                self._flat(t, out)
            else:
                out.append(t)
        return out

    def _waits(self, eng, deps):
        ws = []
        w = self.waited[eng]
        myidx = len(self.q[eng])
        for t in self._flat(deps, []):
            if t.eng == eng and t.sem == eng and myidx - t.idx > 3:
                continue
            if w.get(t.sem, 0) >= t.val:
                continue
            w[t.sem] = t.val
            ws.append((t.sem, t.val))
        return ws

    def op(self, eng, fn, deps=(), inc=True):
        ws = self._waits(eng, deps)
        idx = len(self.q[eng])
        tok = None
        if inc:
            self.cnt[eng] += 1
            tok = Tok(eng, self.cnt[eng], eng, idx)
            self.last[eng] = tok
        sems = self.sems
        rec = _Rec()
        fn(rec)
        name, args, kwargs = rec.call

        def run(e, ws=ws, name=name, args=args, kwargs=kwargs, inc=inc, eng=eng):
            for (s, v) in ws:
                e.wait_ge(sems[s], v)
            ins = getattr(e, name)(*args, **kwargs)
            if inc:
                ins.then_inc(sems[eng], 1)
        self.q[eng].append(run)
        return tok

    def dma(self, out, in_, deps=(), is_output=False, eng="sp", slow=False):
        i = self.ndma
        self.ndma += 1
        slot = i % NDMA
        key = ("dma", slot)
        val = 16 * (i // NDMA + 1)
        ws = self._waits(eng, list(deps) + [self.dma_toks[slot]])
        tok = Tok(key, val, eng, len(self.q[eng]))
        self.dma_toks[slot] = tok
        sems = self.sems

        def run(e, ws=ws, out=out, in_=in_, key=key, slow=slow):
            for (s, v) in ws:
                e.wait_ge(sems[s], v)
            if slow:
                e.dma_start(out=out, in_=in_, allow_slow_non_contiguous=True).then_inc(sems[key], 16)
            else:
                e.dma_start(out=out, in_=in_).then_inc(sems[key], 16)
        self.q[eng].append(run)
        if is_output:
            self.out_toks.append(tok)
        return tok

    def barrier(self):
        toks = [self.last[e] for e in ("pe", "act", "dve", "pool")] + [t for t in self.dma_toks]
        for eng in ENGS:
            ws = self._waits(eng, toks)
            sems = self.sems

            def run(e, ws=ws):
                for (s, v) in ws:
                    e.wait_ge(sems[s], v)
            self.q[eng].append(run)

    def finish(self, block):
        ws = self._waits("sp", self.out_toks)
        sems = self.sems

        def fin(e, ws=ws):
            for (s, v) in ws:
                e.wait_ge(sems[s], v)
        self.q["sp"].append(fin)
        q = self.q

        @block.sync
        def _(e):
            for f in q["sp"]:
                f(e)

        @block.tensor
        def _(e):
            for f in q["pe"]:
                f(e)

        @block.scalar
        def _(e):
            for f in q["act"]:
                f(e)

        @block.vector
        def _(e):
            for f in q["dve"]:
                f(e)

        @block.gpsimd
        def _(e):
            for f in q["pool"]:
                f(e)


class Buf:
    def __init__(self, t):
        self.t = t
        self.w = []
        self.r = {}

    def wdeps(self):
        return [self.w, list(self.r.values())]

    def wrote(self, tok, fresh=True):
        if fresh:
            self.w = [tok]
            self.r = {}
        else:
            self.w.append(tok)

    def rdeps(self):
        return self.w

    def read(self, tok):
        if tok is not None:
            self.r[tok.eng if not isinstance(tok.sem, tuple) else ("d", len(self.r))] = tok


class Ring:
    def __init__(self, bufs):
        self.bufs = bufs
        self.i = 0

    def next(self):
        b = self.bufs[self.i % len(self.bufs)]
        self.i += 1
        return b


def _col_layout(L):
    off = {}
    c = 0
    off["lbf"] = c; c += 4 * DEPTH
    off["lbb"] = c; c += 4 * DEPTH
    for l in range(L):
        off[("dag", l)] = c; c += 1
        off[("hgg", l)] = c; c += 1
        off[("cw", l)] = c; c += 3 * 44
        off[("cb", l)] = c; c += 44
    off["n"] = c
    return off


def _row_layout(L):
    off = {}
    c = 0
    off["ln_in_g"] = c; c += D
    off["ln_in_b"] = c; c += D
    for l in range(L):
        for nm in ("ln1_g", "ln1_b", "ln2_g", "ln2_b"):
            off[(nm, l)] = c; c += D
        off[("lam", l)] = c; c += 256
    off["n"] = c
    return off


def build(L=DEPTH, dbg=False):
    nc = bass.Bass("TRN2", target_bir_lowering=False)
    CO = _col_layout(L)
    RO = _row_layout(L)
    dk = "ExternalOutput" if dbg else None

    def dram(name, shape, dtype, kind=None):
        if kind is None:
            return nc.dram_tensor(name, shape, dtype).ap()
        return nc.dram_tensor(name, shape, dtype, kind=kind).ap()

    x_d = dram("x", [S, D], F32, "ExternalInput")
    mem_d = dram("mem", [NM, D], F32, "ExternalInput")
    pos_d = dram("pos", [1, S], I32, "ExternalInput")
    colp_d = dram("colp", [128, CO["n"]], F32, "ExternalInput")
    rowp_d = dram("rowp", [1, RO["n"]], F32, "ExternalInput")
    cst_d = dram("cst", [128, 1024], F32, "ExternalInput")
    w_in_d = dram("w_in", [L, D, 4608], F32, "ExternalInput")
    w_rot_d = dram("w_rot", [L, D, 1024], F32, "ExternalInput")
    w_kv_d = dram("w_kv", [L, D, 1024], F32, "ExternalInput")
    w_out_d = dram("w_out", [L, 1536, D], F32, "ExternalInput")
    w_up_d = dram("w_up", [L, D, 2 * DFF], F32, "ExternalInput")
    w_dn_d = dram("w_dn", [L, DFF, D], F32, "ExternalInput")
    out_d = dram("out", [S, D], F32, "ExternalOutput")

    xres_d = dram("xres", [S, D], F32, dk)
    hT_d = dram("hT", [8, 128, S], BF16, dk)
    rope_d = dram("rope", [4, 128, S], F32, dk)
    qT_d = dram("qT", [4, 128, S], BF16, dk)
    kT_d = dram("kT", [4, 128, S], BF16, dk)
    v_d = dram("v", [S, 512], BF16, dk)
    hq_d = dram("hq", [4, 128, S], F32, dk)
    zf_d = dram("zf", [4, 128, S], F32, dk)
    zb_d = dram("zb", [4, 128, S], F32, dk)
    hv_d = dram("hv", [S, 512], BF16, dk)
    hg_d = dram("hg", [4, 128, S], F32, dk)
    mq_d = dram("mq", [4, 128, S], BF16, dk)
    of_d = dram("of", [4, 128, S], F32, dk)
    ob_d = dram("ob", [4, 128, S], F32, dk)
    cat_d = dram("cat", [12, 128, S], BF16, dk)
    if dbg:
        dbg_mk = dram("dbg_mk", [128, 4 * NM], BF16, dk)
        dbg_mv = dram("dbg_mv", [128, 1024], BF16, dk)
        dbg_sm = dram("dbg_sm", [128, 4], F32, dk)
        dbg_memT = dram("dbg_memT", [128, 8 * NM], BF16, dk)
        dbg_p = dram("dbg_p", [128, 1024], BF16, dk)
        dbg_oz = dram("dbg_oz", [128, 3, 512], F32, dk)

    with ExitStack() as es:
        kb = KB(nc, es)
        block = es.enter_context(nc.Block())

        uniq = [0]

        def sb(name, shape, dtype, ctx=es):
            uniq[0] += 1
            return ctx.enter_context(nc.sbuf_tensor(f"{name}_{uniq[0]}", shape, dtype))

        cst = sb("cst_s", [128, 1024], F32)
        colp = sb("colp_s", [128, CO["n"]], F32)
        identb = sb("identb", [128, 128], BF16)
        onesb = sb("onesb", [128, 128], BF16)
        trib = sb("trib", [64, 2, 8, 64], BF16)
        lbt = sb("lbt", [128, 2, L, 4], F32)
        oml = sb("oml", [128, 2, L, 4], F32)
        epsc = sb("epsc", [128, 2], F32)
        memT = sb("memT", [128, 8, NM], BF16)
        PSB = [Buf(es.enter_context(nc.psum_tensor(f"psb{i}", [128, 512], F32))) for i in range(0)]
        PP = [Buf(es.enter_context(nc.psum_tensor(f"pp{i}", [128, 2, 512], F32))) for i in range(4)]

        PT3 = [Buf(PP[3].t), Buf(PP[3].t)]
        PT3[0].bank = 0
        PT3[1].bank = 1
        ident = cst[:, 0:128]
        onesf = cst[:, 128:256]

        t_cst = kb.dma(cst[:], cst_d[:, :])
        t_colp = kb.dma(colp[:], colp_d[:, :])
        t_ib = kb.op("dve", lambda e: e.tensor_copy(out=identb[:], in_=cst[:, 0:128]), deps=[t_cst])
        t_ob = kb.op("dve", lambda e: e.tensor_copy(out=onesb[:], in_=cst[:, 128:256]), deps=[t_cst])
        for dr in range(2):
            for rep in range(8):
                t_tb = kb.op("dve", lambda e, dr=dr, rep=rep: e.tensor_copy(
                    out=trib[:, dr, rep, :], in_=cst[0:64, 256 + 64 * dr:320 + 64 * dr]), deps=[t_cst])
        kb.op("dve", lambda e: e.memset(epsc[:, 0:1], LN_EPS))
        kb.op("dve", lambda e: e.memset(epsc[:, 1:2], RMS_EPS))
        with ExitStack() as ps:
            ex = sb("lb_ex", [128, 2, DEPTH, 4], F32, ps)
            ssum = sb("lb_s", [128, 2, 4], F32, ps)
            t1 = kb.op("act", lambda e: e.activation(
                out=ex[:].rearrange("p a l h -> p (a l h)"), in_=colp[:, CO["lbf"]:CO["lbf"] + 8 * DEPTH], func=AF.Exp),
                deps=[t_colp])
            t2 = kb.op("dve", lambda e: e.tensor_tensor(out=ssum[:], in0=ex[:, :, 0, :], in1=ex[:, :, 1, :], op=ALU.add), deps=[t1])
            t2 = kb.op("dve", lambda e: e.tensor_tensor(out=ssum[:], in0=ssum[:], in1=ex[:, :, 2, :], op=ALU.add), deps=[t2])
            t2 = kb.op("dve", lambda e: e.tensor_tensor(out=ssum[:], in0=ssum[:], in1=ex[:, :, 3, :], op=ALU.add), deps=[t2])
            t2 = kb.op("dve", lambda e: e.reciprocal(out=ssum[:], in_=ssum[:]), deps=[t2])
            t3 = kb.op("dve", lambda e: e.memset(lbt[:, :, 0, :], 0.0))
            for l in range(1, L):
                if l == 1:
                    t3 = kb.op("dve", lambda e: e.tensor_copy(out=lbt[:, :, 1, :], in_=ex[:, :, 1, :]), deps=[t1, t3])
                else:
                    t3 = kb.op("dve", lambda e, l=l: e.tensor_tensor(out=lbt[:, :, l, :], in0=lbt[:, :, l - 1, :],
                                                                   in1=ex[:, :, l, :], op=ALU.add), deps=[t3])
            for l in range(1, L):
                t3 = kb.op("dve", lambda e, l=l: e.tensor_tensor(out=lbt[:, :, l, :], in0=lbt[:, :, l, :], in1=ssum[:],
                                                               op=ALU.mult), deps=[t3, t2])
            t3 = kb.op("dve", lambda e: e.tensor_scalar(out=oml[:].rearrange("p a l h -> p (a l h)"),
                                                       in0=lbt[:].rearrange("p a l h -> p (a l h)"),
                                                       scalar1=-1.0, scalar2=1.0, op0=ALU.mult, op1=ALU.add), deps=[t3])
            kb.barrier()

        cast_rr = [0]

        def cast(out, in_, deps):
            engs = ("dve", "pool", "act")
            eg = engs[cast_rr[0] % 3]
            cast_rr[0] += 1
            if eg == "act":
                return kb.op("act", lambda e: e.activation(out=out, in_=in_, func=AF.Copy), deps=deps)
            return kb.op(eg, lambda e: e.tensor_copy(out=out, in_=in_), deps=deps)

        def load_w(dst, src, nk, ncol, stg):
            toks = []
            step = 2048
            for k in range(nk):
                for c0 in range(0, ncol, step):
                    n = min(step, ncol - c0)
                    b = stg.next()
                    td = kb.dma(b.t[:, 0:n], src[k * 128:(k + 1) * 128, c0:c0 + n], deps=b.wdeps())
                    b.wrote(td)
                    tc = cast(dst[:, k, c0:c0 + n], b.t[:, 0:n], deps=[td])
                    b.read(tc)
                    toks.append(tc)
            return toks

        def ln_tile(r, rdeps, n, t0, G, Bt, gdeps, lnb, last, hTt, hcol):
            st, mv, sd, nmr = lnb["st"], lnb["mv"], lnb["sd"], lnb["nmr"]
            ta = kb.op("dve", lambda e: e.bn_stats(out=st.t[0:n, 0, :], in_=r.t[0:n, 0:512]), deps=[rdeps, st.wdeps()])
            tb = kb.op("dve", lambda e: e.bn_stats(out=st.t[0:n, 1, :], in_=r.t[0:n, 512:1024]), deps=[rdeps])
            st.wrote(tb)
            tc = kb.op("dve", lambda e: e.bn_aggr(out=mv.t[0:n, :], in_=st.t[0:n].rearrange("p a b -> p (a b)")),
                       deps=[ta, tb, mv.wdeps()])
            st.read(tc)
            mv.wrote(tc)
            td = kb.op("act", lambda e: e.activation(out=sd.t[0:n, :], in_=mv.t[0:n, 1:2], func=AF.Sqrt,
                                                    bias=epsc[0:n, 0:1], scale=1.0), deps=[tc, sd.wdeps()])
            sd.wrote(td)
            te = kb.op("dve", lambda e: e.reciprocal(out=sd.t[0:n, :], in_=sd.t[0:n, :]), deps=[td])
            sd.wrote(te)
            tf = kb.op("dve", lambda e: e.scalar_tensor_tensor(out=nmr.t[0:n, :], in0=mv.t[0:n, 0:1], scalar=-1.0,
                                                              in1=sd.t[0:n, :], op0=ALU.mult, op1=ALU.mult),
                       deps=[te, nmr.wdeps()])
            nmr.wrote(tf)
            mv.read(tf)
            tg = kb.op("act", lambda e: e.activation(out=r.t[0:n, :], in_=r.t[0:n, :], func=AF.Identity,
                                                    scale=sd.t[0:n, 0:1], bias=nmr.t[0:n, 0:1]), deps=[tf, te, tb])
            sd.read(tg)
            nmr.read(tg)
            th = kb.op("pool", lambda e: e.tensor_tensor(out=r.t[0:n, :], in0=r.t[0:n, :], in1=G[0:n, :], op=ALU.mult),
                       deps=[tg, gdeps])
            ti = kb.op("pool", lambda e: e.tensor_tensor(out=r.t[0:n, :], in0=r.t[0:n, :], in1=Bt[0:n, :], op=ALU.add),
                       deps=[th])
            r.wrote(ti)
            if last:
                tdm = kb.dma(out_d[t0:t0 + n, :], r.t[0:n, :], deps=[ti], is_output=True)
                r.read(tdm)
                return lambda: []
            tdm = kb.dma(xres_d[t0:t0 + n, :], r.t[0:n, :], deps=[ti])
            r.read(tdm)

            def emit_tr():
                toks = []
                for hf in range(2):
                    pb = lnb["pt"].next()
                    bk = pb.bank
                    for j in range(4):
                        kc = hf * 4 + j
                        tk = kb.op("pe", lambda e: e.transpose(
                            out=pb.t[:, bk, j * 128:j * 128 + n], in_=r.t[0:n, kc * 128:(kc + 1) * 128],
                            identity=ident[0:n, 0:n]), deps=[ti, pb.wdeps() if j == 0 else None, t_cst], inc=(j == 3))
                    pb.wrote(tk)
                    r.read(tk)
                    src = pb.t[:, bk, :].rearrange("p (j t) -> p j t", j=4)[:, :, 0:n]
                    dst = hTt.t[:, hf * 4:(hf + 1) * 4, hcol:hcol + n]
                    if hf == 0:
                        te2 = kb.op("act", lambda e: e.activation(out=dst, in_=src, func=AF.Copy),
                                    deps=[tk, hTt.wdeps()])
                    else:
                        te2 = kb.op("dve", lambda e: e.tensor_copy(out=dst, in_=src),
                                    deps=[tk, hTt.wdeps()])
                    pb.read(te2)
                    toks.append(te2)
                return toks
            return emit_tr

        def ln_bufs(ctx):
            return {
                "st": Buf(sb("ln_st", [128, 2, 6], F32, ctx)),
                "mv": Buf(sb("ln_mv", [128, 2], F32, ctx)),
                "sd": Buf(sb("ln_sd", [128, 1], F32, ctx)),
                "nmr": Buf(sb("ln_nmr", [128, 1], F32, ctx)),
                "pt": Ring(PT3),
            }

        def load_gb(G, Bt, og, ob_):
            ta = kb.dma(G[:], rowp_d[0:1, og:og + D].to_broadcast([128, D]))
            tb = kb.dma(Bt[:], rowp_d[0:1, ob_:ob_ + D].to_broadcast([128, D]))
            return [ta, tb]

        hT_v = hT_d.rearrange("k p t -> p k t")

        with ExitStack() as ps:
            mt_ = sb("memf", [128, 2, D], F32, ps)
            td = kb.dma(mt_[:], mem_d.rearrange("(a p) d -> p a d", p=128))
            for a in range(2):
                for hf in range(2):
                    pb = PP[hf]
                    for j in range(4):
                        kc = hf * 4 + j
                        tk = kb.op("pe", lambda e, pb=pb, j=j, kc=kc, a=a: e.transpose(
                            out=pb.t[:, 0, j * 128:(j + 1) * 128], in_=mt_[:, a, kc * 128:(kc + 1) * 128],
                            identity=ident), deps=[td, t_cst, pb.wdeps() if j == 0 else None], inc=(j == 3))
                    pb.wrote(tk)
                    te = kb.op("dve", lambda e, pb=pb, hf=hf, a=a: e.tensor_copy(
                        out=memT[:, hf * 4:(hf + 1) * 4, a * 128:(a + 1) * 128],
                        in_=pb.t[:, 0, :].rearrange("p (j t) -> p j t", j=4)), deps=[tk])
                    pb.read(te)
            posi = sb("posi", [128, S], I32, ps)
            ang = sb("ang", [128, S], F32, ps)
            a2 = sb("ang2", [128, S], F32, ps)
            ki = sb("ki", [128, S], I32, ps)
            kf = sb("kf", [128, S], F32, ps)
            tp = kb.dma(posi[:], pos_d[0:1, :].to_broadcast([128, S]))
            t0_ = kb.op("dve", lambda e: e.tensor_copy(out=ang[:], in_=posi[:]), deps=[tp])
            t0_ = kb.op("dve", lambda e: e.tensor_scalar(out=ang[:], in0=ang[:], scalar1=cst[:, 384:385], scalar2=None,
                                                        op0=ALU.mult), deps=[t0_, t_cst])
            TWO_PI = 2.0 * math.pi

            def reduce_sin(shift, out_scale_col, dst_idx_list):
                t = kb.op("dve", lambda e: e.tensor_scalar(out=kf[:], in0=ang[:], scalar1=shift, scalar2=1.0 / TWO_PI,
                                                          op0=ALU.add, op1=ALU.mult), deps=[t0_])
                t = kb.op("dve", lambda e: e.tensor_copy(out=ki[:], in_=kf[:]), deps=[t])
                t = kb.op("dve", lambda e: e.tensor_copy(out=kf[:], in_=ki[:]), deps=[t])
                t = kb.op("dve", lambda e: e.scalar_tensor_tensor(out=a2[:], in0=kf[:], scalar=-TWO_PI, in1=ang[:],
                                                                 op0=ALU.mult, op1=ALU.add), deps=[t])
                if shift != 0.0:
                    t = kb.op("dve", lambda e: e.tensor_scalar(out=a2[:], in0=a2[:], scalar1=shift, scalar2=None,
                                                              op0=ALU.add), deps=[t])
                t = kb.op("dve", lambda e: e.tensor_scalar(out=kf[:], in0=a2[:], scalar1=math.pi, scalar2=-TWO_PI,
                                                          op0=ALU.is_gt, op1=ALU.mult), deps=[t])
                t = kb.op("dve", lambda e: e.tensor_tensor(out=a2[:], in0=a2[:], in1=kf[:], op=ALU.add), deps=[t])
                t = kb.op("dve", lambda e: e.tensor_scalar(out=kf[:], in0=a2[:], scalar1=-math.pi, scalar2=TWO_PI,
                                                          op0=ALU.is_lt, op1=ALU.mult), deps=[t])
                t = kb.op("dve", lambda e: e.tensor_tensor(out=a2[:], in0=a2[:], in1=kf[:], op=ALU.add), deps=[t])
                t = kb.op("dve", lambda e: e.tensor_scalar(out=a2[:], in0=a2[:], scalar1=-3.14159, scalar2=3.14159,
                                                          op0=ALU.max, op1=ALU.min), deps=[t])
                t = kb.op("act", lambda e: e.activation(out=a2[:], in_=a2[:], func=AF.Sin), deps=[t])
                return t

            t = reduce_sin(math.pi / 2.0, None, None)
            t = kb.op("dve", lambda e: e.tensor_scalar(out=kf[:], in0=a2[:], scalar1=cst[:, 385:386],
                                                      scalar2=cst[:, 386:387], op0=ALU.mult, op1=ALU.add), deps=[t])
            tck = kb.dma(rope_d[2], kf[:], deps=[t])
            t = kb.op("pool", lambda e: e.tensor_scalar(out=a2[:], in0=kf[:], scalar1=0.125, scalar2=None, op0=ALU.mult),
                      deps=[t])
            tcq = kb.dma(rope_d[0], a2[:], deps=[t])
            t = reduce_sin(0.0, None, None) if False else None
            kb.barrier()
            t = reduce_sin(0.0, None, None)
            t = kb.op("dve", lambda e: e.tensor_scalar(out=kf[:], in0=a2[:], scalar1=cst[:, 387:388], scalar2=None,
                                                      op0=ALU.mult), deps=[t])
            kb.dma(rope_d[3], kf[:], deps=[t])
            t = kb.op("pool", lambda e: e.tensor_scalar(out=a2[:], in0=kf[:], scalar1=0.125, scalar2=None, op0=ALU.mult),
                      deps=[t])
            kb.dma(rope_d[1], a2[:], deps=[t])
            kb.barrier()

        with ExitStack() as ps:
            G = sb("G0", [128, D], F32, ps)
            Bt = sb("B0", [128, D], F32, ps)
            gd = load_gb(G, Bt, RO["ln_in_g"], RO["ln_in_b"])
            lnb = ln_bufs(ps)
            rr = Ring([Buf(sb(f"r0_{i}", [128, D], F32, ps)) for i in range(3)])
            hr = Ring([Buf(sb(f"hTt0_{i}", [128, 8, 512], BF16, ps)) for i in range(2)])
            for tt in range(S // 512):
                hTt = hr.next()
                toks = []
                for s4 in range(4):
                    t0 = tt * 512 + s4 * 128
                    r = rr.next()
                    td = kb.dma(r.t[:], x_d[t0:t0 + 128, :], deps=r.wdeps())
                    r.wrote(td)
                    toks += ln_tile(r, [td], 128, t0, G, Bt, gd, lnb, False, hTt, s4 * 128)()
                hTt.wrote(toks[0])
                for t in toks[1:]:
                    hTt.wrote(t, fresh=False)
                tdm = kb.dma(hT_v[:, :, tt * 512:(tt + 1) * 512], hTt.t[:], deps=toks)
                hTt.read(tdm)
            kb.barrier()

        for l in range(L):
            lam_init = 0.8 - 0.6 * math.exp(-0.3 * l)
            with ExitStack() as ps:
                wi = sb("wi", [128, 8, 4608], BF16, ps)
                wr = sb("wr", [128, 8, 1024], BF16, ps)
                with ExitStack() as ps2:
                    stg = Ring([Buf(sb(f"stg{i}", [128, 2048], F32, ps2)) for i in range(3)])
                    wtoks = load_w(wi, w_in_d[l], 8, 4608, stg)
                    wtoks += load_w(wr, w_rot_d[l], 8, 1024, stg)
                    kb.barrier()
                hin = Ring([Buf(sb(f"hin{i}", [128, 8, 512], BF16, ps)) for i in range(2)])
                rtab = Ring([Buf(sb(f"rtab{i}", [128, 4, 512], F32, ps)) for i in range(2)])
                tmpA = Ring([Buf(sb(f"tmpA{i}", [128, 512], F32, ps)) for i in range(2)])
                tmpB = Ring([Buf(sb(f"tmpB{i}", [128, 512], F32, ps)) for i in range(2)])
                stF = Ring([Buf(sb(f"stF{i}", [128, 512], F32, ps)) for i in range(4)])
                stH = Ring([Buf(sb(f"stH{i}", [128, 512], BF16, ps)) for i in range(4)])
                pr = Ring(PP)

                def mm_group(dst, kc_list, lhs_fn, rhs_fn, deps):
                    tk = None
                    nk = len(kc_list)
                    for i, kc in enumerate(kc_list):
                        tk = kb.op("pe", lambda e, kc=kc, i=i: e.matmul(dst, lhsT=lhs_fn(kc), rhs=rhs_fn(kc),
                                                                      start=(i == 0), stop=(i == nk - 1)),
                                   deps=deps if i == 0 else (), inc=(i == nk - 1))
                    return tk

                for tt in range(8):
                    c0 = tt * 512
                    h = hin.next()
                    td = kb.dma(h.t[:], hT_v[:, :, c0:c0 + 512], deps=h.wdeps())
                    h.wrote(td)
                    rt = rtab.next()
                    td2 = kb.dma(rt.t[:], rope_d[:, :, c0:c0 + 512].rearrange("a p t -> p a t"), deps=rt.wdeps())
                    rt.wrote(td2)
                    for which in range(2):
                        for hh in range(4):
                            col = which * 512 + hh * 128
                            pb = pr.next()
                            tA = mm_group(pb.t[:, 0, :], range(8), lambda kc, col=col: wi[:, kc, col:col + 128],
                                          lambda kc, h=h: h.t[:, kc, :], [td, pb.wdeps()])
                            tB = mm_group(pb.t[:, 1, :], range(8), lambda kc, col=col: wr[:, kc, col:col + 128],
                                          lambda kc, h=h: h.t[:, kc, :], [])
                            pb.wrote(tB)
                            h.read(tB)
                            a = tmpA.next()
                            b = tmpB.next()
                            t1 = kb.op("dve", lambda e, a=a, pb=pb, rt=rt, which=which: e.tensor_tensor(
                                out=a.t[:], in0=pb.t[:, 0, :], in1=rt.t[:, 2 * which, :], op=ALU.mult),
                                deps=[tA, td2, a.wdeps()])
                            t2 = kb.op("dve", lambda e, b=b, pb=pb, rt=rt, which=which: e.tensor_tensor(
                                out=b.t[:], in0=pb.t[:, 1, :], in1=rt.t[:, 2 * which + 1, :], op=ALU.mult),
                                deps=[tB, td2, b.wdeps()])
                            pb.read(t2)
                            rt.read(t2)
                            a.wrote(t1)
                            b.wrote(t2)
                            so = stH.next()
                            t3 = kb.op("pool", lambda e, a=a, b=b, so=so: e.tensor_tensor(
                                out=so.t[:], in0=a.t[:], in1=b.t[:], op=ALU.add), deps=[t1, t2, so.wdeps()])
                            a.read(t3)
                            b.read(t3)
                            so.wrote(t3)
                            dst = (qT_d if which == 0 else kT_d)[hh][:, c0:c0 + 512]
                            so.read(kb.dma(dst, so.t[:], deps=[t3]))
                    specs = []
                    for hh in range(4):
                        specs.append((1536 + hh * 128, hq_d[hh], AF.Silu, 1.0, False))
                        specs.append((2048 + hh * 128, zf_d[hh], AF.Copy, 1.0, False))
                        specs.append((2560 + hh * 128, zb_d[hh], AF.Copy, 1.0, False))
                        specs.append((3584 + hh * 128, hg_d[hh], AF.Silu, 1.0, False))
                        specs.append((4096 + hh * 128, mq_d[hh], AF.Copy, 128.0 ** -0.5, True))
                    for i in range(0, len(specs), 2):
                        pb = pr.next()
                        tks = []
                        for j in range(2):
                            col = specs[i + j][0]
                            tks.append(mm_group(pb.t[:, j, :], range(8), lambda kc, col=col: wi[:, kc, col:col + 128],
                                                lambda kc, h=h: h.t[:, kc, :], [td, pb.wdeps()] if j == 0 else []))
                        pb.wrote(tks[1])
                        h.read(tks[1])
                        for j in range(2):
                            col, dst, fn, sc, isb = specs[i + j]
                            so = (stH if isb else stF).next()
                            if fn == AF.Copy and not isb and (i + j) % 2 == 0:
                                te = kb.op("dve", lambda e, so=so, pb=pb, j=j: e.tensor_copy(out=so.t[:], in_=pb.t[:, j, :]),
                                           deps=[tks[j], so.wdeps()])
                            else:
                                te = kb.op("act", lambda e, so=so, pb=pb, j=j, fn=fn, sc=sc: e.activation(
                                    out=so.t[:], in_=pb.t[:, j, :], func=fn, scale=sc), deps=[tks[j], so.wdeps()])
                            pb.read(te)
                            so.wrote(te)
                            so.read(kb.dma(dst[:, c0:c0 + 512], so.t[:], deps=[te]))
                    for (col, dst) in ((1024, v_d), (3072, hv_d)):
                        for s2 in range(2):
                            pb = pr.next()
                            tks = []
                            for j in range(2):
                                sub = s2 * 2 + j
                                tks.append(mm_group(pb.t[:, j, :], range(8),
                                                    lambda kc, h=h, sub=sub: h.t[:, kc, sub * 128:(sub + 1) * 128],
                                                    lambda kc, col=col: wi[:, kc, col:col + 512],
                                                    [td, pb.wdeps()] if j == 0 else []))
                            pb.wrote(tks[1])
                            h.read(tks[1])
                            for j in range(2):
                                sub = s2 * 2 + j
                                so = stH.next()
                                if j == 0:
                                    te = kb.op("dve", lambda e, so=so, pb=pb, j=j: e.tensor_copy(out=so.t[:], in_=pb.t[:, j, :]),
                                               deps=[tks[j], so.wdeps()])
                                else:
                                    te = kb.op("act", lambda e, so=so, pb=pb, j=j: e.activation(
                                        out=so.t[:], in_=pb.t[:, j, :], func=AF.Copy), deps=[tks[j], so.wdeps()])
                                pb.read(te)
                                so.wrote(te)
                                so.read(kb.dma(dst[c0 + sub * 128:c0 + (sub + 1) * 128, :], so.t[:], deps=[te]))
                kb.barrier()
            if dbg and dbg == "inproj":
                break

            with ExitStack() as ps:
                lamt = sb("lamt", [128, 256], F32, ps)
                lamp = sb("lamp", [128, 2, 64], F32, ps)
                lams = sb("lams", [128, 2], F32, ps)
                nlam = sb("nlam", [128, 1], F32, ps)
                gcol = sb("gcol", [128, 1], F32, ps)
                ro = RO[("lam", l)]
                td = kb.dma(lamt[:], rowp_d[0:1, ro:ro + 256].to_broadcast([128, 256]))
                lv = lamt[:].rearrange("p (a d) -> p a d", a=4)
                t = kb.op("dve", lambda e: e.tensor_tensor(out=lamp[:, 0, :], in0=lv[:, 0, :], in1=lv[:, 1, :], op=ALU.mult), deps=[td])
                t = kb.op("dve", lambda e: e.tensor_tensor(out=lamp[:, 1, :], in0=lv[:, 2, :], in1=lv[:, 3, :], op=ALU.mult), deps=[td, t])
                t = kb.op("dve", lambda e: e.tensor_reduce(out=lams[:], in_=lamp[:], axis=mybir.AxisListType.X, op=ALU.add), deps=[t])
                t = kb.op("act", lambda e: e.activation(out=lams[:], in_=lams[:], func=AF.Exp), deps=[t])
                t = kb.op("dve", lambda e: e.tensor_tensor(out=nlam[:], in0=lams[:, 1:2], in1=lams[:, 0:1], op=ALU.subtract), deps=[t])
                t = kb.op("dve", lambda e: e.tensor_scalar(out=nlam[:], in0=nlam[:], scalar1=-lam_init, scalar2=None, op0=ALU.add), deps=[t])
                t_lam = t
                cdg = CO[("dag", l)]
                t_g = kb.op("dve", lambda e: e.tensor_scalar(out=gcol[:], in0=colp[:, cdg:cdg + 1], scalar1=1.0 - lam_init,
                                                            scalar2=None, op0=ALU.mult), deps=[t_colp])
                mkT = sb("mkT", [128, 4, NM], BF16, ps)
                mv_ = sb("mv", [128, 2, 512], BF16, ps)
                with ExitStack() as ps2:
                    wkv = sb("wkv", [128, 8, 1024], BF16, ps2)
                    stg = Ring([Buf(sb(f"stgm{i}", [128, 2048], F32, ps2)) for i in range(3)])
                    wt = load_w(wkv, w_kv_d[l], 8, 1024, stg)
                    for hp in range(2):
                        pb = PP[hp]
                        for j in range(2):
                            hh = hp * 2 + j
                            for kc in range(8):
                                tk = kb.op("pe", lambda e, pb=pb, j=j, kc=kc, hh=hh: e.matmul(
                                    pb.t[:, j, 0:NM], lhsT=wkv[:, kc, hh * 128:(hh + 1) * 128], rhs=memT[:, kc, :],
                                    start=(kc == 0), stop=(kc == 7)), deps=[wt] if kc == 0 else (), inc=(kc == 7))
                            te = kb.op("act", lambda e, pb=pb, j=j, hh=hh: e.activation(
                                out=mkT[:, hh, :], in_=pb.t[:, j, 0:NM], func=AF.Copy), deps=[tk])
                    pb = PP[2]
                    for mt in range(2):
                        for kc in range(8):
                            tk = kb.op("pe", lambda e, mt=mt, kc=kc: e.matmul(
                                PP[2].t[:, mt, :], lhsT=memT[:, kc, mt * 128:(mt + 1) * 128], rhs=wkv[:, kc, 512:1024],
                                start=(kc == 0), stop=(kc == 7)), deps=[wt] if kc == 0 else (), inc=(kc == 7))
                        te = kb.op("dve", lambda e, mt=mt: e.tensor_copy(out=mv_[:, mt, :], in_=PP[2].t[:, mt, :]), deps=[tk])
                    kb.barrier()
                    if dbg:
                        kb.dma(dbg_mk[:, :], mkT[:].rearrange("p a m -> p (a m)"))
                        kb.dma(dbg_mv[:, :], mv_[:].rearrange("p a m -> p (a m)"))
                        kb.dma(dbg_memT[:, :], memT[:].rearrange("p a m -> p (a m)"))
                        kb.dma(dbg_sm[:, 0:1], nlam[:], slow=True)
                        kb.dma(dbg_sm[:, 1:2], gcol[:], slow=True)
                        kb.dma(dbg_sm[:, 2:4], lams[:], slow=True)
                        kb.barrier()

                kts = Ring([Buf(sb(f"kts{i}", [128, S], BF16, ps)) for i in range(2)])
                vts = Ring([Buf(sb(f"vts{i}", [128, 32, 128], BF16, ps)) for i in range(2)])
                qts = Ring([Buf(sb(f"qts{i}", [128, 512], BF16, ps)) for i in range(2)])
                pts = Ring([Buf(sb(f"pts{i}", [128, 2, 512], BF16, ps)) for i in range(3)])
                rz = Buf(sb("rz", [128, 2, 512], F32, ps))
                fa = Buf(sb("fa", [128, 512], F32, ps))
                fb = Buf(sb("fb", [128, 512], F32, ps))
                fo = Buf(sb("fo", [128, 512], F32, ps))
                fsq = Buf(sb("fsq", [128, 512], F32, ps))
                fsd = Buf(sb("fsd", [128, 512], F32, ps))
                fob = Ring([Buf(sb(f"fob{i}", [128, 512], BF16, ps)) for i in range(2)])
                scr = Ring([PP[0], PP[1]])
                Ob, Zb = PP[2], PP[3]

                def attn_core(qb, nkt, lhs_s, lhs_v, ncomp):
                    def issue_S(kt):
                        sc = scr.next()
                        tS = None
                        for c in range(2):
                            tS = kb.op("pe", lambda e, sc=sc, c=c, kt=kt: e.matmul(
                                sc.t[:, c, :], lhsT=lhs_s(kt, c), rhs=qb.t[c * 64:(c + 1) * 64, :], start=True, stop=True),
                                deps=[qb.rdeps(), sc.wdeps()] if c == 0 else (), inc=(c == 1))
                        sc.wrote(tS)
                        return sc, tS
                    tZ = None
                    pend = issue_S(0)
                    for kt in range(nkt):
                        sc, tS = pend
                        pend = issue_S(kt + 1) if kt + 1 < nkt else None
                        p = pts.next()
                        te0 = kb.op("act", lambda e, p=p, sc=sc: e.activation(
                            out=p.t[:, 0, :], in_=sc.t[:, 0, :], func=AF.Exp), deps=[tS, p.wdeps()])
                        te = kb.op("act", lambda e, p=p, sc=sc: e.activation(
                            out=p.t[:, 1, :], in_=sc.t[:, 1, :], func=AF.Exp), deps=[tS])
                        sc.read(te)
                        p.wrote(te)
                        for c in range(2):
                            kb.op("pe", lambda e, p=p, c=c, kt=kt: e.matmul(
                                Ob.t[:, c, :], lhsT=lhs_v(kt), rhs=p.t[:, c, :], start=(kt == 0), stop=(kt == nkt - 1)),
                                deps=[te0 if c == 0 else te, Ob.wdeps() if (c == 0 and kt == 0) else None], inc=False)
                        for c in range(2):
                            tZ = kb.op("pe", lambda e, p=p, c=c, kt=kt: e.matmul(
                                Zb.t[:, c, :], lhsT=onesb[:], rhs=p.t[:, c, :], start=(kt == 0), stop=(kt == nkt - 1)),
                                deps=[Zb.wdeps(), t_ob] if (c == 0 and kt == 0) else (), inc=(c == 1))
                        p.read(tZ)
                    Ob.wrote(tZ)
                    Zb.wrote(tZ)
                    return tZ

                def rms_store(o_buf, t_o, scale_col, gate_buf, dst, extra_deps=()):
                    t1 = kb.op("act", lambda e: e.activation(out=fsq.t[:], in_=o_buf.t[:], func=AF.Square),
                               deps=[t_o, fsq.wdeps()])
                    fsq.wrote(t1)
                    o_buf.read(t1)
                    t2 = kb.op("pe", lambda e: e.matmul(Zb.t[:, 0, :], lhsT=onesf, rhs=fsq.t[:], start=True, stop=True),
                               deps=[t1, Zb.wdeps(), t_cst])
                    fsq.read(t2)
                    Zb.wrote(t2)
                    t3 = kb.op("act", lambda e: e.activation(out=fsd.t[:], in_=Zb.t[:, 0, :], func=AF.Sqrt,
                                                            bias=epsc[:, 1:2], scale=1.0 / 128.0), deps=[t2, fsd.wdeps()])
                    Zb.read(t3)
                    t4 = kb.op("dve", lambda e: e.reciprocal(out=fsd.t[:], in_=fsd.t[:]), deps=[t3])
                    fsd.wrote(t4)
                    t5 = kb.op("dve", lambda e: e.tensor_tensor(out=o_buf.t[:], in0=o_buf.t[:], in1=fsd.t[:], op=ALU.mult),
                               deps=[t4, t1])
                    fsd.read(t5)
                    so = fob.next()
                    if gate_buf is None:
                        t6 = kb.op("dve", lambda e, so=so: e.tensor_scalar(out=so.t[:], in0=o_buf.t[:], scalar1=scale_col,
                                                                         scalar2=None, op0=ALU.mult),
                                   deps=[t5, so.wdeps(), extra_deps])
                    else:
                        t6 = kb.op("dve", lambda e, so=so: e.scalar_tensor_tensor(
                            out=so.t[:], in0=o_buf.t[:], scalar=scale_col, in1=gate_buf.t[:], op0=ALU.mult, op1=ALU.mult),
                            deps=[t5, so.wdeps(), gate_buf.rdeps(), extra_deps])
                        gate_buf.read(t6)
                    o_buf.wrote(t6)
                    so.wrote(t6)
                    so.read(kb.dma(dst, so.t[:], deps=[t6]))

                for hh in range(4 if dbg != "mem" else 0):
                    kt_ = kts.next()
                    tdk = kb.dma(kt_.t[:], kT_d[hh], deps=kt_.wdeps())
                    kt_.wrote(tdk)
                    vt_ = vts.next()
                    tdv = kb.dma(vt_.t[:], v_d[:, hh * 128:(hh + 1) * 128].rearrange("(k p) d -> p k d", p=128),
                                 deps=vt_.wdeps())
                    vt_.wrote(tdv)
                    for qt in range(8):
                        qb = qts.next()
                        tdq = kb.dma(qb.t[:], qT_d[hh][:, qt * 512:(qt + 1) * 512], deps=qb.wdeps())
                        qb.wrote(tdq)
                        first = [True]

                        def lhs_s(kt, c, kt_=kt_):
                            return kt_.t[c * 64:(c + 1) * 64, kt * 128:(kt + 1) * 128]

                        def lhs_v(kt, vt_=vt_):
                            return vt_.t[:, kt, :]
                        qb.w.append(tdk)
                        qb.w.append(tdv)
                        tZ = attn_core(qb, 32, lhs_s, lhs_v, 2)
                        qb.read(tZ)
                        kt_.read(tZ)
                        vt_.read(tZ)
                        t1a = kb.op("dve", lambda e: e.reciprocal(out=rz.t[:, 0, :], in_=Zb.t[:, 0, :]), deps=[tZ, rz.wdeps()])
                        t1 = kb.op("dve", lambda e: e.reciprocal(out=rz.t[:, 1, :], in_=Zb.t[:, 1, :]), deps=[tZ])
                        Zb.read(t1)
                        rz.wrote(t1)
                        t2 = kb.op("dve", lambda e: e.tensor_tensor(out=fa.t[:], in0=Ob.t[:, 0, :], in1=rz.t[:, 0, :], op=ALU.mult),
                                   deps=[t1, fa.wdeps()])
                        t3 = kb.op("dve", lambda e: e.tensor_tensor(out=fb.t[:], in0=Ob.t[:, 1, :], in1=rz.t[:, 1, :], op=ALU.mult),
                                   deps=[t1, fb.wdeps()])
                        Ob.read(t3)
                        rz.read(t3)
                        fa.wrote(t2)
                        fb.wrote(t3)
                        t4 = kb.op("dve", lambda e: e.scalar_tensor_tensor(out=fo.t[:], in0=fb.t[:], scalar=nlam[:, 0:1],
                                                                          in1=fa.t[:], op0=ALU.mult, op1=ALU.add),
                                   deps=[t2, t3, t_lam, fo.wdeps()])
                        fa.read(t4)
                        fb.read(t4)
                        fo.wrote(t4)
                        rms_store(fo, t4, gcol[:, 0:1], None, cat_d[hh][:, qt * 512:(qt + 1) * 512], extra_deps=[t_g])
                if dbg == "dattn":
                    kb.barrier()
                    break
                for hh in range(4):
                    for qt in range(8):
                        qb = qts.next()
                        tdq = kb.dma(qb.t[:], mq_d[hh][:, qt * 512:(qt + 1) * 512], deps=qb.wdeps())
                        qb.wrote(tdq)
                        sc = scr.next()
                        for mt in range(2):
                            tS = kb.op("pe", lambda e, sc=sc, mt=mt, hh=hh, qb=qb: e.matmul(
                                sc.t[:, mt, :], lhsT=mkT[:, hh, mt * 128:(mt + 1) * 128], rhs=qb.t[:], start=True, stop=True),
                                deps=[tdq, sc.wdeps()] if mt == 0 else (), inc=(mt == 1))
                        sc.wrote(tS)
                        qb.read(tS)
                        p = pts.next()
                        te0 = kb.op("act", lambda e, p=p, sc=sc: e.activation(
                            out=p.t[:, 0, :], in_=sc.t[:, 0, :], func=AF.Exp), deps=[tS, p.wdeps()])
                        te = kb.op("act", lambda e, p=p, sc=sc: e.activation(
                            out=p.t[:, 1, :], in_=sc.t[:, 1, :], func=AF.Exp), deps=[tS])
                        sc.read(te)
                        p.wrote(te)
                        for mt in range(2):
                            tO = kb.op("pe", lambda e, p=p, mt=mt, hh=hh: e.matmul(
                                Ob.t[:, 0, :], lhsT=mv_[:, mt, hh * 128:(hh + 1) * 128], rhs=p.t[:, mt, :],
                                start=(mt == 0), stop=(mt == 1)), deps=[te, Ob.wdeps()] if mt == 0 else (), inc=False)
                        for mt in range(2):
                            tZ = kb.op("pe", lambda e, p=p, mt=mt: e.matmul(
                                Zb.t[:, 0, :], lhsT=onesb[:], rhs=p.t[:, mt, :], start=(mt == 0), stop=(mt == 1)),
                                deps=[Zb.wdeps()] if mt == 0 else (), inc=(mt == 1))
                        p.read(tZ)
                        Ob.wrote(tZ)
                        Zb.wrote(tZ)
                        if dbg and hh == 0 and qt == 0:
                            kb.dma(dbg_p[:, :], p.t[:].rearrange("p a t -> p (a t)"), deps=[te])
                            tq1 = kb.op("dve", lambda e: e.tensor_copy(out=fa.t[:], in_=Ob.t[:, 0, :]), deps=[tZ])
                            tq2 = kb.op("dve", lambda e: e.tensor_copy(out=fb.t[:], in_=Zb.t[:, 0, :]), deps=[tZ])
                            kb.dma(dbg_oz[:, 0, :], fa.t[:], deps=[tq1])
                            kb.dma(dbg_oz[:, 1, :], fb.t[:], deps=[tq2])
                            kb.barrier()
                        t1 = kb.op("dve", lambda e: e.reciprocal(out=rz.t[:, 0, :], in_=Zb.t[:, 0, :]), deps=[tZ, rz.wdeps()])
                        Zb.read(t1)
                        rz.wrote(t1)
                        if dbg and hh == 0 and qt == 0:
                            kb.dma(dbg_oz[:, 2, :], rz.t[:, 0, :], deps=[t1])
                        so = fob.next()
                        t2 = kb.op("dve", lambda e, so=so: e.tensor_tensor(out=so.t[:], in0=Ob.t[:, 0, :], in1=rz.t[:, 0, :],
                                                                          op=ALU.mult), deps=[t1, so.wdeps()])
                        Ob.read(t2)
                        rz.read(t2)
                        so.wrote(t2)
                        so.read(kb.dma(cat_d[8 + hh][:, qt * 512:(qt + 1) * 512], so.t[:], deps=[t2]))
                kb.barrier()
            if dbg in ("mem", "attn"):
                break

            with ExitStack() as ps:
                hqs = sb("hqs", [128, S], F32, ps)
                hvs = sb("hvs", [64, NCH, 128], BF16, ps)
                Zt = [sb(f"Zt{d}", [128, S], F32, ps) for d in range(2)]
                KK = [sb(f"KK{d}", [128, S], F32, ps) for d in range(2)]
                Et = [sb(f"Et{d}", [128, S + 1], F32, ps) for d in range(2)]
                Qt = [sb(f"Qt{d}", [128, S], BF16, ps) for d in range(2)]
                Kt = [sb(f"Kt{d}", [128, S], BF16, ps) for d in range(2)]
                csc = [sb(f"csc{d}", [128, 4, NCH], F32, ps) for d in range(2)]
                Sst = [sb(f"Sst{d}", [128, 128], F32, ps) for d in range(2)]
                Stm = [sb(f"Stm{d}", [128, 128], F32, ps) for d in range(2)]
                Sbf = [[Buf(sb(f"Sbf{d}_{i}", [128, 128], BF16, ps)) for i in range(2)] for d in range(2)]
                KTs = [Ring([Buf(sb(f"KTs{d}_{i}", [64, 128], BF16, ps)) for i in range(2)]) for d in range(2)]
                ATs = [Ring([Buf(sb(f"ATs{d}_{i}", [64, 64], BF16, ps)) for i in range(2)]) for d in range(2)]
                ost = [Ring([Buf(sb(f"ost{d}_{i}", [128, 512], F32, ps)) for i in range(2)]) for d in range(2)]
                PT = [PP[2 + d].t[:].rearrange("p a t -> p (a t)").bitcast(BF16) for d in range(2)]
                for hh in range(4):
                    kb.barrier()
                    t_hq = kb.dma(hqs[:], hq_d[hh])
                    t_hv = kb.dma(hvs[:], hv_d[:, hh * 128:(hh + 1) * 128].rearrange("(c p) d -> p c d", p=HC))
                    gate_done = []
                    for d in range(2):
                        zsrc = zf_d if d == 0 else zb_d
                        tz = kb.dma(Zt[d][:], zsrc[hh])
                        lbc = lbt[:, d, l, hh:hh + 1]
                        omc = oml[:, d, l, hh:hh + 1]
                        t = kb.op("act", lambda e, d=d: e.activation(out=Zt[d][:], in_=Zt[d][:], func=AF.Sigmoid), deps=[tz])
                        t = kb.op("dve", lambda e, d=d, lbc=lbc, omc=omc: e.tensor_scalar(
                            out=Zt[d][:], in0=Zt[d][:], scalar1=omc, scalar2=lbc, op0=ALU.mult, op1=ALU.add), deps=[t])
                        tkk = kb.op("pool", lambda e, d=d: e.tensor_scalar(
                            out=KK[d][:], in0=Zt[d][:], scalar1=-1.0, scalar2=1.0, op0=ALU.mult, op1=ALU.add), deps=[t])
                        t = kb.op("dve", lambda e, d=d: e.tensor_scalar(
                            out=Zt[d][:], in0=Zt[d][:], scalar1=1e-20, scalar2=None, op0=ALU.max), deps=[t, tkk])
                        t = kb.op("act", lambda e, d=d: e.activation(out=Zt[d][:], in_=Zt[d][:], func=AF.Ln), deps=[t])
                        t0m = kb.op("dve", lambda e, d=d: e.memset(Et[d][:, 0:1], 0.0))
                        opx = ALU.add if d == 0 else ALU.subtract
                        t = kb.op("dve", lambda e, d=d, opx=opx: e.tensor_tensor_scan(
                            out=Et[d][:, 1:S + 1], data0=cst[:, 128:129].to_broadcast([128, S]), data1=Zt[d][:],
                            initial=0.0, op0=ALU.mult, op1=opx), deps=[t, t0m, t_cst])
                        eo = 1 if d == 0 else 0
                        Ev = Et[d][:, eo:eo + S].rearrange("p (c t) -> p c t", t=HC)
                        rho = Ev[:, :, HC // 2:HC // 2 + 1]
                        Dv = Zt[d][:].rearrange("p (c t) -> p c t", t=HC)
                        tD = kb.op("dve", lambda e, Dv=Dv, Ev=Ev, rho=rho: e.tensor_tensor(
                            out=Dv, in0=Ev, in1=rho.to_broadcast([128, NCH, HC]), op=ALU.subtract), deps=[t])
                        Ec = Et[d][:, 0:S].rearrange("p (c t) -> p c t", t=HC)[:, :, 0]
                        En = Et[d][:, HC:S + HC] if False else None
                        Enx = Et[d][:, 1:S + 1].rearrange("p (c t) -> p c t", t=HC)[:, :, HC - 1]
                        rh2 = Ev[:, :, HC // 2]
                        if d == 0:
                            specs3 = ((Enx, Ec), (Enx, rh2), (rh2, Ec))
                        else:
                            specs3 = ((Ec, Enx), (Ec, rh2), (rh2, Enx))
                        ts3 = []
                        for i3, (aa, bb) in enumerate(specs3):
                            ts3.append(kb.op("dve", lambda e, d=d, i3=i3, aa=aa, bb=bb: e.tensor_tensor(
                                out=csc[d][:, i3, :], in0=aa, in1=bb, op=ALU.subtract), deps=[t]))
                        tcs = kb.op("act", lambda e, d=d: e.activation(
                            out=csc[d][:, 0:3, :].rearrange("p a c -> p (a c)"),
                            in_=csc[d][:, 0:3, :].rearrange("p a c -> p (a c)"), func=AF.Exp), deps=[ts3])
                        tx = kb.op("act", lambda e, d=d: e.activation(out=Et[d][:, 0:S], in_=Zt[d][:], func=AF.Exp),
                                   deps=[tD, ts3])
                        tq = kb.op("dve", lambda e, d=d: e.tensor_tensor(out=Qt[d][:], in0=hqs[:], in1=Et[d][:, 0:S], op=ALU.mult),
                                   deps=[tx, t_hq])
                        tx2 = kb.op("act", lambda e, d=d: e.activation(out=Et[d][:, 0:S], in_=Zt[d][:], func=AF.Exp, scale=-1.0),
                                    deps=[tq])
                        tk_ = kb.op("pool", lambda e, d=d: e.tensor_tensor(out=Kt[d][:], in0=KK[d][:], in1=Et[d][:, 0:S], op=ALU.mult),
                                    deps=[tx2, tkk])
                        tS0 = kb.op("dve", lambda e, d=d: e.memset(Sst[d][:], 0.0))
                        tS1 = kb.op("pool", lambda e, d=d: e.memset(Sbf[d][0].t[:], 0.0))
                        Sbf[d][0].wrote(tS1)
                        gate_done.append([tq, tk_, tcs, tS0, tS1, t_hv])
                    pend = [None, None]
                    st_tok = [gate_done[0][3], gate_done[1][3]]
                    rdT = [[None, None], [None, None]]
                    rdA = [[None, None], [None, None]]
                    rdM = [[None, None], [None, None]]
                    for i in range(NCH + 1):
                        for d in range(2):
                            if i < NCH:
                                c = i if d == 0 else NCH - 1 - i
                                par = i % 2
                                cs = slice(c * HC, (c + 1) * HC)
                                bank = PP[d]
                                Mv = bank.t[:, 0, par * 256:par * 256 + 128]
                                Av = bank.t[0:64, 0, par * 256 + 128:par * 256 + 192]
                                Tv = PT[d][0:64, par * 128:(par + 1) * 128]
                                gd_ = gate_done[d] if i == 0 else ()
                                kts_ = KTs[d].next()
                                ats_ = ATs[d].next()
                                tT = kb.op("pe", lambda e, d=d, cs=cs, Tv=Tv: e.transpose(
                                    out=Tv, in_=Kt[d][:, cs], identity=identb[:]), deps=[gd_, t_ib, rdT[d][par]])
                                tA = kb.op("pe", lambda e, d=d, cs=cs, Av=Av: e.matmul(
                                    Av, lhsT=Kt[d][:, cs], rhs=Qt[d][:, cs], start=True, stop=True), deps=[rdA[d][par]])
                                teT = kb.op("act", lambda e, kts_=kts_, Tv=Tv: e.activation(out=kts_.t[:], in_=Tv, func=AF.Copy),
                                            deps=[tT, kts_.wdeps()])
                                kts_.wrote(teT)
                                rdT[d][par] = teT
                                teA = kb.op("dve", lambda e, ats_=ats_, Av=Av, d=d: e.tensor_tensor(
                                    out=ats_.t[:], in0=Av, in1=trib[:, d, 0, :], op=ALU.mult), deps=[tA, ats_.wdeps(), t_tb])
                                ats_.wrote(teA)
                                rdA[d][par] = teA
                                tM = kb.op("pe", lambda e, kts_=kts_, c=c, Mv=Mv: e.matmul(
                                    Mv, lhsT=kts_.t[:], rhs=hvs[:, c, :], start=True, stop=True), deps=[teT, rdM[d][par]])
                                kts_.read(tM)
                            if pend[d] is not None:
                                (pc, pi, sbuf_prev, ats_prev) = pend[d]
                                pcs = slice(pc * HC, (pc + 1) * HC)
                                slot = pi % 8
                                Ov = PP[d].t[:, 1, slot * HC:(slot + 1) * HC]
                                odeps = [sbuf_prev.rdeps(), ats_prev.rdeps()]
                                if slot == 0:
                                    odeps.append(PP[d].r.get("oev"))
                                kb.op("pe", lambda e, d=d, pcs=pcs, Ov=Ov, sbuf_prev=sbuf_prev: e.matmul(
                                    Ov, lhsT=sbuf_prev.t[:], rhs=Qt[d][:, pcs], start=True, stop=False), deps=odeps, inc=False)
                                tO = kb.op("pe", lambda e, d=d, pc=pc, Ov=Ov, ats_prev=ats_prev: e.matmul(
                                    Ov, lhsT=hvs[:, pc, :], rhs=ats_prev.t[:], start=False, stop=True))
                                sbuf_prev.read(tO)
                                ats_prev.read(tO)
                                if slot == 7:
                                    ob_ = ost[d].next()
                                    tev = kb.op("act", lambda e, d=d, ob_=ob_: e.activation(
                                        out=ob_.t[:], in_=PP[d].t[:, 1, :], func=AF.Copy), deps=[tO, ob_.wdeps()])
                                    PP[d].r["oev"] = tev
                                    ob_.wrote(tev)
                                    g8 = pi // 8
                                    if d == 0:
                                        tok0 = g8 * 512
                                        dstv = of_d[hh][:, tok0:tok0 + 512]
                                        srcv = ob_.t[:]
                                    else:
                                        tok0 = (NCH - 8 * (g8 + 1)) * HC
                                        dstv = ob_d[hh][:, tok0:tok0 + 512].rearrange("p (j t) -> p j t", t=HC)
                                        srcv = ob_.t[:].rearrange("p (j t) -> p j t", t=HC)
                                    if d == 0:
                                        ob_.read(kb.dma(dstv, srcv, deps=[tev]))
                                    else:
                                        for j in range(8):
                                            ob_.read(kb.dma(dstv[:, 7 - j, :], srcv[:, j, :], deps=[tev]))
                                pend[d] = None
                            if i < NCH:
                                cur_sb = Sbf[d][i % 2]
                                nxt_sb = Sbf[d][(i + 1) % 2]
                                pend[d] = (c, i, cur_sb, ats_)
                                tu1 = kb.op("dve", lambda e, d=d, c=c: e.tensor_scalar(
                                    out=Stm[d][:], in0=Sst[d][:], scalar1=csc[d][:, 0, c:c + 1], scalar2=None, op0=ALU.mult),
                                    deps=[st_tok[d], gd_])
                                tu2 = kb.op("dve", lambda e, d=d, c=c, Mv=Mv: e.scalar_tensor_tensor(
                                    out=Sst[d][:], in0=Mv, scalar=csc[d][:, 1, c:c + 1], in1=Stm[d][:], op0=ALU.mult, op1=ALU.add),
                                    deps=[tM, tu1])
                                st_tok[d] = tu2
                                rdM[d][par] = tu2
                                if i + 1 < NCH:
                                    cn = c + 1 if d == 0 else c - 1
                                    tsb = kb.op("act", lambda e, d=d, cn=cn, nxt_sb=nxt_sb: e.activation(
                                        out=nxt_sb.t[:], in_=Sst[d][:], func=AF.Identity, scale=csc[d][:, 2, cn:cn + 1]),
                                        deps=[tu2, nxt_sb.wdeps()])
                                    nxt_sb.wrote(tsb)
                                    st_tok[d] = [tu2, tsb]
                kb.barrier()
                if dbg == "hgrn_raw":
                    break
            with ExitStack() as ps:
                fsq = Buf(sb("fsq2", [128, 512], F32, ps))
                fsd = Buf(sb("fsd2", [128, 512], F32, ps))
                fob = Ring([Buf(sb(f"fob2{i}", [128, 512], BF16, ps)) for i in range(2)])
                lf = Ring([Buf(sb(f"lf{i}", [128, 512], F32, ps)) for i in range(2)])
                lb_ = Ring([Buf(sb(f"lb{i}", [128, 512], F32, ps)) for i in range(2)])
                lg = Ring([Buf(sb(f"lg{i}", [128, 512], F32, ps)) for i in range(2)])
                Zb = PP[3]
                chg = CO[("hgg", l)]
                for hh in range(4):
                    for qt in range(8):
                        sl = slice(qt * 512, (qt + 1) * 512)
                        a = lf.next(); b = lb_.next(); g = lg.next()
                        ta = kb.dma(a.t[:], of_d[hh][:, sl], deps=a.wdeps()); a.wrote(ta)
                        tb = kb.dma(b.t[:], ob_d[hh][:, sl], deps=b.wdeps()); b.wrote(tb)
                        tg = kb.dma(g.t[:], hg_d[hh][:, sl], deps=g.wdeps()); g.wrote(tg)
                        t1 = kb.op("pool", lambda e, a=a, b=b: e.tensor_tensor(out=a.t[:], in0=a.t[:], in1=b.t[:], op=ALU.add),
                                   deps=[ta, tb])
                        b.read(t1)
                        a.wrote(t1)
                        t2 = kb.op("act", lambda e, a=a: e.activation(out=fsq.t[:], in_=a.t[:], func=AF.Square), deps=[t1, fsq.wdeps()])
                        fsq.wrote(t2)
                        t3 = kb.op("pe", lambda e: e.matmul(Zb.t[:, 0, :], lhsT=onesf, rhs=fsq.t[:], start=True, stop=True),
                                   deps=[t2, Zb.wdeps(), t_cst])
                        fsq.read(t3)
                        Zb.wrote(t3)
                        t4 = kb.op("act", lambda e: e.activation(out=fsd.t[:], in_=Zb.t[:, 0, :], func=AF.Sqrt,
                                                                bias=epsc[:, 1:2], scale=1.0 / 128.0), deps=[t3, fsd.wdeps()])
                        Zb.read(t4)
                        t5 = kb.op("dve", lambda e: e.reciprocal(out=fsd.t[:], in_=fsd.t[:]), deps=[t4])
                        fsd.wrote(t5)
                        t6 = kb.op("dve", lambda e, a=a: e.tensor_tensor(out=a.t[:], in0=a.t[:], in1=fsd.t[:], op=ALU.mult), deps=[t5, t2])
                        fsd.read(t6)
                        so = fob.next()
                        t7 = kb.op("dve", lambda e, a=a, g=g, so=so: e.scalar_tensor_tensor(
                            out=so.t[:], in0=a.t[:], scalar=colp[:, chg:chg + 1], in1=g.t[:], op0=ALU.mult, op1=ALU.mult),
                            deps=[t6, tg, so.wdeps(), t_colp])
                        a.read(t7); g.read(t7)
                        so.wrote(t7)
                        so.read(kb.dma(cat_d[4 + hh][:, sl], so.t[:], deps=[t7]))
                kb.barrier()
            if dbg == "mix":
                break

            with ExitStack() as ps:
                wo = sb("wo", [128, 12, D], BF16, ps)
                with ExitStack() as ps2:
                    stg = Ring([Buf(sb(f"stgo{i}", [128, 2048], F32, ps2)) for i in range(3)])
                    wt = load_w(wo, w_out_d[l], 12, D, stg)
                    kb.barrier()
                G = sb("G1", [128, D], F32, ps)
                Bt = sb("B1", [128, D], F32, ps)
                gd = load_gb(G, Bt, RO[("ln1_g", l)], RO[("ln1_b", l)])
                lnb = ln_bufs(ps)
                rr = Ring([Buf(sb(f"r1_{i}", [128, D], F32, ps)) for i in range(3)])
                hr = Ring([Buf(sb(f"hTt1_{i}", [128, 8, 512], BF16, ps)) for i in range(2)])
                cin = Ring([Buf(sb(f"cin{i}", [128, 12, 512], BF16, ps)) for i in range(2)])
                accr = Ring([PP[0], PP[1], PP[2]])
                cat_v = cat_d.rearrange("c p t -> p c t")
                pending = [None]

                def flush():
                    if pending[0] is not None:
                        pending[0]()
                        pending[0] = None
                for tt in range(8):
                    ci = cin.next()
                    tdc = kb.dma(ci.t[:], cat_v[:, :, tt * 512:(tt + 1) * 512], deps=ci.wdeps())
                    ci.wrote(tdc)
                    hTt = hr.next()
                    toks = []
                    for s4 in range(4):
                        t0 = tt * 512 + s4 * 128
                        acc = accr.next()
                        for nn in range(2):
                            for kc in range(12):
                                tk = kb.op("pe", lambda e, acc=acc, nn=nn, kc=kc, ci=ci, s4=s4: e.matmul(
                                    acc.t[:, nn, :], lhsT=ci.t[:, kc, s4 * 128:(s4 + 1) * 128], rhs=wo[:, kc, nn * 512:(nn + 1) * 512],
                                    start=(kc == 0), stop=(kc == 11)),
                                    deps=[tdc, acc.wdeps()] if (kc == 0 and nn == 0) else (), inc=(kc == 11 and nn == 1))
                        acc.wrote(tk)
                        ci.read(tk)
                        r = rr.next()
                        td = kb.dma(r.t[:], xres_d[t0:t0 + 128, :], deps=r.wdeps())
                        r.wrote(td)
                        tr0 = kb.op("dve", lambda e, r=r, acc=acc: e.scalar_tensor_tensor(
                            out=r.t[:, 0:512], in0=r.t[:, 0:512], scalar=ALPHA, in1=acc.t[:, 0, :],
                            op0=ALU.mult, op1=ALU.add), deps=[td, tk])
                        tr = kb.op("dve", lambda e, r=r, acc=acc: e.scalar_tensor_tensor(
                            out=r.t[:, 512:1024], in0=r.t[:, 512:1024], scalar=ALPHA, in1=acc.t[:, 1, :],
                            op0=ALU.mult, op1=ALU.add), deps=[td, tk])
                        acc.read(tr)
                        etr = ln_tile(r, [tr], 128, t0, G, Bt, gd, lnb, False, hTt, s4 * 128)
                        flush()

                        def pend(etr=etr, toks=toks, hTt=hTt, s4=s4, tt=tt):
                            toks.extend(etr())
                            if s4 == 3:
                                hTt.wrote(toks[0])
                                for t in toks[1:]:
                                    hTt.wrote(t, fresh=False)
                                hTt.read(kb.dma(hT_v[:, :, tt * 512:(tt + 1) * 512], hTt.t[:], deps=toks))
                        pending[0] = pend
                flush()
                kb.barrier()
            if dbg == "ln1":
                break

            with ExitStack() as ps:
                wu = sb("wu", [128, 8, 2 * DFF], BF16, ps)
                wd = sb("wd", [128, 22, D], BF16, ps)
                with ExitStack() as ps2:
                    stg = Ring([Buf(sb(f"stgf{i}", [128, 2048], F32, ps2)) for i in range(3)])
                    wt = load_w(wu, w_up_d[l], 8, 2 * DFF, stg)
                    wt += load_w(wd, w_dn_d[l], 22, D, stg)
                    kb.barrier()
                G = sb("G2", [128, D], F32, ps)
                Bt = sb("B2", [128, D], F32, ps)
                gd = load_gb(G, Bt, RO[("ln2_g", l)], RO[("ln2_b", l)])
                lnb = ln_bufs(ps)
                rr = Ring([Buf(sb(f"r2_{i}", [128, D], F32, ps)) for i in range(2)])
                hr = Ring([Buf(sb(f"hTt2_{i}", [128, 8, 256], BF16, ps)) for i in range(2)])
                hin = Ring([Buf(sb(f"hin2_{i}", [128, 8, 256], BF16, ps)) for i in range(2)])
                gT = Ring([Buf(sb(f"gT{i}", [128, 22, 256], BF16, ps)) for i in range(2)])
                ca = Ring([Buf(sb(f"ca{i}", [128, 256], F32, ps)) for i in range(2)])
                cb_ = Ring([Buf(sb(f"cb{i}", [128, 256], F32, ps)) for i in range(2)])
                cs_ = Ring([Buf(sb(f"cs{i}", [128, 256], F32, ps)) for i in range(2)])
                ur = Ring([PP[0], PP[1]])
                acc = PP[2]
                last = (l == L - 1)
                WT = 254
                ffn_pending = [None]
                ntile = (S + WT - 1) // WT
                cw0 = CO[("cw", l)]
                cb0 = CO[("cb", l)]
                for ti in range(ntile):
                    T0 = ti * WT
                    W = min(WT, S - T0)
                    lo = T0 - 1
                    h = hin.next()
                    src_lo = max(lo, 0)
                    src_hi = min(lo + W + 2, S)
                    dlo = src_lo - lo
                    hdeps = h.wdeps()
                    tl = []
                    if dlo > 0:
                        tl.append(kb.op("pool", lambda e, h=h: e.memset(h.t[:, :, 0:1], 0.0), deps=hdeps))
                    if src_hi < lo + W + 2:
                        tl.append(kb.op("pool", lambda e, h=h, W=W: e.memset(h.t[:, :, W + 1:W + 2], 0.0), deps=hdeps))
                    tl.append(kb.dma(h.t[:, :, dlo:dlo + (src_hi - src_lo)], hT_v[:, :, src_lo:src_hi], deps=hdeps))
                    h.wrote(tl[0])
                    for t in tl[1:]:
                        h.wrote(t, fresh=False)
                    g = gT.next()
                    NW = W + 2
                    gw = []
                    for fc in range(22):
                        ub = ur.next()
                        for j, col in enumerate((fc * 128, DFF + fc * 128)):
                            for kc in range(8):
                                tk = kb.op("pe", lambda e, ub=ub, j=j, kc=kc, col=col, h=h, NW=NW: e.matmul(
                                    ub.t[:, j, 0:NW], lhsT=wu[:, kc, col:col + 128], rhs=h.t[:, kc, 0:NW],
                                    start=(kc == 0), stop=(kc == 7)),
                                    deps=[h.rdeps(), ub.wdeps()] if (kc == 0 and j == 0) else (), inc=(kc == 7 and j == 1))
                        ub.wrote(tk)
                        h.read(tk)
                        a = ca.next(); b = cb_.next(); sg = cs_.next()
                        res = []
                        for j, buf in ((0, a), (1, b)):
                            ch = j * 22 + fc
                            w0 = colp[:, cw0 + 0 * 44 + ch:cw0 + 0 * 44 + ch + 1]
                            w1 = colp[:, cw0 + 1 * 44 + ch:cw0 + 1 * 44 + ch + 1]
                            w2 = colp[:, cw0 + 2 * 44 + ch:cw0 + 2 * 44 + ch + 1]
                            t1 = kb.op("act", lambda e, buf=buf, ub=ub, j=j, w1=w1, W=W: e.activation(
                                out=buf.t[:, 0:W], in_=ub.t[:, j, 1:W + 1], func=AF.Identity, scale=w1), deps=[tk, buf.wdeps(), t_colp])
                            t2 = kb.op("dve", lambda e, buf=buf, ub=ub, j=j, w0=w0, W=W: e.scalar_tensor_tensor(
                                out=buf.t[:, 0:W], in0=ub.t[:, j, 0:W], scalar=w0, in1=buf.t[:, 0:W], op0=ALU.mult, op1=ALU.add),
                                deps=[t1])
                            t3 = kb.op("dve", lambda e, buf=buf, ub=ub, j=j, w2=w2, W=W: e.scalar_tensor_tensor(
                                out=buf.t[:, 0:W], in0=ub.t[:, j, 2:W + 2], scalar=w2, in1=buf.t[:, 0:W], op0=ALU.mult, op1=ALU.add),
                                deps=[t2])
                            buf.wrote(t3)
                            res.append(t3)
                        ub.read(res[1])
                        bg = colp[:, cb0 + fc:cb0 + fc + 1]
                        bv = colp[:, cb0 + 22 + fc:cb0 + 22 + fc + 1]
                        t4 = kb.op("act", lambda e, a=a, sg=sg, bg=bg, W=W: e.activation(
                            out=sg.t[:, 0:W], in_=a.t[:, 0:W], func=AF.Silu, bias=bg, scale=1.0), deps=[res[0], sg.wdeps()])
                        a.read(t4)
                        sg.wrote(t4)
                        t5 = kb.op("dve", lambda e, b=b, sg=sg, g=g, fc=fc, bv=bv, W=W: e.scalar_tensor_tensor(
                            out=g.t[:, fc, 0:W], in0=b.t[:, 0:W], scalar=bv, in1=sg.t[:, 0:W], op0=ALU.add, op1=ALU.mult),
                            deps=[res[1], t4, g.wdeps() if fc == 0 else None])
                        b.read(t5); sg.read(t5)
                        gw.append(t5)
                    g.wrote(gw[0])
                    for t in gw[1:]:
                        g.wrote(t, fresh=False)
                    if ffn_pending[0] is not None:
                        ffn_pending[0]()
                        ffn_pending[0] = None
                    hTt = hr.next()
                    toks = []
                    etrs = []
                    s0 = 0
                    while s0 < W:
                        n = min(128, W - s0)
                        t0 = T0 + s0
                        for nn in range(2):
                            for fc in range(22):
                                tk = kb.op("pe", lambda e, nn=nn, fc=fc, g=g, s0=s0, n=n: e.matmul(
                                    acc.t[0:n, nn, :], lhsT=g.t[:, fc, s0:s0 + n], rhs=wd[:, fc, nn * 512:(nn + 1) * 512],
                                    start=(fc == 0), stop=(fc == 21)),
                                    deps=[g.rdeps(), acc.wdeps()] if (fc == 0 and nn == 0) else (), inc=(fc == 21 and nn == 1))
                        acc.wrote(tk)
                        g.read(tk)
                        r = rr.next()
                        td = kb.dma(r.t[0:n, :], xres_d[t0:t0 + n, :], deps=r.wdeps())
                        r.wrote(td)
                        tr0 = kb.op("dve", lambda e, r=r, n=n: e.scalar_tensor_tensor(
                            out=r.t[0:n, 0:512], in0=r.t[0:n, 0:512], scalar=ALPHA, in1=acc.t[0:n, 0, :],
                            op0=ALU.mult, op1=ALU.add), deps=[td, tk])
                        tr = kb.op("dve", lambda e, r=r, n=n: e.scalar_tensor_tensor(
                            out=r.t[0:n, 512:1024], in0=r.t[0:n, 512:1024], scalar=ALPHA, in1=acc.t[0:n, 1, :],
                            op0=ALU.mult, op1=ALU.add), deps=[td, tk])
                        acc.read(tr)
                        etrs.append(ln_tile(r, [tr], n, t0, G, Bt, gd, lnb, last, hTt, s0))
                        s0 += n

                    def fpend(etrs=etrs, toks=toks, hTt=hTt, T0=T0, W=W):
                        for f in etrs:
                            toks.extend(f())
                        if not last:
                            hTt.wrote(toks[0])
                            for t in toks[1:]:
                                hTt.wrote(t, fresh=False)
                            hTt.read(kb.dma(hT_v[:, :, T0:T0 + W], hTt.t[:, :, 0:W], deps=toks))
                    ffn_pending[0] = fpend
                if ffn_pending[0] is not None:
                    ffn_pending[0]()
                    ffn_pending[0] = None
                kb.barrier()
        kb.finish(block)
    return nc


def _consts():
    c = np.zeros((128, 1024), np.float32)
    c[:, 0:128] = np.eye(128, dtype=np.float32)
    c[:, 128:256] = 1.0
    s = np.arange(64)[:, None]
    t = np.arange(64)[None, :]
    c[0:64, 256:320] = (s <= t)
    c[0:64, 320:384] = (s >= t)
    p = np.arange(128)
    dd = p % 64
    inv = (ROPE_THETA ** (-(np.arange(0, 16, 2, dtype=np.float32)) / 16.0)).astype(np.float32)
    c[:, 384] = inv[dd % 8]
    m = (dd < 16).astype(np.float32)
    c[:, 385] = m
    c[:, 386] = 1.0 - m
    c[:, 387] = np.where(dd < 8, -1.0, np.where(dd < 16, 1.0, 0.0))
    return c


def _rot_perm():
    idx = np.arange(512)
    dd = idx % 64
    base = idx - dd
    pd = np.where(dd < 8, dd + 8, np.where(dd < 16, dd - 8, dd))
    return base + pd


def _prep(inputs, L):
    CO = _col_layout(L)
    RO = _row_layout(L)
    f = lambda a: np.ascontiguousarray(np.asarray(a), dtype=np.float32)
    colp = np.zeros((128, CO["n"]), np.float32)
    lbf = f(inputs["hg_lb_fwd"])
    lbb = f(inputs["hg_lb_bwd"])
    colp[:, CO["lbf"]:CO["lbf"] + 16] = lbf.reshape(DEPTH, 4, 128).transpose(2, 0, 1).reshape(128, 16)
    colp[:, CO["lbb"]:CO["lbb"] + 16] = lbb.reshape(DEPTH, 4, 128).transpose(2, 0, 1).reshape(128, 16)
    for l in range(L):
        colp[:, CO[("dag", l)]] = f(inputs["da_norm_g"])[l]
        colp[:, CO[("hgg", l)]] = f(inputs["hg_norm_g"])[l]
        cw = f(inputs["conv_w"])[l]
        colp[:, CO[("cw", l)]:CO[("cw", l)] + 132] = cw.reshape(3, 44, 128).transpose(2, 0, 1).reshape(128, 132)
        colp[:, CO[("cb", l)]:CO[("cb", l)] + 44] = f(inputs["conv_b"])[l].reshape(44, 128).T
    rowp = np.zeros((1, RO["n"]), np.float32)
    rowp[0, RO["ln_in_g"]:RO["ln_in_g"] + D] = f(inputs["ln_in_g"])
    rowp[0, RO["ln_in_b"]:RO["ln_in_b"] + D] = f(inputs["ln_in_b"])
    for l in range(L):
        for nm in ("ln1_g", "ln1_b", "ln2_g", "ln2_b"):
            rowp[0, RO[(nm, l)]:RO[(nm, l)] + D] = f(inputs[nm])[l]
        rowp[0, RO[("lam", l)]:RO[("lam", l)] + 256] = f(inputs["da_lambda"])[l].reshape(256)
    w_in = f(inputs["w_in"])[:L]
    perm = _rot_perm()
    w_rot = np.ascontiguousarray(np.concatenate([w_in[:, :, 0:512][:, :, perm], w_in[:, :, 512:1024][:, :, perm]], axis=2))
    shared = {
        "colp": colp, "rowp": rowp, "cst": _consts(),
        "w_in": w_in, "w_rot": w_rot,
        "w_kv": f(inputs["w_mem_kv"])[:L], "w_out": f(inputs["w_out"])[:L],
        "w_up": f(inputs["w_up"])[:L], "w_dn": f(inputs["w_down"])[:L],
    }
    return shared


_NC_CACHE = {}


def kernel(**inputs):
    L = DEPTH
    x = np.asarray(inputs["x"], dtype=np.float32)
    mem = np.asarray(inputs["mem"], dtype=np.float32)
    pos = np.asarray(inputs["positions"]).astype(np.int32)
    B = x.shape[0]
    shared = _prep(inputs, L)
    if "nc" not in _NC_CACHE:
        _NC_CACHE["nc"] = build(L)
    nc = _NC_CACHE["nc"]
    in_maps = []
    for b in range(B):
        m = dict(shared)
        m["x"] = np.ascontiguousarray(x[b])
        m["mem"] = np.ascontiguousarray(mem[b])
        m["pos"] = np.ascontiguousarray(pos[b].reshape(1, S))
        in_maps.append(m)
    res = run_bass_kernel_spmd(nc, in_maps, core_ids=list(range(B)))
    return np.stack([np.asarray(r["out"], dtype=np.float32) for r in res.results], axis=0)
```

```python
import math
from contextlib import ExitStack
import numpy as np
import concourse.bass as bass
import concourse.mybir as mybir
from concourse.bass_utils import run_bass_kernel_spmd

F32 = mybir.dt.float32
BF16 = mybir.dt.bfloat16
I32 = mybir.dt.int32
AF = mybir.ActivationFunctionType
ALU = mybir.AluOpType

D = 1024
S = 4096
NM = 256
DEPTH = 4
DFF = 2816
ALPHA = (2 * DEPTH) ** 0.25
LN_EPS = 1e-5
RMS_EPS = 1e-6
ROPE_THETA = 500000.0
HC = 64
NCH = S // HC

MERGE_EXP = True
ENGS = ("pe", "act", "dve", "pool", "sp")
NDMA = 40


class Tok:
    __slots__ = ("sem", "val", "eng", "idx")

    def __init__(self, sem, val, eng, idx):
        self.sem, self.val, self.eng, self.idx = sem, val, eng, idx


class _Rec:
    def __init__(self):
        self.call = None

    def __getattr__(self, name):
        def f(*args, **kwargs):
            self.call = (name, args, kwargs)
            return None
        return f


class KB:
    def __init__(self, nc, es):
        self.nc = nc
        self.q = {e: [] for e in ENGS}
        self.cnt = {e: 0 for e in ENGS}
        self.waited = {e: {} for e in ENGS}
        self.sems = {}
        for e in ("pe", "act", "dve", "pool"):
            self.sems[e] = es.enter_context(nc.semaphore("s_" + e))
        for i in range(NDMA):
            self.sems[("dma", i)] = es.enter_context(nc.semaphore(f"s_dma{i}"))
        self.ndma = 0
        self.dma_toks = [None] * NDMA
        self.out_toks = []
        self.last = {e: None for e in ENGS}

    def _flat(self, deps, out):
        for t in deps:
            if t is None:
                continue
            if isinstance(t, (list, tuple)):
                self._flat(t, out)
            else:
                out.append(t)
        return out

    def _waits(self, eng, deps):
        ws = []
        w = self.waited[eng]
        myidx = len(self.q[eng])
        for t in self._flat(deps, []):
            if t.eng == eng and t.sem == eng and myidx - t.idx > 3:
                continue
            if w.get(t.sem, 0) >= t.val:
                continue
            w[t.sem] = t.val
            ws.append((t.sem, t.val))
        return ws

    def op(self, eng, fn, deps=(), inc=True):
        ws = self._waits(eng, deps)
        idx = len(self.q[eng])
        tok = None
        if inc:
            self.cnt[eng] += 1
            tok = Tok(eng, self.cnt[eng], eng, idx)
            self.last[eng] = tok
        sems = self.sems
        rec = _Rec()
        fn(rec)
        name, args, kwargs = rec.call

        def run(e, ws=ws, name=name, args=args, kwargs=kwargs, inc=inc, eng=eng):
            for (s, v) in ws:
                e.wait_ge(sems[s], v)
            ins = getattr(e, name)(*args, **kwargs)
            if inc:
                ins.then_inc(sems[eng], 1)
        self.q[eng].append(run)
        return tok

    def dma(self, out, in_, deps=(), is_output=False, eng="sp", slow=False):
        i = self.ndma
        self.ndma += 1
        slot = i % NDMA
        key = ("dma", slot)
        val = 16 * (i // NDMA + 1)
        ws = self._waits(eng, list(deps) + [self.dma_toks[slot]])
        tok = Tok(key, val, eng, len(self.q[eng]))
        self.dma_toks[slot] = tok
        sems = self.sems

        def run(e, ws=ws, out=out, in_=in_, key=key, slow=slow):
            for (s, v) in ws:
                e.wait_ge(sems[s], v)
            if slow:
                e.dma_start(out=out, in_=in_, allow_slow_non_contiguous=True).then_inc(sems[key], 16)
            else:
                e.dma_start(out=out, in_=in_).then_inc(sems[key], 16)
        self.q[eng].append(run)
        if is_output:
            self.out_toks.append(tok)
        return tok

    def barrier(self):
        toks = [self.last[e] for e in ("pe", "act", "dve", "pool")] + [t for t in self.dma_toks]
        for eng in ENGS:
            ws = self._waits(eng, toks)
            sems = self.sems

            def run(e, ws=ws):
                for (s, v) in ws:
                    e.wait_ge(sems[s], v)
            self.q[eng].append(run)

    def finish(self, block):
        ws = self._waits("sp", self.out_toks)
        sems = self.sems

        def fin(e, ws=ws):
            for (s, v) in ws:
                e.wait_ge(sems[s], v)
        self.q["sp"].append(fin)
        q = self.q

        @block.sync
        def _(e):
            for f in q["sp"]:
                f(e)

        @block.tensor
        def _(e):
            for f in q["pe"]:
                f(e)

        @block.scalar
        def _(e):
            for f in q["act"]:
                f(e)

        @block.vector
        def _(e):
            for f in q["dve"]:
                f(e)

        @block.gpsimd
        def _(e):
            for f in q["pool"]:
                f(e)


class Buf:
    def __init__(self, t):
        self.t = t
        self.w = []
        self.r = {}

    def wdeps(self):
        return [self.w, list(self.r.values())]

    def wrote(self, tok, fresh=True):
        if fresh:
            self.w = [tok]
            self.r = {}
        else:
            self.w.append(tok)

    def rdeps(self):
        return self.w

    def read(self, tok):
        if tok is not None:
            self.r[tok.eng if not isinstance(tok.sem, tuple) else ("d", len(self.r))] = tok


class Ring:
    def __init__(self, bufs):
        self.bufs = bufs
        self.i = 0

    def next(self):
        b = self.bufs[self.i % len(self.bufs)]
        self.i += 1
        return b


def _col_layout(L):
    off = {}
    c = 0
    off["lbf"] = c; c += 4 * DEPTH
    off["lbb"] = c; c += 4 * DEPTH
    for l in range(L):
        off[("dag", l)] = c; c += 1
        off[("hgg", l)] = c; c += 1
        off[("cw", l)] = c; c += 3 * 44
        off[("cb", l)] = c; c += 44
    off["n"] = c
    return off


def _row_layout(L):
    off = {}
    c = 0
    off["ln_in_g"] = c; c += D
    off["ln_in_b"] = c; c += D
    for l in range(L):
        for nm in ("ln1_g", "ln1_b", "ln2_g", "ln2_b"):
            off[(nm, l)] = c; c += D
        off[("lam", l)] = c; c += 256
    off["n"] = c
    return off


def build(L=DEPTH, dbg=False):
    nc = bass.Bass("TRN2", target_bir_lowering=False)
    CO = _col_layout(L)
    RO = _row_layout(L)
    dk = "ExternalOutput" if dbg else None

    def dram(name, shape, dtype, kind=None):
        if kind is None:
            return nc.dram_tensor(name, shape, dtype).ap()
        return nc.dram_tensor(name, shape, dtype, kind=kind).ap()

    x_d = dram("x", [S, D], F32, "ExternalInput")
    mem_d = dram("mem", [NM, D], F32, "ExternalInput")
    pos_d = dram("pos", [1, S], I32, "ExternalInput")
    colp_d = dram("colp", [128, CO["n"]], F32, "ExternalInput")
    rowp_d = dram("rowp", [1, RO["n"]], F32, "ExternalInput")
    cst_d = dram("cst", [128, 1024], F32, "ExternalInput")
    w_in_d = dram("w_in", [L, D, 4608], F32, "ExternalInput")
    w_rot_d = dram("w_rot", [L, D, 1024], F32, "ExternalInput")
    w_kv_d = dram("w_kv", [L, D, 1024], F32, "ExternalInput")
    w_out_d = dram("w_out", [L, 1536, D], F32, "ExternalInput")
    w_up_d = dram("w_up", [L, D, 2 * DFF], F32, "ExternalInput")
    w_dn_d = dram("w_dn", [L, DFF, D], F32, "ExternalInput")
    out_d = dram("out", [S, D], F32, "ExternalOutput")

    xres_d = dram("xres", [S, D], F32, dk)
    hT_d = dram("hT", [8, 128, S], BF16, dk)
    rope_d = dram("rope", [4, 128, S], F32, dk)
    qT_d = dram("qT", [4, 128, S], BF16, dk)
    kT_d = dram("kT", [4, 128, S], BF16, dk)
    v_d = dram("v", [S, 512], BF16, dk)
    hq_d = dram("hq", [4, 128, S], F32, dk)
    zf_d = dram("zf", [4, 128, S], F32, dk)
    zb_d = dram("zb", [4, 128, S], F32, dk)
    hv_d = dram("hv", [S, 512], BF16, dk)
    hg_d = dram("hg", [4, 128, S], F32, dk)
    mq_d = dram("mq", [4, 128, S], BF16, dk)
    of_d = dram("of", [4, 128, S], F32, dk)
    ob_d = dram("ob", [4, 128, S], F32, dk)
    cat_d = dram("cat", [12, 128, S], BF16, dk)
    if dbg:
        dbg_mk = dram("dbg_mk", [128, 4 * NM], BF16, dk)
        dbg_mv = dram("dbg_mv", [128, 1024], BF16, dk)
        dbg_sm = dram("dbg_sm", [128, 4], F32, dk)
        dbg_memT = dram("dbg_memT", [128, 8 * NM], BF16, dk)
        dbg_p = dram("dbg_p", [128, 1024], BF16, dk)
        dbg_oz = dram("dbg_oz", [128, 3, 512], F32, dk)

    with ExitStack() as es:
        kb = KB(nc, es)
        block = es.enter_context(nc.Block())

        uniq = [0]

        def sb(name, shape, dtype, ctx=es):
            uniq[0] += 1
            return ctx.enter_context(nc.sbuf_tensor(f"{name}_{uniq[0]}", shape, dtype))

        cst = sb("cst_s", [128, 1024], F32)
        colp = sb("colp_s", [128, CO["n"]], F32)
        identb = sb("identb", [128, 128], BF16)
        onesb = sb("onesb", [128, 128], BF16)
        trib = sb("trib", [64, 2, 8, 64], BF16)
        lbt = sb("lbt", [128, 2, L, 4], F32)
        oml = sb("oml", [128, 2, L, 4], F32)
        epsc = sb("epsc", [128, 2], F32)
        memT = sb("memT", [128, 8, NM], BF16)
        PSB = [Buf(es.enter_context(nc.psum_tensor(f"psb{i}", [128, 512], F32))) for i in range(0)]
        PP = [Buf(es.enter_context(nc.psum_tensor(f"pp{i}", [128, 2, 512], F32))) for i in range(4)]

        PT3 = [Buf(PP[3].t), Buf(PP[3].t)]
        PT3[0].bank = 0
        PT3[1].bank = 1
        ident = cst[:, 0:128]
        onesf = cst[:, 128:256]

        t_cst = kb.dma(cst[:], cst_d[:, :])
        t_colp = kb.dma(colp[:], colp_d[:, :])
        t_ib = kb.op("dve", lambda e: e.tensor_copy(out=identb[:], in_=cst[:, 0:128]), deps=[t_cst])
        t_ob = kb.op("dve", lambda e: e.tensor_copy(out=onesb[:], in_=cst[:, 128:256]), deps=[t_cst])
        for dr in range(2):
            for rep in range(8):
                t_tb = kb.op("dve", lambda e, dr=dr, rep=rep: e.tensor_copy(
                    out=trib[:, dr, rep, :], in_=cst[0:64, 256 + 64 * dr:320 + 64 * dr]), deps=[t_cst])
        kb.op("dve", lambda e: e.memset(epsc[:, 0:1], LN_EPS))
        kb.op("dve", lambda e: e.memset(epsc[:, 1:2], RMS_EPS))
        with ExitStack() as ps:
            ex = sb("lb_ex", [128, 2, DEPTH, 4], F32, ps)
            ssum = sb("lb_s", [128, 2, 4], F32, ps)
            t1 = kb.op("act", lambda e: e.activation(
                out=ex[:].rearrange("p a l h -> p (a l h)"), in_=colp[:, CO["lbf"]:CO["lbf"] + 8 * DEPTH], func=AF.Exp),
                deps=[t_colp])
            t2 = kb.op("dve", lambda e: e.tensor_tensor(out=ssum[:], in0=ex[:, :, 0, :], in1=ex[:, :, 1, :], op=ALU.add), deps=[t1])
            t2 = kb.op("dve", lambda e: e.tensor_tensor(out=ssum[:], in0=ssum[:], in1=ex[:, :, 2, :], op=ALU.add), deps=[t2])
            t2 = kb.op("dve", lambda e: e.tensor_tensor(out=ssum[:], in0=ssum[:], in1=ex[:, :, 3, :], op=ALU.add), deps=[t2])
            t2 = kb.op("dve", lambda e: e.reciprocal(out=ssum[:], in_=ssum[:]), deps=[t2])
            t3 = kb.op("dve", lambda e: e.memset(lbt[:, :, 0, :], 0.0))
            for l in range(1, L):
                if l == 1:
                    t3 = kb.op("dve", lambda e: e.tensor_copy(out=lbt[:, :, 1, :], in_=ex[:, :, 1, :]), deps=[t1, t3])
                else:
                    t3 = kb.op("dve", lambda e, l=l: e.tensor_tensor(out=lbt[:, :, l, :], in0=lbt[:, :, l - 1, :],
                                                                   in1=ex[:, :, l, :], op=ALU.add), deps=[t3])
            for l in range(1, L):
                t3 = kb.op("dve", lambda e, l=l: e.tensor_tensor(out=lbt[:, :, l, :], in0=lbt[:, :, l, :], in1=ssum[:],
                                                               op=ALU.mult), deps=[t3, t2])
            t3 = kb.op("dve", lambda e: e.tensor_scalar(out=oml[:].rearrange("p a l h -> p (a l h)"),
                                                       in0=lbt[:].rearrange("p a l h -> p (a l h)"),
                                                       scalar1=-1.0, scalar2=1.0, op0=ALU.mult, op1=ALU.add), deps=[t3])
            kb.barrier()

        cast_rr = [0]

        def cast(out, in_, deps):
            engs = ("dve", "pool", "act")
            eg = engs[cast_rr[0] % 3]
            cast_rr[0] += 1
            if eg == "act":
                return kb.op("act", lambda e: e.activation(out=out, in_=in_, func=AF.Copy), deps=deps)
            return kb.op(eg, lambda e: e.tensor_copy(out=out, in_=in_), deps=deps)

        def load_w(dst, src, nk, ncol, stg):
            toks = []
            step = 2048
            for k in range(nk):
                for c0 in range(0, ncol, step):
                    n = min(step, ncol - c0)
                    b = stg.next()
                    td = kb.dma(b.t[:, 0:n], src[k * 128:(k + 1) * 128, c0:c0 + n], deps=b.wdeps())
                    b.wrote(td)
                    tc = cast(dst[:, k, c0:c0 + n], b.t[:, 0:n], deps=[td])
                    b.read(tc)
                    toks.append(tc)
            return toks

        def ln_tile(r, rdeps, n, t0, G, Bt, gdeps, lnb, last, hTt, hcol):
            st, mv, sd, nmr = lnb["st"], lnb["mv"], lnb["sd"], lnb["nmr"]
            ta = kb.op("dve", lambda e: e.bn_stats(out=st.t[0:n, 0, :], in_=r.t[0:n, 0:512]), deps=[rdeps, st.wdeps()])
            tb = kb.op("dve", lambda e: e.bn_stats(out=st.t[0:n, 1, :], in_=r.t[0:n, 512:1024]), deps=[rdeps])
            st.wrote(tb)
            tc = kb.op("dve", lambda e: e.bn_aggr(out=mv.t[0:n, :], in_=st.t[0:n].rearrange("p a b -> p (a b)")),
                       deps=[ta, tb, mv.wdeps()])
            st.read(tc)
            mv.wrote(tc)
            td = kb.op("act", lambda e: e.activation(out=sd.t[0:n, :], in_=mv.t[0:n, 1:2], func=AF.Sqrt,
                                                    bias=epsc[0:n, 0:1], scale=1.0), deps=[tc, sd.wdeps()])
            sd.wrote(td)
            te = kb.op("dve", lambda e: e.reciprocal(out=sd.t[0:n, :], in_=sd.t[0:n, :]), deps=[td])
            sd.wrote(te)
            tf = kb.op("dve", lambda e: e.scalar_tensor_tensor(out=nmr.t[0:n, :], in0=mv.t[0:n, 0:1], scalar=-1.0,
                                                              in1=sd.t[0:n, :], op0=ALU.mult, op1=ALU.mult),
                       deps=[te, nmr.wdeps()])
            nmr.wrote(tf)
            mv.read(tf)
            tg = kb.op("act", lambda e: e.activation(out=r.t[0:n, :], in_=r.t[0:n, :], func=AF.Identity,
                                                    scale=sd.t[0:n, 0:1], bias=nmr.t[0:n, 0:1]), deps=[tf, te, tb])
            sd.read(tg)
            nmr.read(tg)
            th = kb.op("pool", lambda e: e.tensor_tensor(out=r.t[0:n, :], in0=r.t[0:n, :], in1=G[0:n, :], op=ALU.mult),
                       deps=[tg, gdeps])
            ti = kb.op("pool", lambda e: e.tensor_tensor(out=r.t[0:n, :], in0=r.t[0:n, :], in1=Bt[0:n, :], op=ALU.add),
                       deps=[th])
            r.wrote(ti)
            if last:
                tdm = kb.dma(out_d[t0:t0 + n, :], r.t[0:n, :], deps=[ti], is_output=True)
                r.read(tdm)
                return lambda: []
            tdm = kb.dma(xres_d[t0:t0 + n, :], r.t[0:n, :], deps=[ti])
            r.read(tdm)

            def emit_tr():
                toks = []
                for hf in range(2):
                    pb = lnb["pt"].next()
                    bk = pb.bank
                    for j in range(4):
                        kc = hf * 4 + j
                        tk = kb.op("pe", lambda e: e.transpose(
                            out=pb.t[:, bk, j * 128:j * 128 + n], in_=r.t[0:n, kc * 128:(kc + 1) * 128],
                            identity=ident[0:n, 0:n]), deps=[ti, pb.wdeps() if j == 0 else None, t_cst], inc=(j == 3))
                    pb.wrote(tk)
                    r.read(tk)
                    src = pb.t[:, bk, :].rearrange("p (j t) -> p j t", j=4)[:, :, 0:n]
                    dst = hTt.t[:, hf * 4:(hf + 1) * 4, hcol:hcol + n]
                    if hf == 0:
                        te2 = kb.op("act", lambda e: e.activation(out=dst, in_=src, func=AF.Copy),
                                    deps=[tk, hTt.wdeps()])
                    else:
                        te2 = kb.op("dve", lambda e: e.tensor_copy(out=dst, in_=src),
                                    deps=[tk, hTt.wdeps()])
                    pb.read(te2)
                    toks.append(te2)
                return toks
            return emit_tr

        def ln_bufs(ctx):
            return {
                "st": Buf(sb("ln_st", [128, 2, 6], F32, ctx)),
                "mv": Buf(sb("ln_mv", [128, 2], F32, ctx)),
                "sd": Buf(sb("ln_sd", [128, 1], F32, ctx)),
                "nmr": Buf(sb("ln_nmr", [128, 1], F32, ctx)),
                "pt": Ring(PT3),
            }

        def load_gb(G, Bt, og, ob_):
            ta = kb.dma(G[:], rowp_d[0:1, og:og + D].to_broadcast([128, D]))
            tb = kb.dma(Bt[:], rowp_d[0:1, ob_:ob_ + D].to_broadcast([128, D]))
            return [ta, tb]

        hT_v = hT_d.rearrange("k p t -> p k t")

        with ExitStack() as ps:
            mt_ = sb("memf", [128, 2, D], F32, ps)
            td = kb.dma(mt_[:], mem_d.rearrange("(a p) d -> p a d", p=128))
            for a in range(2):
                for hf in range(2):
                    pb = PP[hf]
                    for j in range(4):
                        kc = hf * 4 + j
                        tk = kb.op("pe", lambda e, pb=pb, j=j, kc=kc, a=a: e.transpose(
                            out=pb.t[:, 0, j * 128:(j + 1) * 128], in_=mt_[:, a, kc * 128:(kc + 1) * 128],
                            identity=ident), deps=[td, t_cst, pb.wdeps() if j == 0 else None], inc=(j == 3))
                    pb.wrote(tk)
                    te = kb.op("dve", lambda e, pb=pb, hf=hf, a=a: e.tensor_copy(
                        out=memT[:, hf * 4:(hf + 1) * 4, a * 128:(a + 1) * 128],
                        in_=pb.t[:, 0, :].rearrange("p (j t) -> p j t", j=4)), deps=[tk])
                    pb.read(te)
            posi = sb("posi", [128, S], I32, ps)
            ang = sb("ang", [128, S], F32, ps)
            a2 = sb("ang2", [128, S], F32, ps)
            ki = sb("ki", [128, S], I32, ps)
            kf = sb("kf", [128, S], F32, ps)
            tp = kb.dma(posi[:], pos_d[0:1, :].to_broadcast([128, S]))
            t0_ = kb.op("dve", lambda e: e.tensor_copy(out=ang[:], in_=posi[:]), deps=[tp])
            t0_ = kb.op("dve", lambda e: e.tensor_scalar(out=ang[:], in0=ang[:], scalar1=cst[:, 384:385], scalar2=None,
                                                        op0=ALU.mult), deps=[t0_, t_cst])
            TWO_PI = 2.0 * math.pi

            def reduce_sin(shift, out_scale_col, dst_idx_list):
                t = kb.op("dve", lambda e: e.tensor_scalar(out=kf[:], in0=ang[:], scalar1=shift, scalar2=1.0 / TWO_PI,
                                                          op0=ALU.add, op1=ALU.mult), deps=[t0_])
                t = kb.op("dve", lambda e: e.tensor_copy(out=ki[:], in_=kf[:]), deps=[t])
                t = kb.op("dve", lambda e: e.tensor_copy(out=kf[:], in_=ki[:]), deps=[t])
                t = kb.op("dve", lambda e: e.scalar_tensor_tensor(out=a2[:], in0=kf[:], scalar=-TWO_PI, in1=ang[:],
                                                                 op0=ALU.mult, op1=ALU.add), deps=[t])
                if shift != 0.0:
                    t = kb.op("dve", lambda e: e.tensor_scalar(out=a2[:], in0=a2[:], scalar1=shift, scalar2=None,
                                                              op0=ALU.add), deps=[t])
                t = kb.op("dve", lambda e: e.tensor_scalar(out=kf[:], in0=a2[:], scalar1=math.pi, scalar2=-TWO_PI,
                                                          op0=ALU.is_gt, op1=ALU.mult), deps=[t])
                t = kb.op("dve", lambda e: e.tensor_tensor(out=a2[:], in0=a2[:], in1=kf[:], op=ALU.add), deps=[t])
                t = kb.op("dve", lambda e: e.tensor_scalar(out=kf[:], in0=a2[:], scalar1=-math.pi, scalar2=TWO_PI,
                                                          op0=ALU.is_lt, op1=ALU.mult), deps=[t])
                t = kb.op("dve", lambda e: e.tensor_tensor(out=a2[:], in0=a2[:], in1=kf[:], op=ALU.add), deps=[t])
                t = kb.op("dve", lambda e: e.tensor_scalar(out=a2[:], in0=a2[:], scalar1=-3.14159, scalar2=3.14159,
                                                          op0=ALU.max, op1=ALU.min), deps=[t])
                t = kb.op("act", lambda e: e.activation(out=a2[:], in_=a2[:], func=AF.Sin), deps=[t])
                return t

            t = reduce_sin(math.pi / 2.0, None, None)
            t = kb.op("dve", lambda e: e.tensor_scalar(out=kf[:], in0=a2[:], scalar1=cst[:, 385:386],
                                                      scalar2=cst[:, 386:387], op0=ALU.mult, op1=ALU.add), deps=[t])
            tck = kb.dma(rope_d[2], kf[:], deps=[t])
            t = kb.op("pool", lambda e: e.tensor_scalar(out=a2[:], in0=kf[:], scalar1=0.125, scalar2=None, op0=ALU.mult),
                      deps=[t])
            tcq = kb.dma(rope_d[0], a2[:], deps=[t])
            t = reduce_sin(0.0, None, None) if False else None
            kb.barrier()
            t = reduce_sin(0.0, None, None)
            t = kb.op("dve", lambda e: e.tensor_scalar(out=kf[:], in0=a2[:], scalar1=cst[:, 387:388], scalar2=None,
                                                      op0=ALU.mult), deps=[t])
            kb.dma(rope_d[3], kf[:], deps=[t])
            t = kb.op("pool", lambda e: e.tensor_scalar(out=a2[:], in0=kf[:], scalar1=0.125, scalar2=None, op0=ALU.mult),
                      deps=[t])
            kb.dma(rope_d[1], a2[:], deps=[t])
            kb.barrier()

        with ExitStack() as ps:
            G = sb("G0", [128, D], F32, ps)
            Bt = sb("B0", [128, D], F32, ps)
            gd = load_gb(G, Bt, RO["ln_in_g"], RO["ln_in_b"])
            lnb = ln_bufs(ps)
            rr = Ring([Buf(sb(f"r0_{i}", [128, D], F32, ps)) for i in range(3)])
            hr = Ring([Buf(sb(f"hTt0_{i}", [128, 8, 512], BF16, ps)) for i in range(2)])
            for tt in range(S // 512):
                hTt = hr.next()
                toks = []
                for s4 in range(4):
                    t0 = tt * 512 + s4 * 128
                    r = rr.next()
                    td = kb.dma(r.t[:], x_d[t0:t0 + 128, :], deps=r.wdeps())
                    r.wrote(td)
                    toks += ln_tile(r, [td], 128, t0, G, Bt, gd, lnb, False, hTt, s4 * 128)()
                hTt.wrote(toks[0])
                for t in toks[1:]:
                    hTt.wrote(t, fresh=False)
                tdm = kb.dma(hT_v[:, :, tt * 512:(tt + 1) * 512], hTt.t[:], deps=toks)
                hTt.read(tdm)
            kb.barrier()

        for l in range(L):
            lam_init = 0.8 - 0.6 * math.exp(-0.3 * l)
            with ExitStack() as ps:
                wi = sb("wi", [128, 8, 4608], BF16, ps)
                wr = sb("wr", [128, 8, 1024], BF16, ps)
                with ExitStack() as ps2:
                    stg = Ring([Buf(sb(f"stg{i}", [128, 2048], F32, ps2)) for i in range(3)])
                    wtoks = load_w(wi, w_in_d[l], 8, 4608, stg)
                    wtoks += load_w(wr, w_rot_d[l], 8, 1024, stg)
                    kb.barrier()
                hin = Ring([Buf(sb(f"hin{i}", [128, 8, 512], BF16, ps)) for i in range(2)])
                rtab = Ring([Buf(sb(f"rtab{i}", [128, 4, 512], F32, ps)) for i in range(2)])
                tmpA = Ring([Buf(sb(f"tmpA{i}", [128, 512], F32, ps)) for i in range(2)])
                tmpB = Ring([Buf(sb(f"tmpB{i}", [128, 512], F32, ps)) for i in range(2)])
                stF = Ring([Buf(sb(f"stF{i}", [128, 512], F32, ps)) for i in range(4)])
                stH = Ring([Buf(sb(f"stH{i}", [128, 512], BF16, ps)) for i in range(4)])
                pr = Ring(PP)

                def mm_group(dst, kc_list, lhs_fn, rhs_fn, deps):
                    tk = None
                    nk = len(kc_list)
                    for i, kc in enumerate(kc_list):
                        tk = kb.op("pe", lambda e, kc=kc, i=i: e.matmul(dst, lhsT=lhs_fn(kc), rhs=rhs_fn(kc),
                                                                      start=(i == 0), stop=(i == nk - 1)),
                                   deps=deps if i == 0 else (), inc=(i == nk - 1))
                    return tk

                for tt in range(8):
                    c0 = tt * 512
                    h = hin.next()
                    td = kb.dma(h.t[:], hT_v[:, :, c0:c0 + 512], deps=h.wdeps())
                    h.wrote(td)
                    rt = rtab.next()
                    td2 = kb.dma(rt.t[:], rope_d[:, :, c0:c0 + 512].rearrange("a p t -> p a t"), deps=rt.wdeps())
                    rt.wrote(td2)
                    for which in range(2):
                        for hh in range(4):
                            col = which * 512 + hh * 128
                            pb = pr.next()
                            tA = mm_group(pb.t[:, 0, :], range(8), lambda kc, col=col: wi[:, kc, col:col + 128],
                                          lambda kc, h=h: h.t[:, kc, :], [td, pb.wdeps()])
                            tB = mm_group(pb.t[:, 1, :], range(8), lambda kc, col=col: wr[:, kc, col:col + 128],
                                          lambda kc, h=h: h.t[:, kc, :], [])
                            pb.wrote(tB)
                            h.read(tB)
                            a = tmpA.next()
                            b = tmpB.next()
                            t1 = kb.op("dve", lambda e, a=a, pb=pb, rt=rt, which=which: e.tensor_tensor(
                                out=a.t[:], in0=pb.t[:, 0, :], in1=rt.t[:, 2 * which, :], op=ALU.mult),
                                deps=[tA, td2, a.wdeps()])
                            t2 = kb.op("dve", lambda e, b=b, pb=pb, rt=rt, which=which: e.tensor_tensor(
                                out=b.t[:], in0=pb.t[:, 1, :], in1=rt.t[:, 2 * which + 1, :], op=ALU.mult),
                                deps=[tB, td2, b.wdeps()])
                            pb.read(t2)
                            rt.read(t2)
                            a.wrote(t1)
                            b.wrote(t2)
                            so = stH.next()
                            t3 = kb.op("pool", lambda e, a=a, b=b, so=so: e.tensor_tensor(
                                out=so.t[:], in0=a.t[:], in1=b.t[:], op=ALU.add), deps=[t1, t2, so.wdeps()])
                            a.read(t3)
                            b.read(t3)
                            so.wrote(t3)
                            dst = (qT_d if which == 0 else kT_d)[hh][:, c0:c0 + 512]
                            so.read(kb.dma(dst, so.t[:], deps=[t3]))
                    specs = []
                    for hh in range(4):
                        specs.append((1536 + hh * 128, hq_d[hh], AF.Silu, 1.0, False))
                        specs.append((2048 + hh * 128, zf_d[hh], AF.Copy, 1.0, False))
                        specs.append((2560 + hh * 128, zb_d[hh], AF.Copy, 1.0, False))
                        specs.append((3584 + hh * 128, hg_d[hh], AF.Silu, 1.0, False))
                        specs.append((4096 + hh * 128, mq_d[hh], AF.Copy, 128.0 ** -0.5, True))
                    for i in range(0, len(specs), 2):
                        pb = pr.next()
                        tks = []
                        for j in range(2):
                            col = specs[i + j][0]
                            tks.append(mm_group(pb.t[:, j, :], range(8), lambda kc, col=col: wi[:, kc, col:col + 128],
                                                lambda kc, h=h: h.t[:, kc, :], [td, pb.wdeps()] if j == 0 else []))
                        pb.wrote(tks[1])
                        h.read(tks[1])
                        for j in range(2):
                            col, dst, fn, sc, isb = specs[i + j]
                            so = (stH if isb else stF).next()
                            if fn == AF.Copy and not isb and (i + j) % 2 == 0:
                                te = kb.op("dve", lambda e, so=so, pb=pb, j=j: e.tensor_copy(out=so.t[:], in_=pb.t[:, j, :]),
                                           deps=[tks[j], so.wdeps()])
                            else:
                                te = kb.op("act", lambda e, so=so, pb=pb, j=j, fn=fn, sc=sc: e.activation(
                                    out=so.t[:], in_=pb.t[:, j, :], func=fn, scale=sc), deps=[tks[j], so.wdeps()])
                            pb.read(te)
                            so.wrote(te)
                            so.read(kb.dma(dst[:, c0:c0 + 512], so.t[:], deps=[te]))
                    for (col, dst) in ((1024, v_d), (3072, hv_d)):
                        for s2 in range(2):
                            pb = pr.next()
                            tks = []
                            for j in range(2):
                                sub = s2 * 2 + j
                                tks.append(mm_group(pb.t[:, j, :], range(8),
                                                    lambda kc, h=h, sub=sub: h.t[:, kc, sub * 128:(sub + 1) * 128],
                                                    lambda kc, col=col: wi[:, kc, col:col + 512],
                                                    [td, pb.wdeps()] if j == 0 else []))
                            pb.wrote(tks[1])
                            h.read(tks[1])
                            for j in range(2):
                                sub = s2 * 2 + j
                                so = stH.next()
                                if j == 0:
                                    te = kb.op("dve", lambda e, so=so, pb=pb, j=j: e.tensor_copy(out=so.t[:], in_=pb.t[:, j, :]),
                                               deps=[tks[j], so.wdeps()])
                                else:
                                    te = kb.op("act", lambda e, so=so, pb=pb, j=j: e.activation(
                                        out=so.t[:], in_=pb.t[:, j, :], func=AF.Copy), deps=[tks[j], so.wdeps()])
                                pb.read(te)
                                so.wrote(te)
                                so.read(kb.dma(dst[c0 + sub * 128:c0 + (sub + 1) * 128, :], so.t[:], deps=[te]))
                kb.barrier()
            if dbg and dbg == "inproj":
                break

            with ExitStack() as ps:
                lamt = sb("lamt", [128, 256], F32, ps)
                lamp = sb("lamp", [128, 2, 64], F32, ps)
                lams = sb("lams", [128, 2], F32, ps)
                nlam = sb("nlam", [128, 1], F32, ps)
                gcol = sb("gcol", [128, 1], F32, ps)
                ro = RO[("lam", l)]
                td = kb.dma(lamt[:], rowp_d[0:1, ro:ro + 256].to_broadcast([128, 256]))
                lv = lamt[:].rearrange("p (a d) -> p a d", a=4)
                t = kb.op("dve", lambda e: e.tensor_tensor(out=lamp[:, 0, :], in0=lv[:, 0, :], in1=lv[:, 1, :], op=ALU.mult), deps=[td])
                t = kb.op("dve", lambda e: e.tensor_tensor(out=lamp[:, 1, :], in0=lv[:, 2, :], in1=lv[:, 3, :], op=ALU.mult), deps=[td, t])
                t = kb.op("dve", lambda e: e.tensor_reduce(out=lams[:], in_=lamp[:], axis=mybir.AxisListType.X, op=ALU.add), deps=[t])
                t = kb.op("act", lambda e: e.activation(out=lams[:], in_=lams[:], func=AF.Exp), deps=[t])
                t = kb.op("dve", lambda e: e.tensor_tensor(out=nlam[:], in0=lams[:, 1:2], in1=lams[:, 0:1], op=ALU.subtract), deps=[t])
                t = kb.op("dve", lambda e: e.tensor_scalar(out=nlam[:], in0=nlam[:], scalar1=-lam_init, scalar2=None, op0=ALU.add), deps=[t])
                t_lam = t
                cdg = CO[("dag", l)]
                t_g = kb.op("dve", lambda e: e.tensor_scalar(out=gcol[:], in0=colp[:, cdg:cdg + 1], scalar1=1.0 - lam_init,
                                                            scalar2=None, op0=ALU.mult), deps=[t_colp])
                mkT = sb("mkT", [128, 4, NM], BF16, ps)
                mv_ = sb("mv", [128, 2, 512], BF16, ps)
                with ExitStack() as ps2:
                    wkv = sb("wkv", [128, 8, 1024], BF16, ps2)
                    stg = Ring([Buf(sb(f"stgm{i}", [128, 2048], F32, ps2)) for i in range(3)])
                    wt = load_w(wkv, w_kv_d[l], 8, 1024, stg)
                    for hp in range(2):
                        pb = PP[hp]
                        for j in range(2):
                            hh = hp * 2 + j
                            for kc in range(8):
                                tk = kb.op("pe", lambda e, pb=pb, j=j, kc=kc, hh=hh: e.matmul(
                                    pb.t[:, j, 0:NM], lhsT=wkv[:, kc, hh * 128:(hh + 1) * 128], rhs=memT[:, kc, :],
                                    start=(kc == 0), stop=(kc == 7)), deps=[wt] if kc == 0 else (), inc=(kc == 7))
                            te = kb.op("act", lambda e, pb=pb, j=j, hh=hh: e.activation(
                                out=mkT[:, hh, :], in_=pb.t[:, j, 0:NM], func=AF.Copy), deps=[tk])
                    pb = PP[2]
                    for mt in range(2):
                        for kc in range(8):
                            tk = kb.op("pe", lambda e, mt=mt, kc=kc: e.matmul(
                                PP[2].t[:, mt, :], lhsT=memT[:, kc, mt * 128:(mt + 1) * 128], rhs=wkv[:, kc, 512:1024],
                                start=(kc == 0), stop=(kc == 7)), deps=[wt] if kc == 0 else (), inc=(kc == 7))
                        te = kb.op("dve", lambda e, mt=mt: e.tensor_copy(out=mv_[:, mt, :], in_=PP[2].t[:, mt, :]), deps=[tk])
                    kb.barrier()
                    if dbg:
                        kb.dma(dbg_mk[:, :], mkT[:].rearrange("p a m -> p (a m)"))
                        kb.dma(dbg_mv[:, :], mv_[:].rearrange("p a m -> p (a m)"))
                        kb.dma(dbg_memT[:, :], memT[:].rearrange("p a m -> p (a m)"))
                        kb.dma(dbg_sm[:, 0:1], nlam[:], slow=True)
                        kb.dma(dbg_sm[:, 1:2], gcol[:], slow=True)
                        kb.dma(dbg_sm[:, 2:4], lams[:], slow=True)
                        kb.barrier()

                kts = Ring([Buf(sb(f"kts{i}", [128, S], BF16, ps)) for i in range(2)])
                vts = Ring([Buf(sb(f"vts{i}", [128, 32, 128], BF16, ps)) for i in range(2)])
                qts = Ring([Buf(sb(f"qts{i}", [128, 512], BF16, ps)) for i in range(2)])
                pts = Ring([Buf(sb(f"pts{i}", [128, 2, 512], BF16, ps)) for i in range(3)])
                rz = Buf(sb("rz", [128, 2, 512], F32, ps))
                zacc = Buf(sb("zacc", [128, 2, 512], F32, ps))
                fa = Buf(sb("fa", [128, 512], F32, ps))
                fb = Buf(sb("fb", [128, 512], F32, ps))
                fo = Buf(sb("fo", [128, 512], F32, ps))
                fsq = Buf(sb("fsq", [128, 512], F32, ps))
                fsd = Buf(sb("fsd", [128, 512], F32, ps))
                fob = Ring([Buf(sb(f"fob{i}", [128, 512], BF16, ps)) for i in range(2)])
                scr = Ring([PP[0], PP[1]])
                Ob, Zb = PP[2], PP[3]

                def attn_core(qb, nkt, lhs_s, lhs_v, ncomp):
                    def issue_S(kt):
                        sc = scr.next()
                        tS = None
                        for c in range(2):
                            tS = kb.op("pe", lambda e: e.matmul(
                                sc.t[:, c, :], lhsT=lhs_s(kt, c), rhs=qb.t[c * 64:(c + 1) * 64, :], start=True, stop=True),
                                deps=[qb.rdeps(), sc.wdeps()] if c == 0 else (), inc=(c == 1))
                        sc.wrote(tS)
                        return sc, tS
                    tP = tacc = None
                    pend = issue_S(0)
                    for kt in range(nkt):
                        sc, tS = pend
                        pend = issue_S(kt + 1) if kt + 1 < nkt else None
                        p = pts.next()
                        if MERGE_EXP:
                            te0 = te = kb.op("act", lambda e: e.activation(
                                out=p.t[:].rearrange("p a t -> p (a t)"), in_=sc.t[:].rearrange("p a t -> p (a t)"),
                                func=AF.Exp), deps=[tS, p.wdeps()])
                        else:
                            te0 = kb.op("act", lambda e: e.activation(
                                out=p.t[:, 0, :], in_=sc.t[:, 0, :], func=AF.Exp), deps=[tS, p.wdeps()])
                            te = kb.op("act", lambda e: e.activation(
                                out=p.t[:, 1, :], in_=sc.t[:, 1, :], func=AF.Exp), deps=[tS])
                        sc.read(te)
                        p.wrote(te)
                        for c in range(2):
                            tP = kb.op("pe", lambda e: e.matmul(
                                Ob.t[:, c, :], lhsT=lhs_v(kt), rhs=p.t[:, c, :], start=(kt == 0), stop=(kt == nkt - 1)),
                                deps=[te0 if c == 0 else te, Ob.wdeps() if (c == 0 and kt == 0) else None], inc=(c == 1))
                        p.read(tP)
                        za = zacc.t[:].rearrange("p a t -> p (a t)")
                        pa = p.t[:].rearrange("p a t -> p (a t)")
                        if kt == 0:
                            tacc = kb.op("dve", lambda e: e.tensor_copy(out=za, in_=pa), deps=[te, zacc.wdeps()])
                        else:
                            tacc = kb.op("dve", lambda e: e.tensor_tensor(out=za, in0=za, in1=pa, op=ALU.add), deps=[te])
                        p.read(tacc)
                    zacc.wrote(tacc)
                    tZ = None
                    for c in range(2):
                        tZ = kb.op("pe", lambda e: e.matmul(Zb.t[:, c, :], lhsT=onesf, rhs=zacc.t[:, c, :], start=True, stop=True),
                                   deps=[tacc, Zb.wdeps(), t_cst] if c == 0 else (), inc=(c == 1))
                    zacc.read(tZ)
                    Ob.wrote(tZ)
                    Zb.wrote(tZ)
                    return tZ

                def rms_store(o_buf, t_o, scale_col, gate_buf, dst, extra_deps=()):
                    t1 = kb.op("act", lambda e: e.activation(out=fsq.t[:], in_=o_buf.t[:], func=AF.Square),
                               deps=[t_o, fsq.wdeps()])
                    fsq.wrote(t1)
                    o_buf.read(t1)
                    t2 = kb.op("pe", lambda e: e.matmul(Zb.t[:, 0, :], lhsT=onesf, rhs=fsq.t[:], start=True, stop=True),
                               deps=[t1, Zb.wdeps(), t_cst])
                    fsq.read(t2)
                    Zb.wrote(t2)
                    t3 = kb.op("act", lambda e: e.activation(out=fsd.t[:], in_=Zb.t[:, 0, :], func=AF.Sqrt,
                                                            bias=epsc[:, 1:2], scale=1.0 / 128.0), deps=[t2, fsd.wdeps()])
                    Zb.read(t3)
                    t4 = kb.op("dve", lambda e: e.reciprocal(out=fsd.t[:], in_=fsd.t[:]), deps=[t3])
                    fsd.wrote(t4)
                    t5 = kb.op("dve", lambda e: e.tensor_tensor(out=o_buf.t[:], in0=o_buf.t[:], in1=fsd.t[:], op=ALU.mult),
                               deps=[t4, t1])
                    fsd.read(t5)
                    so = fob.next()
                    if gate_buf is None:
                        t6 = kb.op("dve", lambda e, so=so: e.tensor_scalar(out=so.t[:], in0=o_buf.t[:], scalar1=scale_col,
                                                                         scalar2=None, op0=ALU.mult),
                                   deps=[t5, so.wdeps(), extra_deps])
                    else:
                        t6 = kb.op("dve", lambda e, so=so: e.scalar_tensor_tensor(
                            out=so.t[:], in0=o_buf.t[:], scalar=scale_col, in1=gate_buf.t[:], op0=ALU.mult, op1=ALU.mult),
                            deps=[t5, so.wdeps(), gate_buf.rdeps(), extra_deps])
                        gate_buf.read(t6)
                    o_buf.wrote(t6)
                    so.wrote(t6)
                    so.read(kb.dma(dst, so.t[:], deps=[t6]))

                for hh in range(4 if dbg != "mem" else 0):
                    kt_ = kts.next()
                    tdk = kb.dma(kt_.t[:], kT_d[hh], deps=kt_.wdeps())
                    kt_.wrote(tdk)
                    vt_ = vts.next()
                    tdv = kb.dma(vt_.t[:], v_d[:, hh * 128:(hh + 1) * 128].rearrange("(k p) d -> p k d", p=128),
                                 deps=vt_.wdeps())
                    vt_.wrote(tdv)
                    for qt in range(8):
                        qb = qts.next()
                        tdq = kb.dma(qb.t[:], qT_d[hh][:, qt * 512:(qt + 1) * 512], deps=qb.wdeps())
                        qb.wrote(tdq)
                        first = [True]

                        def lhs_s(kt, c, kt_=kt_):
                            return kt_.t[c * 64:(c + 1) * 64, kt * 128:(kt + 1) * 128]

                        def lhs_v(kt, vt_=vt_):
                            return vt_.t[:, kt, :]
                        qb.w.append(tdk)
                        qb.w.append(tdv)
                        tZ = attn_core(qb, 32, lhs_s, lhs_v, 2)
                        qb.read(tZ)
                        kt_.read(tZ)
                        vt_.read(tZ)
                        t1a = kb.op("dve", lambda e: e.reciprocal(out=rz.t[:, 0, :], in_=Zb.t[:, 0, :]), deps=[tZ, rz.wdeps()])
                        t1 = kb.op("dve", lambda e: e.reciprocal(out=rz.t[:, 1, :], in_=Zb.t[:, 1, :]), deps=[tZ])
                        Zb.read(t1)
                        rz.wrote(t1)
                        t2 = kb.op("dve", lambda e: e.tensor_tensor(out=fa.t[:], in0=Ob.t[:, 0, :], in1=rz.t[:, 0, :], op=ALU.mult),
                                   deps=[t1, fa.wdeps()])
                        t3 = kb.op("dve", lambda e: e.tensor_tensor(out=fb.t[:], in0=Ob.t[:, 1, :], in1=rz.t[:, 1, :], op=ALU.mult),
                                   deps=[t1, fb.wdeps()])
                        Ob.read(t3)
                        rz.read(t3)
                        fa.wrote(t2)
                        fb.wrote(t3)
                        t4 = kb.op("dve", lambda e: e.scalar_tensor_tensor(out=fo.t[:], in0=fb.t[:], scalar=nlam[:, 0:1],
                                                                          in1=fa.t[:], op0=ALU.mult, op1=ALU.add),
                                   deps=[t2, t3, t_lam, fo.wdeps()])
                        fa.read(t4)
                        fb.read(t4)
                        fo.wrote(t4)
                        rms_store(fo, t4, gcol[:, 0:1], None, cat_d[hh][:, qt * 512:(qt + 1) * 512], extra_deps=[t_g])
                if dbg == "dattn":
                    kb.barrier()
                    break
                for hh in range(4):
                    for qt in range(8):
                        qb = qts.next()
                        tdq = kb.dma(qb.t[:], mq_d[hh][:, qt * 512:(qt + 1) * 512], deps=qb.wdeps())
                        qb.wrote(tdq)
                        sc = scr.next()
                        for mt in range(2):
                            tS = kb.op("pe", lambda e, sc=sc, mt=mt, hh=hh, qb=qb: e.matmul(
                                sc.t[:, mt, :], lhsT=mkT[:, hh, mt * 128:(mt + 1) * 128], rhs=qb.t[:], start=True, stop=True),
                                deps=[tdq, sc.wdeps()] if mt == 0 else (), inc=(mt == 1))
                        sc.wrote(tS)
                        qb.read(tS)
                        p = pts.next()
                        te0 = kb.op("act", lambda e, p=p, sc=sc: e.activation(
                            out=p.t[:, 0, :], in_=sc.t[:, 0, :], func=AF.Exp), deps=[tS, p.wdeps()])
                        te = kb.op("act", lambda e, p=p, sc=sc: e.activation(
                            out=p.t[:, 1, :], in_=sc.t[:, 1, :], func=AF.Exp), deps=[tS])
                        sc.read(te)
                        p.wrote(te)
                        for mt in range(2):
                            tO = kb.op("pe", lambda e, p=p, mt=mt, hh=hh: e.matmul(
                                Ob.t[:, 0, :], lhsT=mv_[:, mt, hh * 128:(hh + 1) * 128], rhs=p.t[:, mt, :],
                                start=(mt == 0), stop=(mt == 1)), deps=[te, Ob.wdeps()] if mt == 0 else (), inc=False)
                        for mt in range(2):
                            tZ = kb.op("pe", lambda e, p=p, mt=mt: e.matmul(
                                Zb.t[:, 0, :], lhsT=onesb[:], rhs=p.t[:, mt, :], start=(mt == 0), stop=(mt == 1)),
                                deps=[Zb.wdeps()] if mt == 0 else (), inc=(mt == 1))
                        p.read(tZ)
                        Ob.wrote(tZ)
                        Zb.wrote(tZ)
                        if dbg and hh == 0 and qt == 0:
                            kb.dma(dbg_p[:, :], p.t[:].rearrange("p a t -> p (a t)"), deps=[te])
                            tq1 = kb.op("dve", lambda e: e.tensor_copy(out=fa.t[:], in_=Ob.t[:, 0, :]), deps=[tZ])
                            tq2 = kb.op("dve", lambda e: e.tensor_copy(out=fb.t[:], in_=Zb.t[:, 0, :]), deps=[tZ])
                            kb.dma(dbg_oz[:, 0, :], fa.t[:], deps=[tq1])
                            kb.dma(dbg_oz[:, 1, :], fb.t[:], deps=[tq2])
                            kb.barrier()
                        t1 = kb.op("dve", lambda e: e.reciprocal(out=rz.t[:, 0, :], in_=Zb.t[:, 0, :]), deps=[tZ, rz.wdeps()])
                        Zb.read(t1)
                        rz.wrote(t1)
                        if dbg and hh == 0 and qt == 0:
                            kb.dma(dbg_oz[:, 2, :], rz.t[:, 0, :], deps=[t1])
                        so = fob.next()
                        t2 = kb.op("dve", lambda e, so=so: e.tensor_tensor(out=so.t[:], in0=Ob.t[:, 0, :], in1=rz.t[:, 0, :],
                                                                          op=ALU.mult), deps=[t1, so.wdeps()])
                        Ob.read(t2)
                        rz.read(t2)
                        so.wrote(t2)
                        so.read(kb.dma(cat_d[8 + hh][:, qt * 512:(qt + 1) * 512], so.t[:], deps=[t2]))
                kb.barrier()
            if dbg in ("mem", "attn"):
                break

            with ExitStack() as ps:
                hqs = sb("hqs", [128, S], F32, ps)
                hvs = sb("hvs", [64, NCH, 128], BF16, ps)
                Zt = [sb(f"Zt{d}", [128, S], F32, ps) for d in range(2)]
                KK = [sb(f"KK{d}", [128, S], F32, ps) for d in range(2)]
                Et = [sb(f"Et{d}", [128, S + 1], F32, ps) for d in range(2)]
                Qt = [sb(f"Qt{d}", [128, S], BF16, ps) for d in range(2)]
                Kt = [sb(f"Kt{d}", [128, S], BF16, ps) for d in range(2)]
                csc = [sb(f"csc{d}", [128, 4, NCH], F32, ps) for d in range(2)]
                Sst = [sb(f"Sst{d}", [128, 128], F32, ps) for d in range(2)]
                Stm = [sb(f"Stm{d}", [128, 128], F32, ps) for d in range(2)]
                Sbf = [[Buf(sb(f"Sbf{d}_{i}", [128, 128], BF16, ps)) for i in range(2)] for d in range(2)]
                KTs = [Ring([Buf(sb(f"KTs{d}_{i}", [64, 128], BF16, ps)) for i in range(2)]) for d in range(2)]
                ATs = [Ring([Buf(sb(f"ATs{d}_{i}", [64, 64], BF16, ps)) for i in range(2)]) for d in range(2)]
                ost = [Ring([Buf(sb(f"ost{d}_{i}", [128, 512], F32, ps)) for i in range(2)]) for d in range(2)]
                PT = [PP[2 + d].t[:].rearrange("p a t -> p (a t)").bitcast(BF16) for d in range(2)]
                for hh in range(4):
                    kb.barrier()
                    t_hq = kb.dma(hqs[:], hq_d[hh])
                    t_hv = kb.dma(hvs[:], hv_d[:, hh * 128:(hh + 1) * 128].rearrange("(c p) d -> p c d", p=HC))
                    gate_done = []
                    for d in range(2):
                        zsrc = zf_d if d == 0 else zb_d
                        tz = kb.dma(Zt[d][:], zsrc[hh])
                        lbc = lbt[:, d, l, hh:hh + 1]
                        omc = oml[:, d, l, hh:hh + 1]
                        t = kb.op("act", lambda e, d=d: e.activation(out=Zt[d][:], in_=Zt[d][:], func=AF.Sigmoid), deps=[tz])
                        t = kb.op("dve", lambda e, d=d, lbc=lbc, omc=omc: e.tensor_scalar(
                            out=Zt[d][:], in0=Zt[d][:], scalar1=omc, scalar2=lbc, op0=ALU.mult, op1=ALU.add), deps=[t])
                        tkk = kb.op("pool", lambda e, d=d: e.tensor_scalar(
                            out=KK[d][:], in0=Zt[d][:], scalar1=-1.0, scalar2=1.0, op0=ALU.mult, op1=ALU.add), deps=[t])
                        t = kb.op("dve", lambda e, d=d: e.tensor_scalar(
                            out=Zt[d][:], in0=Zt[d][:], scalar1=1e-20, scalar2=None, op0=ALU.max), deps=[t, tkk])
                        t = kb.op("act", lambda e, d=d: e.activation(out=Zt[d][:], in_=Zt[d][:], func=AF.Ln), deps=[t])
                        t0m = kb.op("dve", lambda e, d=d: e.memset(Et[d][:, 0:1], 0.0))
                        opx = ALU.add if d == 0 else ALU.subtract
                        t = kb.op("dve", lambda e, d=d, opx=opx: e.tensor_tensor_scan(
                            out=Et[d][:, 1:S + 1], data0=cst[:, 128:129].to_broadcast([128, S]), data1=Zt[d][:],
                            initial=0.0, op0=ALU.mult, op1=opx), deps=[t, t0m, t_cst])
                        eo = 1 if d == 0 else 0
                        Ev = Et[d][:, eo:eo + S].rearrange("p (c t) -> p c t", t=HC)
                        rho = Ev[:, :, HC // 2:HC // 2 + 1]
                        Dv = Zt[d][:].rearrange("p (c t) -> p c t", t=HC)
                        tD = kb.op("dve", lambda e, Dv=Dv, Ev=Ev, rho=rho: e.tensor_tensor(
                            out=Dv, in0=Ev, in1=rho.to_broadcast([128, NCH, HC]), op=ALU.subtract), deps=[t])
                        Ec = Et[d][:, 0:S].rearrange("p (c t) -> p c t", t=HC)[:, :, 0]
                        En = Et[d][:, HC:S + HC] if False else None
                        Enx = Et[d][:, 1:S + 1].rearrange("p (c t) -> p c t", t=HC)[:, :, HC - 1]
                        rh2 = Ev[:, :, HC // 2]
                        if d == 0:
                            specs3 = ((Enx, Ec), (Enx, rh2), (rh2, Ec))
                        else:
                            specs3 = ((Ec, Enx), (Ec, rh2), (rh2, Enx))
                        ts3 = []
                        for i3, (aa, bb) in enumerate(specs3):
                            ts3.append(kb.op("dve", lambda e, d=d, i3=i3, aa=aa, bb=bb: e.tensor_tensor(
                                out=csc[d][:, i3, :], in0=aa, in1=bb, op=ALU.subtract), deps=[t]))
                        tcs = kb.op("act", lambda e, d=d: e.activation(
                            out=csc[d][:, 0:3, :].rearrange("p a c -> p (a c)"),
                            in_=csc[d][:, 0:3, :].rearrange("p a c -> p (a c)"), func=AF.Exp), deps=[ts3])
                        tx = kb.op("act", lambda e, d=d: e.activation(out=Et[d][:, 0:S], in_=Zt[d][:], func=AF.Exp),
                                   deps=[tD, ts3])
                        tq = kb.op("dve", lambda e, d=d: e.tensor_tensor(out=Qt[d][:], in0=hqs[:], in1=Et[d][:, 0:S], op=ALU.mult),
                                   deps=[tx, t_hq])
                        tx2 = kb.op("act", lambda e, d=d: e.activation(out=Et[d][:, 0:S], in_=Zt[d][:], func=AF.Exp, scale=-1.0),
                                    deps=[tq])
                        tk_ = kb.op("pool", lambda e, d=d: e.tensor_tensor(out=Kt[d][:], in0=KK[d][:], in1=Et[d][:, 0:S], op=ALU.mult),
                                    deps=[tx2, tkk])
                        tS0 = kb.op("dve", lambda e, d=d: e.memset(Sst[d][:], 0.0))
                        tS1 = kb.op("pool", lambda e, d=d: e.memset(Sbf[d][0].t[:], 0.0))
                        Sbf[d][0].wrote(tS1)
                        gate_done.append([tq, tk_, tcs, tS0, tS1, t_hv])
                    pend = [None, None]
                    st_tok = [gate_done[0][3], gate_done[1][3]]
                    rdT = [[None, None], [None, None]]
                    rdA = [[None, None], [None, None]]
                    rdM = [[None, None], [None, None]]
                    for i in range(NCH + 1):
                        for d in range(2):
                            if i < NCH:
                                c = i if d == 0 else NCH - 1 - i
                                par = i % 2
                                cs = slice(c * HC, (c + 1) * HC)
                                bank = PP[d]
                                Mv = bank.t[:, 0, par * 256:par * 256 + 128]
                                Av = bank.t[0:64, 0, par * 256 + 128:par * 256 + 192]
                                Tv = PT[d][0:64, par * 128:(par + 1) * 128]
                                gd_ = gate_done[d] if i == 0 else ()
                                kts_ = KTs[d].next()
                                ats_ = ATs[d].next()
                                tT = kb.op("pe", lambda e, d=d, cs=cs, Tv=Tv: e.transpose(
                                    out=Tv, in_=Kt[d][:, cs], identity=identb[:]), deps=[gd_, t_ib, rdT[d][par]])
                                tA = kb.op("pe", lambda e, d=d, cs=cs, Av=Av: e.matmul(
                                    Av, lhsT=Kt[d][:, cs], rhs=Qt[d][:, cs], start=True, stop=True), deps=[rdA[d][par]])
                                teT = kb.op("act", lambda e, kts_=kts_, Tv=Tv: e.activation(out=kts_.t[:], in_=Tv, func=AF.Copy),
                                            deps=[tT, kts_.wdeps()])
                                kts_.wrote(teT)
                                rdT[d][par] = teT
                                teA = kb.op("dve", lambda e, ats_=ats_, Av=Av, d=d: e.tensor_tensor(
                                    out=ats_.t[:], in0=Av, in1=trib[:, d, 0, :], op=ALU.mult), deps=[tA, ats_.wdeps(), t_tb])
                                ats_.wrote(teA)
                                rdA[d][par] = teA
                                tM = kb.op("pe", lambda e, kts_=kts_, c=c, Mv=Mv: e.matmul(
                                    Mv, lhsT=kts_.t[:], rhs=hvs[:, c, :], start=True, stop=True), deps=[teT, rdM[d][par]])
                                kts_.read(tM)
                            if pend[d] is not None:
                                (pc, pi, sbuf_prev, ats_prev) = pend[d]
                                pcs = slice(pc * HC, (pc + 1) * HC)
                                slot = pi % 8
                                Ov = PP[d].t[:, 1, slot * HC:(slot + 1) * HC]
                                odeps = [sbuf_prev.rdeps(), ats_prev.rdeps()]
                                if slot == 0:
                                    odeps.append(PP[d].r.get("oev"))
                                kb.op("pe", lambda e, d=d, pcs=pcs, Ov=Ov, sbuf_prev=sbuf_prev: e.matmul(
                                    Ov, lhsT=sbuf_prev.t[:], rhs=Qt[d][:, pcs], start=True, stop=False), deps=odeps, inc=False)
                                tO = kb.op("pe", lambda e, d=d, pc=pc, Ov=Ov, ats_prev=ats_prev: e.matmul(
                                    Ov, lhsT=hvs[:, pc, :], rhs=ats_prev.t[:], start=False, stop=True))
                                sbuf_prev.read(tO)
                                ats_prev.read(tO)
                                if slot == 7:
                                    ob_ = ost[d].next()
                                    tev = kb.op("act", lambda e, d=d, ob_=ob_: e.activation(
                                        out=ob_.t[:], in_=PP[d].t[:, 1, :], func=AF.Copy), deps=[tO, ob_.wdeps()])
                                    PP[d].r["oev"] = tev
                                    ob_.wrote(tev)
                                    g8 = pi // 8
                                    if d == 0:
                                        tok0 = g8 * 512
                                        dstv = of_d[hh][:, tok0:tok0 + 512]
                                        srcv = ob_.t[:]
                                    else:
                                        tok0 = (NCH - 8 * (g8 + 1)) * HC
                                        dstv = ob_d[hh][:, tok0:tok0 + 512].rearrange("p (j t) -> p j t", t=HC)
                                        srcv = ob_.t[:].rearrange("p (j t) -> p j t", t=HC)
                                    if d == 0:
                                        ob_.read(kb.dma(dstv, srcv, deps=[tev]))
                                    else:
                                        for j in range(8):
                                            ob_.read(kb.dma(dstv[:, 7 - j, :], srcv[:, j, :], deps=[tev]))
                                pend[d] = None
                            if i < NCH:
                                cur_sb = Sbf[d][i % 2]
                                nxt_sb = Sbf[d][(i + 1) % 2]
                                pend[d] = (c, i, cur_sb, ats_)
                                tu1 = kb.op("dve", lambda e, d=d, c=c: e.tensor_scalar(
                                    out=Stm[d][:], in0=Sst[d][:], scalar1=csc[d][:, 0, c:c + 1], scalar2=None, op0=ALU.mult),
                                    deps=[st_tok[d], gd_])
                                tu2 = kb.op("dve", lambda e, d=d, c=c, Mv=Mv: e.scalar_tensor_tensor(
                                    out=Sst[d][:], in0=Mv, scalar=csc[d][:, 1, c:c + 1], in1=Stm[d][:], op0=ALU.mult, op1=ALU.add),
                                    deps=[tM, tu1])
                                st_tok[d] = tu2
                                rdM[d][par] = tu2
                                if i + 1 < NCH:
                                    cn = c + 1 if d == 0 else c - 1
                                    tsb = kb.op("act", lambda e, d=d, cn=cn, nxt_sb=nxt_sb: e.activation(
                                        out=nxt_sb.t[:], in_=Sst[d][:], func=AF.Identity, scale=csc[d][:, 2, cn:cn + 1]),
                                        deps=[tu2, nxt_sb.wdeps()])
                                    nxt_sb.wrote(tsb)
                                    st_tok[d] = [tu2, tsb]
                kb.barrier()
                if dbg == "hgrn_raw":
                    break
            with ExitStack() as ps:
                fsq = Buf(sb("fsq2", [128, 512], F32, ps))
                fsd = Buf(sb("fsd2", [128, 512], F32, ps))
                fob = Ring([Buf(sb(f"fob2{i}", [128, 512], BF16, ps)) for i in range(2)])
                lf = Ring([Buf(sb(f"lf{i}", [128, 512], F32, ps)) for i in range(2)])
                lb_ = Ring([Buf(sb(f"lb{i}", [128, 512], F32, ps)) for i in range(2)])
                lg = Ring([Buf(sb(f"lg{i}", [128, 512], F32, ps)) for i in range(2)])
                Zb = PP[3]
                chg = CO[("hgg", l)]
                for hh in range(4):
                    for qt in range(8):
                        sl = slice(qt * 512, (qt + 1) * 512)
                        a = lf.next(); b = lb_.next(); g = lg.next()
                        ta = kb.dma(a.t[:], of_d[hh][:, sl], deps=a.wdeps()); a.wrote(ta)
                        tb = kb.dma(b.t[:], ob_d[hh][:, sl], deps=b.wdeps()); b.wrote(tb)
                        tg = kb.dma(g.t[:], hg_d[hh][:, sl], deps=g.wdeps()); g.wrote(tg)
                        t1 = kb.op("pool", lambda e, a=a, b=b: e.tensor_tensor(out=a.t[:], in0=a.t[:], in1=b.t[:], op=ALU.add),
                                   deps=[ta, tb])
                        b.read(t1)
                        a.wrote(t1)
                        t2 = kb.op("act", lambda e, a=a: e.activation(out=fsq.t[:], in_=a.t[:], func=AF.Square), deps=[t1, fsq.wdeps()])
                        fsq.wrote(t2)
                        t3 = kb.op("pe", lambda e: e.matmul(Zb.t[:, 0, :], lhsT=onesf, rhs=fsq.t[:], start=True, stop=True),
                                   deps=[t2, Zb.wdeps(), t_cst])
                        fsq.read(t3)
                        Zb.wrote(t3)
                        t4 = kb.op("act", lambda e: e.activation(out=fsd.t[:], in_=Zb.t[:, 0, :], func=AF.Sqrt,
                                                                bias=epsc[:, 1:2], scale=1.0 / 128.0), deps=[t3, fsd.wdeps()])
                        Zb.read(t4)
                        t5 = kb.op("dve", lambda e: e.reciprocal(out=fsd.t[:], in_=fsd.t[:]), deps=[t4])
                        fsd.wrote(t5)
                        t6 = kb.op("dve", lambda e, a=a: e.tensor_tensor(out=a.t[:], in0=a.t[:], in1=fsd.t[:], op=ALU.mult), deps=[t5, t2])
                        fsd.read(t6)
                        so = fob.next()
                        t7 = kb.op("dve", lambda e, a=a, g=g, so=so: e.scalar_tensor_tensor(
                            out=so.t[:], in0=a.t[:], scalar=colp[:, chg:chg + 1], in1=g.t[:], op0=ALU.mult, op1=ALU.mult),
                            deps=[t6, tg, so.wdeps(), t_colp])
                        a.read(t7); g.read(t7)
                        so.wrote(t7)
                        so.read(kb.dma(cat_d[4 + hh][:, sl], so.t[:], deps=[t7]))
                kb.barrier()
            if dbg == "mix":
                break

            with ExitStack() as ps:
                wo = sb("wo", [128, 12, D], BF16, ps)
                with ExitStack() as ps2:
                    stg = Ring([Buf(sb(f"stgo{i}", [128, 2048], F32, ps2)) for i in range(3)])
                    wt = load_w(wo, w_out_d[l], 12, D, stg)
                    kb.barrier()
                G = sb("G1", [128, D], F32, ps)
                Bt = sb("B1", [128, D], F32, ps)
                gd = load_gb(G, Bt, RO[("ln1_g", l)], RO[("ln1_b", l)])
                lnb = ln_bufs(ps)
                rr = Ring([Buf(sb(f"r1_{i}", [128, D], F32, ps)) for i in range(3)])
                hr = Ring([Buf(sb(f"hTt1_{i}", [128, 8, 512], BF16, ps)) for i in range(2)])
                cin = Ring([Buf(sb(f"cin{i}", [128, 12, 512], BF16, ps)) for i in range(2)])
                accr = Ring([PP[0], PP[1], PP[2]])
                cat_v = cat_d.rearrange("c p t -> p c t")
                pending = [None]

                def flush():
                    if pending[0] is not None:
                        pending[0]()
                        pending[0] = None
                for tt in range(8):
                    ci = cin.next()
                    tdc = kb.dma(ci.t[:], cat_v[:, :, tt * 512:(tt + 1) * 512], deps=ci.wdeps())
                    ci.wrote(tdc)
                    hTt = hr.next()
                    toks = []
                    for s4 in range(4):
                        t0 = tt * 512 + s4 * 128
                        acc = accr.next()
                        for nn in range(2):
                            for kc in range(12):
                                tk = kb.op("pe", lambda e, acc=acc, nn=nn, kc=kc, ci=ci, s4=s4: e.matmul(
                                    acc.t[:, nn, :], lhsT=ci.t[:, kc, s4 * 128:(s4 + 1) * 128], rhs=wo[:, kc, nn * 512:(nn + 1) * 512],
                                    start=(kc == 0), stop=(kc == 11)),
                                    deps=[tdc, acc.wdeps()] if (kc == 0 and nn == 0) else (), inc=(kc == 11 and nn == 1))
                        acc.wrote(tk)
                        ci.read(tk)
                        r = rr.next()
                        td = kb.dma(r.t[:], xres_d[t0:t0 + 128, :], deps=r.wdeps())
                        r.wrote(td)
                        tr0 = kb.op("dve", lambda e, r=r, acc=acc: e.scalar_tensor_tensor(
                            out=r.t[:, 0:512], in0=r.t[:, 0:512], scalar=ALPHA, in1=acc.t[:, 0, :],
                            op0=ALU.mult, op1=ALU.add), deps=[td, tk])
                        tr = kb.op("dve", lambda e, r=r, acc=acc: e.scalar_tensor_tensor(
                            out=r.t[:, 512:1024], in0=r.t[:, 512:1024], scalar=ALPHA, in1=acc.t[:, 1, :],
                            op0=ALU.mult, op1=ALU.add), deps=[td, tk])
                        acc.read(tr)
                        etr = ln_tile(r, [tr], 128, t0, G, Bt, gd, lnb, False, hTt, s4 * 128)
                        flush()

                        def pend(etr=etr, toks=toks, hTt=hTt, s4=s4, tt=tt):
                            toks.extend(etr())
                            if s4 == 3:
                                hTt.wrote(toks[0])
                                for t in toks[1:]:
                                    hTt.wrote(t, fresh=False)
                                hTt.read(kb.dma(hT_v[:, :, tt * 512:(tt + 1) * 512], hTt.t[:], deps=toks))
                        pending[0] = pend
                flush()
                kb.barrier()
            if dbg == "ln1":
                break

            with ExitStack() as ps:
                wu = sb("wu", [128, 8, 2 * DFF], BF16, ps)
                wd = sb("wd", [128, 22, D], BF16, ps)
                with ExitStack() as ps2:
                    stg = Ring([Buf(sb(f"stgf{i}", [128, 2048], F32, ps2)) for i in range(3)])
                    wt = load_w(wu, w_up_d[l], 8, 2 * DFF, stg)
                    wt += load_w(wd, w_dn_d[l], 22, D, stg)
                    kb.barrier()
                G = sb("G2", [128, D], F32, ps)
                Bt = sb("B2", [128, D], F32, ps)
                gd = load_gb(G, Bt, RO[("ln2_g", l)], RO[("ln2_b", l)])
                lnb = ln_bufs(ps)
                rr = Ring([Buf(sb(f"r2_{i}", [128, D], F32, ps)) for i in range(2)])
                hr = Ring([Buf(sb(f"hTt2_{i}", [128, 8, 256], BF16, ps)) for i in range(2)])
                hin = Ring([Buf(sb(f"hin2_{i}", [128, 8, 256], BF16, ps)) for i in range(2)])
                gT = Ring([Buf(sb(f"gT{i}", [128, 22, 256], BF16, ps)) for i in range(2)])
                ca = Ring([Buf(sb(f"ca{i}", [128, 256], F32, ps)) for i in range(2)])
                cb_ = Ring([Buf(sb(f"cb{i}", [128, 256], F32, ps)) for i in range(2)])
                cs_ = Ring([Buf(sb(f"cs{i}", [128, 256], F32, ps)) for i in range(2)])
                ur = Ring([PP[0], PP[1]])
                acc = PP[2]
                last = (l == L - 1)
                WT = 254
                ffn_pending = [None]
                ntile = (S + WT - 1) // WT
                cw0 = CO[("cw", l)]
                cb0 = CO[("cb", l)]
                for ti in range(ntile):
                    T0 = ti * WT
                    W = min(WT, S - T0)
                    lo = T0 - 1
                    h = hin.next()
                    src_lo = max(lo, 0)
                    src_hi = min(lo + W + 2, S)
                    dlo = src_lo - lo
                    hdeps = h.wdeps()
                    tl = []
                    if dlo > 0:
                        tl.append(kb.op("pool", lambda e, h=h: e.memset(h.t[:, :, 0:1], 0.0), deps=hdeps))
                    if src_hi < lo + W + 2:
                        tl.append(kb.op("pool", lambda e, h=h, W=W: e.memset(h.t[:, :, W + 1:W + 2], 0.0), deps=hdeps))
                    tl.append(kb.dma(h.t[:, :, dlo:dlo + (src_hi - src_lo)], hT_v[:, :, src_lo:src_hi], deps=hdeps))
                    h.wrote(tl[0])
                    for t in tl[1:]:
                        h.wrote(t, fresh=False)
                    g = gT.next()
                    NW = W + 2
                    gw = []
                    for fc in range(22):
                        ub = ur.next()
                        for j, col in enumerate((fc * 128, DFF + fc * 128)):
                            for kc in range(8):
                                tk = kb.op("pe", lambda e, ub=ub, j=j, kc=kc, col=col, h=h, NW=NW: e.matmul(
                                    ub.t[:, j, 0:NW], lhsT=wu[:, kc, col:col + 128], rhs=h.t[:, kc, 0:NW],
                                    start=(kc == 0), stop=(kc == 7)),
                                    deps=[h.rdeps(), ub.wdeps()] if (kc == 0 and j == 0) else (), inc=(kc == 7 and j == 1))
                        ub.wrote(tk)
                        h.read(tk)
                        a = ca.next(); b = cb_.next(); sg = cs_.next()
                        res = []
                        for j, buf in ((0, a), (1, b)):
                            ch = j * 22 + fc
                            w0 = colp[:, cw0 + 0 * 44 + ch:cw0 + 0 * 44 + ch + 1]
                            w1 = colp[:, cw0 + 1 * 44 + ch:cw0 + 1 * 44 + ch + 1]
                            w2 = colp[:, cw0 + 2 * 44 + ch:cw0 + 2 * 44 + ch + 1]
                            t1 = kb.op("act", lambda e, buf=buf, ub=ub, j=j, w1=w1, W=W: e.activation(
                                out=buf.t[:, 0:W], in_=ub.t[:, j, 1:W + 1], func=AF.Identity, scale=w1), deps=[tk, buf.wdeps(), t_colp])
                            t2 = kb.op("dve", lambda e, buf=buf, ub=ub, j=j, w0=w0, W=W: e.scalar_tensor_tensor(
                                out=buf.t[:, 0:W], in0=ub.t[:, j, 0:W], scalar=w0, in1=buf.t[:, 0:W], op0=ALU.mult, op1=ALU.add),
                                deps=[t1])
                            t3 = kb.op("dve", lambda e, buf=buf, ub=ub, j=j, w2=w2, W=W: e.scalar_tensor_tensor(
                                out=buf.t[:, 0:W], in0=ub.t[:, j, 2:W + 2], scalar=w2, in1=buf.t[:, 0:W], op0=ALU.mult, op1=ALU.add),
                                deps=[t2])
                            buf.wrote(t3)
                            res.append(t3)
                        ub.read(res[1])
                        bg = colp[:, cb0 + fc:cb0 + fc + 1]
                        bv = colp[:, cb0 + 22 + fc:cb0 + 22 + fc + 1]
                        t4 = kb.op("act", lambda e, a=a, sg=sg, bg=bg, W=W: e.activation(
                            out=sg.t[:, 0:W], in_=a.t[:, 0:W], func=AF.Silu, bias=bg, scale=1.0), deps=[res[0], sg.wdeps()])
                        a.read(t4)
                        sg.wrote(t4)
                        t5 = kb.op("dve", lambda e, b=b, sg=sg, g=g, fc=fc, bv=bv, W=W: e.scalar_tensor_tensor(
                            out=g.t[:, fc, 0:W], in0=b.t[:, 0:W], scalar=bv, in1=sg.t[:, 0:W], op0=ALU.add, op1=ALU.mult),
                            deps=[res[1], t4, g.wdeps() if fc == 0 else None])
                        b.read(t5); sg.read(t5)
                        gw.append(t5)
                    g.wrote(gw[0])
                    for t in gw[1:]:
                        g.wrote(t, fresh=False)
                    if ffn_pending[0] is not None:
                        ffn_pending[0]()
                        ffn_pending[0] = None
                    hTt = hr.next()
                    toks = []
                    etrs = []
                    s0 = 0
                    while s0 < W:
                        n = min(128, W - s0)
                        t0 = T0 + s0
                        for nn in range(2):
                            for fc in range(22):
                                tk = kb.op("pe", lambda e, nn=nn, fc=fc, g=g, s0=s0, n=n: e.matmul(
                                    acc.t[0:n, nn, :], lhsT=g.t[:, fc, s0:s0 + n], rhs=wd[:, fc, nn * 512:(nn + 1) * 512],
                                    start=(fc == 0), stop=(fc == 21)),
                                    deps=[g.rdeps(), acc.wdeps()] if (fc == 0 and nn == 0) else (), inc=(fc == 21 and nn == 1))
                        acc.wrote(tk)
                        g.read(tk)
                        r = rr.next()
                        td = kb.dma(r.t[0:n, :], xres_d[t0:t0 + n, :], deps=r.wdeps())
                        r.wrote(td)
                        tr0 = kb.op("dve", lambda e, r=r, n=n: e.scalar_tensor_tensor(
                            out=r.t[0:n, 0:512], in0=r.t[0:n, 0:512], scalar=ALPHA, in1=acc.t[0:n, 0, :],
                            op0=ALU.mult, op1=ALU.add), deps=[td, tk])
                        tr = kb.op("dve", lambda e, r=r, n=n: e.scalar_tensor_tensor(
                            out=r.t[0:n, 512:1024], in0=r.t[0:n, 512:1024], scalar=ALPHA, in1=acc.t[0:n, 1, :],
                            op0=ALU.mult, op1=ALU.add), deps=[td, tk])
                        acc.read(tr)
                        etrs.append(ln_tile(r, [tr], n, t0, G, Bt, gd, lnb, last, hTt, s0))
                        s0 += n

                    def fpend(etrs=etrs, toks=toks, hTt=hTt, T0=T0, W=W):
                        for f in etrs:
                            toks.extend(f())
                        if not last:
                            hTt.wrote(toks[0])
                            for t in toks[1:]:
                                hTt.wrote(t, fresh=False)
                            hTt.read(kb.dma(hT_v[:, :, T0:T0 + W], hTt.t[:, :, 0:W], deps=toks))
                    ffn_pending[0] = fpend
                if ffn_pending[0] is not None:
                    ffn_pending[0]()
                    ffn_pending[0] = None
                kb.barrier()
        kb.finish(block)
    return nc


def _consts():
    c = np.zeros((128, 1024), np.float32)
    c[:, 0:128] = np.eye(128, dtype=np.float32)
    c[:, 128:256] = 1.0
    s = np.arange(64)[:, None]
    t = np.arange(64)[None, :]
    c[0:64, 256:320] = (s <= t)
    c[0:64, 320:384] = (s >= t)
    p = np.arange(128)
    dd = p % 64
    inv = (ROPE_THETA ** (-(np.arange(0, 16, 2, dtype=np.float32)) / 16.0)).astype(np.float32)
    c[:, 384] = inv[dd % 8]
    m = (dd < 16).astype(np.float32)
    c[:, 385] = m
    c[:, 386] = 1.0 - m
    c[:, 387] = np.where(dd < 8, -1.0, np.where(dd < 16, 1.0, 0.0))
    return c


def _rot_perm():
    idx = np.arange(512)
    dd = idx % 64
    base = idx - dd
    pd = np.where(dd < 8, dd + 8, np.where(dd < 16, dd - 8, dd))
    return base + pd


def _prep(inputs, L):
    CO = _col_layout(L)
    RO = _row_layout(L)
    f = lambda a: np.ascontiguousarray(np.asarray(a), dtype=np.float32)
    colp = np.zeros((128, CO["n"]), np.float32)
    lbf = f(inputs["hg_lb_fwd"])
    lbb = f(inputs["hg_lb_bwd"])
    colp[:, CO["lbf"]:CO["lbf"] + 16] = lbf.reshape(DEPTH, 4, 128).transpose(2, 0, 1).reshape(128, 16)
    colp[:, CO["lbb"]:CO["lbb"] + 16] = lbb.reshape(DEPTH, 4, 128).transpose(2, 0, 1).reshape(128, 16)
    for l in range(L):
        colp[:, CO[("dag", l)]] = f(inputs["da_norm_g"])[l]
        colp[:, CO[("hgg", l)]] = f(inputs["hg_norm_g"])[l]
        cw = f(inputs["conv_w"])[l]
        colp[:, CO[("cw", l)]:CO[("cw", l)] + 132] = cw.reshape(3, 44, 128).transpose(2, 0, 1).reshape(128, 132)
        colp[:, CO[("cb", l)]:CO[("cb", l)] + 44] = f(inputs["conv_b"])[l].reshape(44, 128).T
    rowp = np.zeros((1, RO["n"]), np.float32)
    rowp[0, RO["ln_in_g"]:RO["ln_in_g"] + D] = f(inputs["ln_in_g"])
    rowp[0, RO["ln_in_b"]:RO["ln_in_b"] + D] = f(inputs["ln_in_b"])
    for l in range(L):
        for nm in ("ln1_g", "ln1_b", "ln2_g", "ln2_b"):
            rowp[0, RO[(nm, l)]:RO[(nm, l)] + D] = f(inputs[nm])[l]
        rowp[0, RO[("lam", l)]:RO[("lam", l)] + 256] = f(inputs["da_lambda"])[l].reshape(256)
    w_in = f(inputs["w_in"])[:L]
    perm = _rot_perm()
    w_rot = np.ascontiguousarray(np.concatenate([w_in[:, :, 0:512][:, :, perm], w_in[:, :, 512:1024][:, :, perm]], axis=2))
    shared = {
        "colp": colp, "rowp": rowp, "cst": _consts(),
        "w_in": w_in, "w_rot": w_rot,
        "w_kv": f(inputs["w_mem_kv"])[:L], "w_out": f(inputs["w_out"])[:L],
        "w_up": f(inputs["w_up"])[:L], "w_dn": f(inputs["w_down"])[:L],
    }
    return shared


_NC_CACHE = {}


def kernel(**inputs):
    L = DEPTH
    x = np.asarray(inputs["x"], dtype=np.float32)
    mem = np.asarray(inputs["mem"], dtype=np.float32)
    pos = np.asarray(inputs["positions"]).astype(np.int32)
    B = x.shape[0]
    shared = _prep(inputs, L)
    if "nc" not in _NC_CACHE:
        _NC_CACHE["nc"] = build(L)
    nc = _NC_CACHE["nc"]
    in_maps = []
    for b in range(B):
        m = dict(shared)
        m["x"] = np.ascontiguousarray(x[b])
        m["mem"] = np.ascontiguousarray(mem[b])
        m["pos"] = np.ascontiguousarray(pos[b].reshape(1, S))
        in_maps.append(m)
    res = run_bass_kernel_spmd(nc, in_maps, core_ids=list(range(B)))
    return np.stack([np.asarray(r["out"], dtype=np.float32) for r in res.results], axis=0)
```
